# Optimizing a Trainium2 kernel written in Bass

```python
import jax, jax.numpy as jnp
from jax import lax
import numpy as np

D_MODEL = 1024
BATCH = 4
SEQ = 4096
DEPTH = 4
DEC_BATCH = 4
DEC_SEQ = 8192
PAST_LEN = 128

GRID_W = 64
NA_HEADS = 8
NA_HEAD_DIM = 64
NA_KR = 8
NA_KC = 16
HG_HEADS = 4
HG_K = 64
HG_V = 64
HG_DIM = HG_HEADS * HG_K
GLA_HEADS = 4
GLA_K = 32
GLA_V = 64
GLA_RANK = 16
GLA_GATE_NORM = 16.0
CHUNK = 64
CA_HEADS = 4
CA_HEAD_DIM = D_MODEL // CA_HEADS
N_MEM = 256
D_FF = 2816
EPS = 1e-6

D_MIX = NA_HEADS * NA_HEAD_DIM + HG_HEADS * HG_V + GLA_HEADS * GLA_V
SPLIT_SIZES = (
    NA_HEADS * NA_HEAD_DIM, NA_HEADS * NA_HEAD_DIM, NA_HEADS * NA_HEAD_DIM,
    HG_DIM, HG_DIM, HG_DIM, HG_HEADS * HG_V, HG_HEADS * HG_V,
    GLA_HEADS * GLA_K, GLA_HEADS * GLA_K, GLA_HEADS * GLA_V, GLA_HEADS * GLA_V,
    GLA_RANK, GLA_RANK,
)
IN_COLS = sum(SPLIT_SIZES)
SPLIT_POINTS = [int(s) for s in np.cumsum(SPLIT_SIZES)[:-1]]

kernel_name = "hybrid_na_hgrn2_gla_encoder"


def rmsnorm(x, g):
    xf = x.astype(jnp.float32)
    y = xf * lax.rsqrt(jnp.mean(xf * xf, axis=-1, keepdims=True) + EPS)
    return (y * g.astype(jnp.float32)).astype(x.dtype)


def swiglu(x, g, w_up, w_down):
    h = rmsnorm(x, g)
    gate, up = jnp.split(h @ w_up, 2, axis=-1)
    return (jax.nn.silu(gate) * up) @ w_down


def split_heads(a, n_heads):
    return a.reshape(*a.shape[:-1], n_heads, -1)


def neighbourhood_attention(q, k, v, rpb):
    B, T, H, d = q.shape
    rows = T // GRID_W
    kr = min(NA_KR, rows)
    qg = q.reshape(B, rows, GRID_W, H, d)
    kg = k.reshape(B, rows, GRID_W, H, d)
    vg = v.reshape(B, rows, GRID_W, H, d)
    col_start = np.clip(np.arange(GRID_W) - NA_KC // 2, 0, GRID_W - NA_KC)
    col_idx = col_start[:, None] + np.arange(NA_KC)[None, :]
    dc = col_idx - np.arange(GRID_W)[:, None] + (NA_KC - 1)
    scale = d ** -0.5

    def row_block(r):
        r0 = jnp.clip(r - kr // 2, 0, rows - kr)
        q_r = lax.dynamic_index_in_dim(qg, r, axis=1, keepdims=False)
        k_band = lax.dynamic_slice_in_dim(kg, r0, kr, axis=1)
        v_band = lax.dynamic_slice_in_dim(vg, r0, kr, axis=1)
        k_win = k_band[:, :, col_idx]
        v_win = v_band[:, :, col_idx]
        dr = r0 + jnp.arange(kr) - r + (NA_KR - 1)
        bias = rpb[:, dr[None, :, None], dc[:, None, :]].astype(jnp.float32)
        s = jnp.einsum('bqhd,brqkhd->bhqrk', q_r, k_win).astype(jnp.float32) * scale + bias[None]
        p = jax.nn.softmax(s, axis=(-2, -1))
        return jnp.einsum('bhqrk,brqkhd->bqhd', p.astype(v.dtype), v_win)

    out = lax.map(row_block, jnp.arange(rows))
    return out.transpose(1, 0, 2, 3, 4).reshape(B, T, H * d)


def chunk_gated_linear(q, k, v, g):
    B, T, H, K = q.shape
    V = v.shape[-1]
    N = T // CHUNK

    def to_chunks(a):
        return a.astype(jnp.float32).reshape(B, N, CHUNK, H, a.shape[-1]).transpose(1, 0, 3, 2, 4)

    qc, kc, vc, gc = to_chunks(q), to_chunks(k), to_chunks(v), to_chunks(g)
    bc = jnp.cumsum(gc, axis=3)
    mask = jnp.tril(jnp.ones((CHUNK, CHUNK), dtype=bool))

    def step(S, inp):
        qt, kt, vt, bt = inp
        o_inter = jnp.einsum('bhck,bhkv->bhcv', qt * jnp.exp(bt), S)
        diff = bt[:, :, :, None, :] - bt[:, :, None, :, :]
        decay = jnp.exp(jnp.where(mask[:, :, None], diff, -jnp.inf))
        A = jnp.einsum('bhtk,bhsk,bhtsk->bhts', qt, kt, decay)
        o = o_inter + jnp.einsum('bhts,bhsv->bhtv', A, vt)
        b_last = bt[:, :, -1:, :]
        S = jnp.exp(b_last[:, :, 0, :])[..., None] * S + jnp.einsum(
            'bhsk,bhsv->bhkv', kt * jnp.exp(b_last - bt), vt)
        return S, o

    S0 = jnp.zeros((B, H, K, V), jnp.float32)
    _, o = lax.scan(step, S0, (qc, kc, vc, bc))
    return o.transpose(1, 0, 3, 2, 4).reshape(B, T, H, V)


def bidirectional_gated_linear(q, k_f, g_f, k_b, g_b, v):
    flip = lambda a: jnp.flip(a, axis=1)
    fwd = chunk_gated_linear(q, k_f, v, g_f)
    bwd = flip(chunk_gated_linear(flip(q), flip(k_b), flip(v), flip(g_b)))
    return fwd + bwd


def head_norm_gate(o, gain, gate):
    y = o * lax.rsqrt(jnp.mean(o * o, axis=-1, keepdims=True) + EPS) * gain.astype(jnp.float32)
    return y.reshape(*o.shape[:2], -1) * jax.nn.silu(gate.astype(jnp.float32))


def hgrn_gates(zf, lb):
    zf = zf.astype(jnp.float32)
    log_f = jnp.logaddexp(jnp.log(lb), jnp.log1p(-lb) + jax.nn.log_sigmoid(zf))
    k = (1.0 - lb) * jax.nn.sigmoid(-zf)
    return k, log_f


def cross_attention(h, mem_n, wq, wkv, wo):
    B, T, _ = h.shape
    q = split_heads(h @ wq, CA_HEADS)
    k, v = jnp.split(mem_n @ wkv, 2, axis=-1)
    k, v = split_heads(k, CA_HEADS), split_heads(v, CA_HEADS)
    s = jnp.einsum('bthd,bmhd->bhtm', q, k).astype(jnp.float32) * (CA_HEAD_DIM ** -0.5)
    p = jax.nn.softmax(s, axis=-1)
    o = jnp.einsum('bhtm,bmhd->bthd', p.astype(v.dtype), v).reshape(B, T, D_MODEL)
    return o @ wo


def encode(x, mem, g_ffn1, w_ffn1_up, w_ffn1_down, g_mix, w_in, na_rpb, hg_lb, hg_norm,
           gla_w_alpha, gla_b_alpha, gla_norm, w_out, g_cross, g_mem, ca_wq, ca_wkv, ca_wo,
           g_ffn2, w_ffn2_up, w_ffn2_down, g_final):
    B, T, _ = x.shape
    lb_all = jnp.cumsum(jax.nn.softmax(hg_lb.astype(jnp.float32), axis=0), axis=0)
    lb_all = lb_all - lb_all[0:1]
    for l in range(DEPTH):
        x = x + 0.5 * swiglu(x, g_ffn1[l], w_ffn1_up[l], w_ffn1_down[l])

        h = rmsnorm(x, g_mix[l])
        (na_q, na_k, na_v, hq, hzf, hzb, hi, hgate,
         gq, gk, gv, ggate, gaf, gab) = jnp.split(h @ w_in[l], SPLIT_POINTS, axis=-1)

        na_o = neighbourhood_attention(split_heads(na_q, NA_HEADS), split_heads(na_k, NA_HEADS),
                                       split_heads(na_v, NA_HEADS), na_rpb[l])

        lb = lb_all[l]
        hk_f, hlog_f = hgrn_gates(hzf, lb[0])
        hk_b, hlog_b = hgrn_gates(hzb, lb[1])
        hg_o = bidirectional_gated_linear(
            split_heads(hq, HG_HEADS), split_heads(hk_f, HG_HEADS), split_heads(hlog_f, HG_HEADS),
            split_heads(hk_b, HG_HEADS), split_heads(hlog_b, HG_HEADS), split_heads(hi, HG_HEADS))
        hg_o = head_norm_gate(hg_o, hg_norm[l], hgate)

        glog_f = jax.nn.log_sigmoid((gaf @ gla_w_alpha[l, 0] + gla_b_alpha[l, 0]).astype(jnp.float32)) / GLA_GATE_NORM
        glog_b = jax.nn.log_sigmoid((gab @ gla_w_alpha[l, 1] + gla_b_alpha[l, 1]).astype(jnp.float32)) / GLA_GATE_NORM
        gk_h = split_heads(gk, GLA_HEADS)
        gla_o = bidirectional_gated_linear(
            split_heads(gq, GLA_HEADS) * (GLA_K ** -0.5), gk_h, split_heads(glog_f, GLA_HEADS),
            gk_h, split_heads(glog_b, GLA_HEADS), split_heads(gv, GLA_HEADS))
        gla_o = head_norm_gate(gla_o, gla_norm[l], ggate)

        mix = jnp.concatenate([na_o.astype(x.dtype), hg_o.astype(x.dtype), gla_o.astype(x.dtype)], axis=-1)
        x = x + mix @ w_out[l]

        x = x + cross_attention(rmsnorm(x, g_cross[l]), rmsnorm(mem, g_mem[l]),
                                ca_wq[l], ca_wkv[l], ca_wo[l])

        x = x + 0.5 * swiglu(x, g_ffn2[l], w_ffn2_up[l], w_ffn2_down[l])
    return rmsnorm(x, g_final)


def setup_inputs(seed: int = 0) -> dict:
    key = jax.random.key(seed)
    ks = jax.random.split(key, 32)

    def nrm(k, shape, scale):
        return jax.random.normal(k, shape, jnp.float32) * scale

    def gain(k, shape):
        return 1.0 + 0.02 * jax.random.normal(k, shape, jnp.float32)

    D = D_MODEL
    return {
        "x_prompt": nrm(ks[0], (BATCH, SEQ, D), 1.0),
        "x_sample": nrm(ks[1], (DEC_BATCH, DEC_SEQ, D), 1.0),
        "mem_prompt": nrm(ks[2], (BATCH, N_MEM, D), 1.0),
        "mem_sample": nrm(ks[3], (DEC_BATCH, N_MEM, D), 1.0),
        "g_ffn1": gain(ks[4], (DEPTH, D)),
        "w_ffn1_up": nrm(ks[5], (DEPTH, D, 2 * D_FF), D ** -0.5),
        "w_ffn1_down": nrm(ks[6], (DEPTH, D_FF, D), D_FF ** -0.5),
        "g_mix": gain(ks[7], (DEPTH, D)),
        "w_in": nrm(ks[8], (DEPTH, D, IN_COLS), D ** -0.5),
        "na_rpb": nrm(ks[9], (DEPTH, NA_HEADS, 2 * NA_KR - 1, 2 * NA_KC - 1), 0.02),
        "hg_lb": nrm(ks[10], (DEPTH, 2, HG_DIM), 0.1),
        "hg_norm": gain(ks[11], (DEPTH, HG_V)),
        "gla_w_alpha": nrm(ks[12], (DEPTH, 2, GLA_RANK, GLA_HEADS * GLA_K), GLA_RANK ** -0.5),
        "gla_b_alpha": nrm(ks[13], (DEPTH, 2, GLA_HEADS * GLA_K), 0.1),
        "gla_norm": gain(ks[14], (DEPTH, GLA_V)),
        "w_out": nrm(ks[15], (DEPTH, D_MIX, D), D_MIX ** -0.5),
        "g_cross": gain(ks[16], (DEPTH, D)),
        "g_mem": gain(ks[17], (DEPTH, D)),
        "ca_wq": nrm(ks[18], (DEPTH, D, D), D ** -0.5),
        "ca_wkv": nrm(ks[19], (DEPTH, D, 2 * D), D ** -0.5),
        "ca_wo": nrm(ks[20], (DEPTH, D, D), D ** -0.5),
        "g_ffn2": gain(ks[21], (DEPTH, D)),
        "w_ffn2_up": nrm(ks[22], (DEPTH, D, 2 * D_FF), D ** -0.5),
        "w_ffn2_down": nrm(ks[23], (DEPTH, D_FF, D), D_FF ** -0.5),
        "g_final": gain(ks[24], (D,)),
    }


def reference(x_prompt, x_sample, mem_prompt, mem_sample, g_ffn1, w_ffn1_up, w_ffn1_down, g_mix,
              w_in, na_rpb, hg_lb, hg_norm, gla_w_alpha, gla_b_alpha, gla_norm, w_out, g_cross,
              g_mem, ca_wq, ca_wkv, ca_wo, g_ffn2, w_ffn2_up, w_ffn2_down, g_final):
    weights = (g_ffn1, w_ffn1_up, w_ffn1_down, g_mix, w_in, na_rpb, hg_lb, hg_norm, gla_w_alpha,
               gla_b_alpha, gla_norm, w_out, g_cross, g_mem, ca_wq, ca_wkv, ca_wo, g_ffn2,
               w_ffn2_up, w_ffn2_down, g_final)
    y_prompt = encode(x_prompt, mem_prompt, *weights)
    y_sample = encode(x_sample, mem_sample, *weights)
    return (y_prompt, y_sample)
```

```python
import numpy as np
from contextlib import ExitStack
import concourse.bass as bass
import concourse.mybir as mybir
from concourse.bass_utils import run_bass_kernel_spmd

F32 = mybir.dt.float32
BF16 = mybir.dt.bfloat16
ALU = mybir.AluOpType
AF = mybir.ActivationFunctionType

ENGS = ("pe", "act", "dve", "pool", "sp")
D = 1024
DT = 8
DFF = 2816
NJ = 22
GW = 64
EPS = 1e-6
NTAB = 14
FM_BF = [0, 128, 256, 384, 512, 640, 768, 896]
FM_F = [1536, 1664, 1792, 1920, 2048, 2176, 2560, 2688, 2816, 2944, 3328, 3456, 3584]
R_HQ, R_HZF, R_HZB, R_HGATE, R_GQ, R_GK, R_GGATE, R_GA = 0, 256, 512, 768, 1024, 1152, 1280, 1536


class V:
    __slots__ = ("b", "ap")

    def __init__(self, b, ap):
        self.b = b
        self.ap = ap

    def __getitem__(self, k):
        return V(self.b, self.ap[k])

    def re(self, pat, **kw):
        return V(self.b, self.ap.rearrange(pat, **kw))

    def bc(self, shape):
        return V(self.b, self.ap.broadcast_to(list(shape)))

    def cast(self, dt):
        return V(self.b, self.ap.bitcast(dt))


class Buf:
    __slots__ = ("name", "ap", "w", "r", "sem", "semv")

    def __init__(self, name, ap=None):
        self.name = name
        self.ap = ap
        self.w = None
        self.r = {}
        self.sem = None
        self.semv = 0

    def __getitem__(self, k):
        return V(self, self.ap[k])

    @property
    def v(self):
        return V(self, self.ap)


class Prog:
    def __init__(self, nc, es):
        self.nc = nc
        self.es = es
        self.sems = []
        self.streams = {e: [] for e in ENGS}
        self.esem = {}
        self.ecnt = {e: 0 for e in ENGS}
        self.seen = {e: {} for e in ENGS}
        for e in ENGS:
            self.esem[e] = len(self.sems)
            self.sems.append(es.enter_context(nc.semaphore("s_" + e)))
        self.bufs = []
        self.ninstr = 0
        self.dbufs = {}
        self.sem_pool = []

    def buf(self, name, ap=None):
        b = Buf(name, ap)
        self.bufs.append(b)
        return b

    def dbuf(self, *key):
        b = self.dbufs.get(key)
        if b is None:
            b = self.buf(str(key))
            self.dbufs[key] = b
        return b

    def _waits(self, eng, reads, writes, extra=()):
        need = {}
        seen = self.seen[eng]

        def add(s, v):
            if seen.get(s, 0) >= v:
                return
            if need.get(s, 0) < v:
                need[s] = v
        for b in reads:
            if b.w is not None:
                add(*b.w)
        for b in writes:
            if b.w is not None:
                add(*b.w)
            for s, v in b.r.items():
                add(s, v)
        for s, v in extra:
            add(s, v)
        for s, v in need.items():
            seen[s] = v
        return list(need.items())

    def _mark(self, tok, reads, writes):
        s, v = tok
        for b in reads:
            if b.r.get(s, 0) < v:
                b.r[s] = v
        for b in writes:
            b.w = tok
            b.r = {}

    def op(self, eng, fn, reads=(), writes=()):
        waits = self._waits(eng, reads, writes)
        self.ecnt[eng] += 1
        tok = (self.esem[eng], self.ecnt[eng])
        self.streams[eng].append((waits, fn, self.esem[eng]))
        self._mark(tok, reads, writes)
        self.ninstr += 1

    def dma(self, eng, pairs, sembuf, reads=(), writes=()):
        if sembuf.sem is None:
            if self.sem_pool:
                sembuf.sem, sembuf.semv = self.sem_pool.pop()
            else:
                sembuf.sem = len(self.sems)
                sembuf.semv = 0
                self.sems.append(self.es.enter_context(self.nc.semaphore("d%d" % len(self.sems))))
        s = sembuf.sem
        waits = self._waits(eng, reads, writes, extra=((s, sembuf.semv),) if sembuf.semv else ())
        np_ = []
        for o, i in pairs:
            sh = tuple(o.shape)
            if len(sh) == 3 and sh[0] * sh[1] > 512 and tuple(i.shape) == sh:
                step = max(1, 512 // sh[0])
                for g0 in range(0, sh[1], step):
                    np_.append((o[:, g0:min(g0 + step, sh[1]), :], i[:, g0:min(g0 + step, sh[1]), :]))
            else:
                np_.append((o, i))
        pairs = np_
        sembuf.semv += 16 * len(pairs)
        tok = (s, sembuf.semv)
        sem = self.sems[s]

        def fn(e):
            for o, i in pairs:
                e.dma_start(out=o, in_=i).then_inc(sem, 16)
            return None
        self.streams[eng].append((waits, fn, None))
        self._mark(tok, reads, writes)
        self.ninstr += len(pairs)

    def barrier(self):
        toks = [(self.esem[e], self.ecnt[e]) for e in ENGS if self.ecnt[e]]
        toks += [(b.sem, b.semv) for b in self.bufs if b.sem is not None and b.semv]
        for e in ENGS:
            waits = self._waits(e, (), (), extra=toks)
            if waits:
                self.streams[e].append((waits, None, None))
        for b in self.bufs:
            if b.sem is not None:
                self.sem_pool.append((b.sem, b.semv))
                b.sem = None
                b.semv = 0

    def emit(self):
        sems = self.sems
        streams = self.streams

        def replay(name, e):
            for waits, fn, inc in streams[name]:
                for s, v in waits:
                    e.wait_ge(sems[s], v)
                if fn is not None:
                    ins = fn(e)
                    if inc is not None:
                        ins.then_inc(sems[inc], 1)
        with self.nc.Block() as block:
            @block.tensor
            def _(e):
                replay("pe", e)

            @block.scalar
            def _(e):
                replay("act", e)

            @block.vector
            def _(e):
                replay("dve", e)

            @block.gpsimd
            def _(e):
                replay("pool", e)

            @block.sync
            def _(e):
                replay("sp", e)


def _bufs(*vs):
    out = []
    for v in vs:
        if v is None or isinstance(v, (int, float)):
            continue
        if v.b not in out:
            out.append(v.b)
    return out


def _a(v):
    return v.ap if isinstance(v, V) else v


class K:
    def __init__(self, SEG, L):
        self.SEG = SEG
        self.L = L
        self.TS = 2 * SEG
        self.TT = 512
        self.NT = self.TS // self.TT
        self.ROWS = SEG // GW
        self.NCH = self.TS // 64

    def mm(self, groups, ps, extra_reads=()):
        reads = []
        gl = []
        for o, ops in groups:
            ol = []
            for l, r in ops:
                for b in (l.b, r.b):
                    if b not in reads:
                        reads.append(b)
                ol.append((l.ap, r.ap))
            gl.append((o.ap, ol))
        for b in extra_reads:
            if b not in reads:
                reads.append(b)

        def fn(e):
            ins = None
            for o, ol in gl:
                n = len(ol)
                for i, (l, r) in enumerate(ol):
                    ins = e.matmul(o, lhsT=l, rhs=r, start=(i == 0), stop=(i == n - 1))
            return ins
        self.P.op("pe", fn, reads=reads, writes=[ps])

    def transp(self, groups, ps, ident):
        reads = [ident.b]
        gl = []
        for o, i in groups:
            if i.b not in reads:
                reads.append(i.b)
            gl.append((o.ap, i.ap))
        ia = ident.ap

        def fn(e):
            ins = None
            for o, i in gl:
                ins = e.transpose(o, i, ia)
            return ins
        self.P.op("pe", fn, reads=reads, writes=[ps])

    def tt(self, eng, out, in0, in1, op):
        o, a, b = out.ap, in0.ap, in1.ap
        self.P.op(eng, lambda e: e.tensor_tensor(out=o, in0=a, in1=b, op=op),
                  reads=_bufs(in0, in1), writes=[out.b])

    def ts(self, eng, out, in0, s1, s2=None, op0=ALU.mult, op1=None):
        o, a = out.ap, in0.ap
        x1, x2 = _a(s1), _a(s2)
        if op1 is None:
            fn = lambda e: e.tensor_scalar(out=o, in0=a, scalar1=x1, scalar2=None, op0=op0)
        else:
            fn = lambda e: e.tensor_scalar(out=o, in0=a, scalar1=x1, scalar2=x2, op0=op0, op1=op1)
        self.P.op(eng, fn, reads=_bufs(in0, s1 if isinstance(s1, V) else None, s2 if isinstance(s2, V) else None),
                  writes=[out.b])

    def stt(self, eng, out, in0, sc, in1, op0, op1):
        o, a, b = out.ap, in0.ap, in1.ap
        s = _a(sc)
        self.P.op(eng, lambda e: e.scalar_tensor_tensor(out=o, in0=a, scalar=s, in1=b, op0=op0, op1=op1),
                  reads=_bufs(in0, in1, sc if isinstance(sc, V) else None), writes=[out.b])

    def act(self, out, in_, func, scale=1.0, bias=0.0):
        o, a = out.ap, in_.ap
        sc, bi = _a(scale), _a(bias)
        self.P.op("act", lambda e: e.activation(out=o, in_=a, func=func, bias=bi, scale=sc),
                  reads=_bufs(in_, scale if isinstance(scale, V) else None, bias if isinstance(bias, V) else None),
                  writes=[out.b])

    def copy(self, eng, out, in_):
        o, a = out.ap, in_.ap
        if eng == "act":
            self.P.op("act", lambda e: e.activation(out=o, in_=a, func=AF.Copy), reads=[in_.b], writes=[out.b])
        else:
            self.P.op(eng, lambda e: e.tensor_copy(out=o, in_=a), reads=[in_.b], writes=[out.b])

    def recip(self, out, in_):
        o, a = out.ap, in_.ap
        self.P.op("dve", lambda e: e.reciprocal(out=o, in_=a), reads=[in_.b], writes=[out.b])

    def memset(self, eng, out, val):
        o = out.ap
        self.P.op(eng, lambda e: e.memset(o, val), writes=[out.b])

    def scan(self, out, d0, d1):
        o, a, b = out.ap, d0.ap, d1.ap
        self.P.op("dve", lambda e: e.tensor_tensor_scan(out=o, data0=a, data1=b, initial=0.0,
                                                       op0=ALU.mult, op1=ALU.add),
                  reads=_bufs(d0, d1), writes=[out.b])

    def load(self, dst, src_ap, reads=(), eng="sp"):
        self.P.dma(eng, [(dst.ap, src_ap)], dst.b, reads=list(reads), writes=[dst.b])

    def store(self, dst_ap, src, writes=(), eng="pool"):
        self.P.dma(eng, [(dst_ap, src.ap)], src.b, reads=[src.b], writes=list(writes))

    def sb(self, name, shape, dt):
        n = int(np.prod(shape[1:])) * (4 if dt == F32 else 2)
        assert self.off % 4 == 0
        a = self.arena[:, self.off // 4:(self.off + n) // 4]
        if dt != F32:
            a = a.bitcast(dt)
        if len(shape) == 3:
            a = a.rearrange("p (a b) -> p a b", a=shape[1])
        elif len(shape) == 4:
            a = a.rearrange("p (a b c) -> p a b c", a=shape[1], b=shape[2])
        if shape[0] < 128:
            a = a[0:shape[0]]
        self.off += (n + 63) // 64 * 64
        assert self.off <= self.ARENA, (name, self.off)
        return self.P.buf(name, a)

    def ring(self, name, n, shape, dt):
        return Ring([self.sb("%s%d" % (name, i), shape, dt) for i in range(n)])

    def psn(self):
        b = self.psb[self.psi % 8]
        self.psi += 1
        return b

    def build(self):
        SEG, L, TS, TT, NT = self.SEG, self.L, self.TS, self.TT, self.NT
        nc = bass.Bass("TRN2", target_bir_lowering=False)
        self.nc = nc

        def din(name, shape, dt=F32):
            return nc.dram_tensor(name, list(shape), dt, kind="ExternalInput").ap()

        def dscr(name, shape, dt):
            return nc.dram_tensor(name, list(shape), dt, kind="Internal").ap()
        I = {}
        I["x"] = din("x", [TS, D])
        I["mem"] = din("mem", [2, 256, D])
        I["cflag"] = din("cflag", [128, 2])
        I["w_ffn1_up"] = din("w_ffn1_up", [L, D, 2 * DFF])
        I["w_ffn1_down"] = din("w_ffn1_down", [L, DFF, D])
        I["w_ffn2_up"] = din("w_ffn2_up", [L, D, 2 * DFF])
        I["w_ffn2_down"] = din("w_ffn2_down", [L, DFF, D])
        I["w_in"] = din("w_in", [L, D, 3616])
        I["w_out"] = din("w_out", [L, D, D])
        I["ca_wq"] = din("ca_wq", [L, D, D])
        I["ca_wkv"] = din("ca_wkv", [L, D, 2 * D])
        I["ca_wo"] = din("ca_wo", [L, D, D])
        NG = 5 * L + 1
        I["gains"] = din("gains", [128, NG * DT])
        I["hglb"] = din("hglb", [128, L * 8])
        I["hnorm"] = din("hnorm", [128, 2 * L])
        I["walpha"] = din("walpha", [16, L * 8 * 64])
        I["balpha"] = din("balpha", [128, L * 8])
        I["rpbg"] = din("rpbg", [L, 128, 8 * NTAB * 64])
        I["namask"] = din("namask", [128, 2 * NTAB * 64])
        I["ident"] = din("ident", [128, 128])
        I["tri"] = din("tri", [64, 2 * 64])
        I["rmask"] = din("rmask", [128, 512])
        self.I = I
        yout = nc.dram_tensor("y", [TS, D], F32, kind="ExternalOutput").ap()

        S = {}
        S["XT"] = dscr("XT", [D, TS], F32)
        S["ZTB"] = dscr("ZTB", [1024, TS], BF16)
        S["ZTF"] = dscr("ZTF", [1568, TS], F32)
        S["VTOK"] = dscr("VTOK", [TS, 1024], BF16)
        S["MIXT"] = dscr("MIXT", [1024, TS], BF16)
        S["OF"] = dscr("OF", [4, 64, TS], F32)
        for l in range(L):
            for f in (1, 2):
                S["UP%d_%d" % (f, l)] = dscr("UP%d_%d" % (f, l), [NJ, 128, 8 * 256], BF16)
                S["DN%d_%d" % (f, l)] = dscr("DN%d_%d" % (f, l), [8, 128, NJ * 128], BF16)
            S["WINF_%d" % l] = dscr("WINF_%d" % l, [21, 128, 8 * 128], BF16)
            S["WINT_%d" % l] = dscr("WINT_%d" % l, [128, 8 * 1024], BF16)
            S["WO_%d" % l] = dscr("WO_%d" % l, [8, 128, 8 * 128], BF16)
            S["WQ_%d" % l] = dscr("WQ_%d" % l, [8, 128, 8 * 128], BF16)
            S["WCO_%d" % l] = dscr("WCO_%d" % l, [8, 128, 8 * 128], BF16)
            S["WK_%d" % l] = dscr("WK_%d" % l, [8, 128, 8 * 128], BF16)
            S["WV_%d" % l] = dscr("WV_%d" % l, [128, 8 * 1024], BF16)
        self.S = S

        with ExitStack() as es:
            P = Prog(nc, es)
            self.P = P
            self.ARENA = 206000
            self.arena = nc.alloc_sbuf_tensor("arena", [128, self.ARENA // 4], F32).ap()
            self.psb = [P.buf("ps%d" % i, es.enter_context(nc.psum_tensor("ps%d" % i, [128, 512], F32))[:, :])
                        for i in range(8)]
            self.psi = 0
            self.off = 0
            self.ident = self.sb("ident", [128, 128], F32)
            self.identb = self.sb("identb", [128, 128], BF16)
            self.ones = self.sb("ones", [128, 128], BF16)
            self.ones64 = self.sb("ones64", [128, 64], BF16)
            self.ones1 = self.sb("ones1", [128, 128], BF16)
            self.gains = self.sb("gains", [128, NG * DT], F32)
            self.cflag = self.sb("cflag", [128, 2], F32)
            self.hglb = self.sb("hglb", [128, L * 8], F32)
            self.lb = self.sb("lb", [128, L * 8], F32)
            self.omlb = self.sb("omlb", [128, L * 8], F32)
            self.hnorm = self.sb("hnorm", [128, 2 * L], F32)
            self.walpha = self.sb("walpha", [16, L * 8 * 64], F32)
            self.walphab = self.sb("walphab", [16, L * 8 * 64], BF16)
            self.balpha = self.sb("balpha", [128, L * 8], F32)
            self.nbalpha = self.sb("nbalpha", [128, L * 8], F32)
            self.tri = self.sb("tri", [64, 2, 64], F32)
            self.rmask = self.sb("rmask", [128, 512], F32)
            self.memn = [self.sb("memn%d" % s, [128, 8, 256], BF16) for s in range(2)]
            self.epsb = self.sb("epsb", [128, 1], F32)
            self.one_f = self.sb("one_f", [128, 1], F32)
            self.base_off = self.off
            for nm in ("gains", "cflag", "hglb", "hnorm", "balpha", "ident", "rmask"):
                self.load(getattr(self, nm).v, I[nm])
            self.load(self.walpha.v, I["walpha"])
            self.load(self.tri.v, I["tri"].rearrange("p (a b) -> p a b", a=2))
            self.memset("pool", self.ones.v, 1.0 / 1024)
            self.memset("pool", self.ones64.v, 1.0 / 64)
            self.memset("pool", self.ones1.v, 1.0)
            self.memset("pool", self.epsb.v, EPS)
            self.memset("pool", self.one_f.v, 1.0)
            self.copy("dve", self.identb.v, self.ident.v)
            self.copy("dve", self.walphab.v, self.walpha.v)
            self.ts("dve", self.nbalpha.v, self.balpha.v, -1.0)
            import os
            stop = int(os.environ.get("KSTOP", "99"))
            stages = [self.lb_compute, self.phase_mem, self.phase_weights]
            for l in range(L):
                stages += [lambda l=l: self.phase_a(l), lambda l=l: self.phase_na(l), lambda l=l: self.phase_rec(l, 0),
                           lambda l=l: self.phase_rec(l, 1), lambda l=l: self.phase_c(l, yout if l == L - 1 else None)]
            for i, st in enumerate(stages):
                if i > stop:
                    break
                st()
                P.barrier()
            P.emit()
        return nc

    def gain(self, which, l, dt):
        c = (which * self.L + l) * DT + dt
        return self.gains[:, c:c + 1]

    def lb_compute(self):
        L = self.L
        e = self.sb("lb_e", [128, L * 8], F32)
        s = self.sb("lb_s", [128, 8], F32)
        self.act(e.v, self.hglb.v, AF.Exp)
        self.copy("dve", s.v, e[:, 0:8])
        for l in range(1, L):
            self.tt("dve", s.v, s.v, e[:, 8 * l:8 * l + 8], ALU.add)
        self.recip(s.v, s.v)
        self.memset("dve", self.lb[:, 0:8], 0.0)
        for l in range(1, L):
            self.tt("dve", e[:, 8 * l:8 * l + 8], e[:, 8 * l:8 * l + 8], s.v, ALU.mult)
            self.tt("dve", self.lb[:, 8 * l:8 * l + 8], self.lb[:, 8 * l - 8:8 * l], e[:, 8 * l:8 * l + 8], ALU.add)
        self.ts("dve", self.omlb.v, self.lb.v, -1.0, 1.0, ALU.mult, ALU.add)

    def rstd_from(self, sq, rstd, nfree, ones=None):
        ps = self.psn()
        o = ps[:, 0:nfree]
        ones = self.ones.v if ones is None else ones
        self.mm([(o, [(ones, sq[:, k, :]) for k in range(DT)])], ps)
        self.act(rstd, o, AF.Ln, bias=self.epsb[:, 0:1])
        self.act(rstd, rstd, AF.Exp, scale=-0.5)

    def phase_mem(self):
        self.off = self.base_off
        for s in range(2):
            memT = self.sb("memT", [128, 8, 256], F32)
            for mt in range(2):
                mt_in = self.sb("mem_in", [128, D], F32)
                self.load(mt_in.v, self.I["mem"][s, mt * 128:(mt + 1) * 128, :])
                for half in range(2):
                    ps = self.psn()
                    self.transp([(ps[:, k * 128:(k + 1) * 128], mt_in[:, (half * 4 + k) * 128:(half * 4 + k + 1) * 128])
                                 for k in range(4)], ps, self.ident.v)
                    self.copy("dve", memT[:, half * 4:half * 4 + 4, mt * 128:(mt + 1) * 128],
                              ps.v.re("p (a b) -> p a b", a=4))
            sq = self.sb("mem_sq", [128, 8, 256], BF16)
            rstd = self.sb("mem_rstd", [128, 256], F32)
            self.tt("pool", sq.v, memT.v, memT.v, ALU.mult)
            self.rstd_from(sq.v, rstd.v, 256)
            self.tt("dve", self.memn[s].v, memT.v, rstd.v.re("p (o b) -> p o b", o=1).bc([128, 8, 256]), ALU.mult)

    def phase_weights(self):
        I, S, L = self.I, self.S, self.L
        self.off = self.base_off
        stg = self.ring("wst", 3, [128, 2816], F32)
        stb = self.ring("wsb", 3, [128, 2816], BF16)
        self.wcnt = 0
        engs = ("act", "dve", "pool")

        def piece(src_ap, ncols, g, outs):
            a = stg.next()
            b = stb.next()
            self.load(a[:, 0:ncols], src_ap)
            eng = engs[self.wcnt % 3]
            self.wcnt += 1
            if g is None:
                self.copy(eng, b[:, 0:ncols], a[:, 0:ncols])
            elif eng == "act":
                self.act(b[:, 0:ncols], a[:, 0:ncols], AF.Copy, scale=g)
            else:
                self.ts(eng, b[:, 0:ncols], a[:, 0:ncols], g)
            prs = []
            for dst, c0, c1, w in outs:
                src = b.ap[:, c0:c1]
                if w is not None:
                    src = src.rearrange("p (c w) -> p c w", w=w)
                    nchk = (c1 - c0) // w
                    for g0 in range(0, nchk, 4):
                        prs.append((dst[:, g0:min(g0 + 4, nchk), :], src[:, g0:min(g0 + 4, nchk), :]))
                else:
                    prs.append((dst, src))
            self.P.dma("pool", prs, b, reads=[b], writes=[])

        for l in range(L):
            for f, wu, wd, gi in ((1, "w_ffn1_up", "w_ffn1_down", 0), (2, "w_ffn2_up", "w_ffn2_down", 4)):
                UP = S["UP%d_%d" % (f, l)].rearrange("c p (k w) -> p c k w", k=8)
                DN = S["DN%d_%d" % (f, l)].rearrange("c p (k w) -> p c k w", k=NJ)
                for kt in range(8):
                    g = self.gain(gi, l, kt)
                    rows = I[wu][l, kt * 128:(kt + 1) * 128, :]
                    piece(rows[:, 0:DFF], DFF, g, [(UP[:, :, kt, 0:128], 0, DFF, 128)])
                    piece(rows[:, DFF:2 * DFF], DFF, g, [(UP[:, :, kt, 128:256], 0, DFF, 128)])
                for kt in range(NJ):
                    rows = I[wd][l, kt * 128:(kt + 1) * 128, :]
                    piece(rows, D, None, [(DN[:, :, kt, :], 0, D, 128)])
            WINF = S["WINF_%d" % l].rearrange("c p (k w) -> p c k w", k=8)
            WINT = S["WINT_%d" % l].rearrange("p (k w) -> p k w", k=8)
            for kt in range(8):
                g = self.gain(1, l, kt)
                rows = I["w_in"][l, kt * 128:(kt + 1) * 128, :]
                piece(rows[:, 0:1024], 1024, g, [(WINF[:, 0:8, kt, :], 0, 1024, 128)])
                piece(rows[:, 1024:1536], 512, g, [(WINT[:, kt, 0:512], 0, 512, None)])
                b0 = 1536
                piece(rows[:, 1536:3616], 2080, g, [
                    (WINF[:, 8:14, kt, :], 1536 - b0, 2304 - b0, 128),
                    (WINT[:, kt, 512:768], 2304 - b0, 2560 - b0, None),
                    (WINF[:, 14:18, kt, :], 2560 - b0, 3072 - b0, 128),
                    (WINT[:, kt, 768:1024], 3072 - b0, 3328 - b0, None),
                    (WINF[:, 18:20, kt, :], 3328 - b0, 3584 - b0, 128),
                    (WINF[:, 20, kt, 0:32], 3584 - b0, 3616 - b0, None),
                ])
            for nm, src, gi in (("WO", "w_out", None), ("WQ", "ca_wq", 2), ("WCO", "ca_wo", None)):
                W = S["%s_%d" % (nm, l)].rearrange("c p (k w) -> p c k w", k=8)
                for kt in range(8):
                    g = None if gi is None else self.gain(gi, l, kt)
                    piece(I[src][l, kt * 128:(kt + 1) * 128, :], D, g, [(W[:, :, kt, :], 0, D, 128)])
            WK = S["WK_%d" % l].rearrange("c p (k w) -> p c k w", k=8)
            WV = S["WV_%d" % l].rearrange("p (k w) -> p k w", k=8)
            for kt in range(8):
                g = self.gain(3, l, kt)
                rows = I["ca_wkv"][l, kt * 128:(kt + 1) * 128, :]
                piece(rows[:, 0:1024], 1024, g, [(WK[:, :, kt, :], 0, 1024, 128)])
                piece(rows[:, 1024:2048], 1024, g, [(WV[:, kt, :], 0, 1024, None)])

    def wload(self, src_ap, shape):
        slot = self.wslots.next()
        n = shape[1] * shape[2]
        v = slot[:, 0:n].re("p (a b) -> p a b", a=shape[1])
        self.load(v, src_ap)
        return v

    def norm_h(self, x, h, sq, rstd):
        self.tt("pool", sq.v, x.v, x.v, ALU.mult)
        self.rstd_from(sq.v, rstd.v, self.TT)
        self.tt("dve", h.v, x.v, rstd.v.re("p (o b) -> p o b", o=1).bc([128, DT, self.TT]), ALU.mult)

    def proj_fm_resid(self, x, rhs, wname, l, scale):
        W = self.S["%s_%d" % (wname, l)]
        for dt in range(DT):
            w = self.wload(W[dt].rearrange("p (k w) -> p k w", k=8), [128, 8, 128])
            ps = self.psn()
            self.mm([(ps.v, [(w[:, k, :], rhs[:, k, :]) for k in range(8)])], ps)
            self.stt("dve", x[:, dt, :], ps.v, scale, x[:, dt, :], ALU.mult, ALU.add)

    def ffn(self, x, h, hid, f, l):
        UP = self.S["UP%d_%d" % (f, l)]
        DN = self.S["DN%d_%d" % (f, l)]
        for j in range(NJ):
            w = self.wload(UP[j].rearrange("p (k w) -> p k w", k=8), [128, 8, 256])
            pg = self.psn()
            pu = self.psn()
            self.mm([(pg.v, [(w[:, k, 0:128], h[:, k, :]) for k in range(8)])], pg)
            self.mm([(pu.v, [(w[:, k, 128:256], h[:, k, :]) for k in range(8)])], pu)
            t = self.silu_t.next()
            self.act(t.v, pg.v, AF.Silu)
            self.tt("dve", hid[:, j, :], t.v, pu.v, ALU.mult)
        for dt in range(DT):
            w = self.wload(DN[dt].rearrange("p (k w) -> p k w", k=NJ), [128, NJ, 128])
            ps = self.psn()
            self.mm([(ps.v, [(w[:, j, :], hid[:, j, :]) for j in range(NJ)])], ps)
            self.stt("dve", x[:, dt, :], ps.v, 0.5, x[:, dt, :], ALU.mult, ALU.add)

    def alloc_tok(self):
        TT = self.TT
        self.off = self.base_off
        self.xr = self.ring("x", 2, [128, DT, TT], F32)
        self.hr = self.ring("h", 2, [128, DT, TT], BF16)
        self.sq = self.sb("sq", [128, DT, TT], BF16)
        self.rstd = self.ring("rstd", 2, [128, TT], F32)
        self.hid = self.sb("hid", [128, NJ, TT], BF16)
        self.wslots = self.ring("wslot", 4, [128, 4096], BF16)
        self.silu_t = self.ring("silu", 2, [128, TT], F32)

    def phase_a(self, l):
        S, I, TT, NT = self.S, self.I, self.TT, self.NT
        self.alloc_tok()
        self.stg_f = self.ring("stgf", 3, [128, TT], F32)
        self.stg_b = self.ring("stgb", 3, [128, TT], BF16)
        self.tin = self.ring("tin", 2, [128, D], F32)
        XT = S["XT"].rearrange("(k p) t -> p k t", p=128)
        WINF = S["WINF_%d" % l]
        WINT = S["WINT_%d" % l].rearrange("p (k w) -> p k w", k=8)
        for ti in range(NT):
            t0 = ti * TT
            x = self.xr.next()
            if l == 0:
                for sub in range(TT // 128):
                    tin = self.tin.next()
                    self.load(tin.v, I["x"][t0 + sub * 128:t0 + (sub + 1) * 128, :])
                    for half in range(2):
                        ps = self.psn()
                        self.transp([(ps[:, k * 128:(k + 1) * 128],
                                      tin[:, (half * 4 + k) * 128:(half * 4 + k + 1) * 128]) for k in range(4)],
                                    ps, self.ident.v)
                        self.copy("act" if half else "dve", x[:, half * 4:half * 4 + 4, sub * 128:(sub + 1) * 128],
                                  ps.v.re("p (a b) -> p a b", a=4))
            else:
                self.load(x.v, XT[:, :, t0:t0 + TT], reads=[self.P.dbuf("XT", ti)])
            h = self.hr.next()
            self.norm_h(x, h, self.sq, self.rstd.next())
            self.ffn(x, h, self.hid, 1, l)
            h2 = self.hr.next()
            self.norm_h(x, h2, self.sq, self.rstd.next())
            self.store(XT[:, :, t0:t0 + TT], x.v, writes=[self.P.dbuf("XT", ti)])
            for ci in range(21):
                ncol = 128 if ci < 20 else 32
                w = self.wload(WINF[ci].rearrange("p (k w) -> p k w", k=8), [128, 8, 128])
                ps = self.psn()
                self.mm([(ps[0:ncol, :], [(w[:, k, 0:ncol], h2[:, k, :]) for k in range(8)])], ps)
                if ci < 8:
                    st = self.stg_b.next()
                    self.copy("act" if ci % 2 else "dve", st.v, ps.v)
                    self.store(S["ZTB"][ci * 128:(ci + 1) * 128, t0:t0 + TT], st.v, writes=[self.P.dbuf("ZTB", ti)])
                else:
                    st = self.stg_f.next()
                    self.copy("act" if ci % 2 else "dve", st[0:ncol, :], ps[0:ncol, :])
                    r0 = (ci - 8) * 128
                    self.store(S["ZTF"][r0:r0 + ncol, t0:t0 + TT], st[0:ncol, :], writes=[self.P.dbuf("ZTF", ti)])
            for half in range(2):
                w = self.wload(WINT[:, :, half * 512:(half + 1) * 512], [128, 8, 512])
                for sub in range(TT // 128):
                    ps = self.psn()
                    self.mm([(ps.v, [(h2[:, k, sub * 128:(sub + 1) * 128], w[:, k, :]) for k in range(8)])], ps)
                    st = self.stg_b.next()
                    self.copy("act" if sub % 2 else "dve", st.v, ps.v)
                    self.store(S["VTOK"][t0 + sub * 128:t0 + (sub + 1) * 128, half * 512:(half + 1) * 512], st.v,
                               writes=[self.P.dbuf("VTOK", ti)])

    def phase_c(self, l, yout):
        S, TT, NT = self.S, self.TT, self.NT
        self.alloc_tok()
        mixr = self.ring("mix", 1, [128, DT, TT], BF16)
        qT = self.sb("qT", [128, DT, TT], BF16)
        caT = self.sb("caT", [128, DT, TT], BF16)
        pT = self.ring("pT", 2, [128, 2, TT], BF16)
        rden = self.ring("rden", 2, [128, TT], F32)
        kmem = [self.sb("kmem%d" % s, [128, DT, 256], BF16) for s in range(2)]
        vmem = [self.sb("vmem%d" % s, [128, 2, D], BF16) for s in range(2)]
        yt = self.ring("yt", 2, [128, D], F32)
        XT = S["XT"].rearrange("(k p) t -> p k t", p=128)
        MIXT = S["MIXT"].rearrange("(k p) t -> p k t", p=128)
        WK = S["WK_%d" % l]
        WV = S["WV_%d" % l].rearrange("p (k w) -> p k w", k=8)
        for s in range(2):
            for dt in range(DT):
                w = self.wload(WK[dt].rearrange("p (k w) -> p k w", k=8), [128, 8, 128])
                ps = self.psn()
                self.mm([(ps[:, 0:256], [(w[:, k, :], self.memn[s][:, k, :]) for k in range(8)])], ps)
                self.copy("act" if dt % 2 else "dve", kmem[s][:, dt, :], ps[:, 0:256])
            for half in range(2):
                w = self.wload(WV[:, :, half * 512:(half + 1) * 512], [128, 8, 512])
                for mt in range(2):
                    ps = self.psn()
                    self.mm([(ps.v, [(self.memn[s][:, k, mt * 128:(mt + 1) * 128], w[:, k, :]) for k in range(8)])], ps)
                    self.copy("act" if mt else "dve", vmem[s][:, mt, half * 512:(half + 1) * 512], ps.v)
        for ti in range(NT):
            t0 = ti * TT
            s = t0 // self.SEG
            x = self.xr.next()
            self.load(x.v, XT[:, :, t0:t0 + TT], reads=[self.P.dbuf("XT", ti)])
            mix = mixr.next()
            self.load(mix.v, MIXT[:, :, t0:t0 + TT], reads=[self.P.dbuf("MIXT", ti)])
            self.proj_fm_resid(x, mix, "WO", l, 1.0)
            h = self.hr.next()
            self.norm_h(x, h, self.sq, self.rstd.next())
            WQ = S["WQ_%d" % l]
            for dt in range(DT):
                w = self.wload(WQ[dt].rearrange("p (k w) -> p k w", k=8), [128, 8, 128])
                ps = self.psn()
                self.mm([(ps.v, [(w[:, k, :], h[:, k, :]) for k in range(8)])], ps)
                self.copy("act" if dt % 2 else "dve", qT[:, dt, :], ps.v)
            for hd in range(4):
                p = pT.next()
                for mt in range(2):
                    ps = self.psn()
                    self.mm([(ps.v, [(kmem[s][:, 2 * hd + kk, mt * 128:(mt + 1) * 128], qT[:, 2 * hd + kk, :])
                                     for kk in range(2)])], ps)
                    self.act(p[:, mt, :], ps.v, AF.Exp, scale=1.0 / 16.0)
                psd = self.psn()
                self.mm([(psd.v, [(self.ones1.v, p[:, mt, :]) for mt in range(2)])], psd)
                rd = rden.next()
                self.recip(rd.v, psd.v)
                for dd in range(2):
                    ps = self.psn()
                    c0 = (2 * hd + dd) * 128
                    self.mm([(ps.v, [(vmem[s][:, mt, c0:c0 + 128], p[:, mt, :]) for mt in range(2)])], ps)
                    self.tt("dve", caT[:, 2 * hd + dd, :], ps.v, rd.v, ALU.mult)
            self.proj_fm_resid(x, caT, "WCO", l, 1.0)
            h2 = self.hr.next()
            self.norm_h(x, h2, self.sq, self.rstd.next())
            self.ffn(x, h2, self.hid, 2, l)
            if yout is None:
                self.store(XT[:, :, t0:t0 + TT], x.v, writes=[self.P.dbuf("XT", ti)])
            else:
                rs = self.rstd.next()
                self.tt("pool", self.sq.v, x.v, x.v, ALU.mult)
                self.rstd_from(self.sq.v, rs.v, TT)
                for dt in range(DT):
                    self.stt("dve", x[:, dt, :], x[:, dt, :], self.gain(0, 0, 5 * self.L * DT + dt - 0) if False else
                             self.gains[:, 5 * self.L * DT + dt:5 * self.L * DT + dt + 1], rs.v, ALU.mult, ALU.mult)
                for sub in range(TT // 128):
                    y = yt.next()
                    for half in range(2):
                        ps = self.psn()
                        self.transp([(ps[:, k * 128:(k + 1) * 128], x[:, half * 4 + k, sub * 128:(sub + 1) * 128])
                                     for k in range(4)], ps, self.ident.v)
                        self.copy("act" if half else "dve", y[:, half * 512:(half + 1) * 512], ps.v)
                    self.store(yout[t0 + sub * 128:t0 + (sub + 1) * 128, :], y.v)

    def phase_na(self, l):
        S, I, SEG, TS, ROWS = self.S, self.I, self.SEG, self.TS, self.ROWS
        self.off = self.base_off
        NKT = TS // 128
        tabs = self.sb("tabs", [128, 2, 8, NTAB * 64], BF16)
        msk = self.sb("msk", [128, 2, NTAB * 64], F32)
        gst = self.ring("gst", 2, [128, NTAB * 64], F32)
        qk = self.ring("qk", 2, [64, 2, TS], BF16)
        vg = self.sb("vg", [128, NKT, 256], BF16)
        pf = self.ring("pf", 3, [128, 256], F32)
        pt = self.ring("pt", 3, [128, 256], BF16)
        rd = self.ring("nrd", 2, [64, 256], F32)
        ob = self.ring("ob", 3, [64, 256], BF16)
        oe = self.ring("oe", 2, [64, 256], F32)
        self.load(msk.v, I["namask"].rearrange("p (a b) -> p a b", a=2))
        for h in range(8):
            g = gst.next()
            self.load(g.v, I["rpbg"][l, :, h * NTAB * 64:(h + 1) * NTAB * 64])
            self.act(g.v, g.v, AF.Exp)
            self.tt("dve", tabs[:, 0, h, :], g.v, msk[:, 0, :], ALU.mult)
            self.tt("pool", tabs[:, 1, h, :], g.v, msk[:, 1, :], ALU.mult)
        VT = S["VTOK"].rearrange("(k p) c -> p k c", p=128)
        zdeps = [self.P.dbuf("ZTB", ti) for ti in range(self.NT)]
        vdeps = [self.P.dbuf("VTOK", ti) for ti in range(self.NT)]

        def block(h, hl, q, Q0, a_list, which, out_v):
            qs = q[:, 0, Q0 * 64:Q0 * 64 + 256]
            psn_ = self.psn()
            psd_ = self.psn()
            nk = len(a_list)
            for u, a in enumerate(a_list):
                n0 = 6 - (a - Q0)
                ps = self.psn()
                self.mm([(ps[:, 0:256], [(q[:, 1, a * 64:a * 64 + 128], qs)])], ps)
                p_f = pf.next()
                self.act(p_f.v, ps[:, 0:256], AF.Exp, scale=0.125)
                p_t = pt.next()
                self.tt("dve", p_t.v, p_f.v, tabs[:, which, h, n0 * 64:(n0 + 4) * 64], ALU.mult)
                kt = a // 2

                def fn(e, o1=psn_.ap[0:64, 0:256], o2=psd_.ap[0:64, 0:256], va=vg.ap[:, kt, hl * 64:(hl + 1) * 64],
                       on=self.ones1.ap[:, 0:64], pa=p_t.ap, first=(u == 0), last=(u == nk - 1)):
                    e.matmul(o1, lhsT=va, rhs=pa, start=first, stop=last)
                    return e.matmul(o2, lhsT=on, rhs=pa, start=first, stop=last)
                self.P.op("pe", fn, reads=[vg, self.ones1, p_t], writes=[psn_, psd_])
            r = rd.next()
            self.recip(r.v, psd_[0:64, 0:256])
            self.tt("dve", out_v, psn_[0:64, 0:256], r.v, ALU.mult)

        for hg in range(2):
            self.P.dma("sp", [(vg.ap[:, k0:k0 + 16, :], VT[:, k0:k0 + 16, hg * 256:(hg + 1) * 256])
                              for k0 in range(0, NKT, 16)], vg, reads=vdeps, writes=[vg])
            for hl in range(4):
                h = hg * 4 + hl
                q = qk.next()
                self.P.dma("sp", [(q.ap[:, 0, :], S["ZTB"][h * 64:(h + 1) * 64, :]),
                                  (q.ap[:, 1, :], S["ZTB"][512 + h * 64:512 + (h + 1) * 64, :])],
                           q, reads=zdeps, writes=[q])
                for s in range(2):
                    R0 = s * ROWS
                    for jb in range(ROWS // 4):
                        Q0 = R0 + 4 * jb
                        if jb == 0:
                            al, which = [R0 + 6, R0 + 4, R0 + 2, R0], 1
                        elif jb == ROWS // 4 - 1:
                            al, which = [R0 + ROWS - 2, R0 + ROWS - 4, R0 + ROWS - 6, R0 + ROWS - 8], 1
                        else:
                            al, which = [Q0 - 4 + 2 * t for t in range(5, -1, -1)], 0
                        o = ob.next()
                        mid = (s == 0 and jb == ROWS // 4 - 1) or (s == 1 and jb == 0)
                        if not mid:
                            block(h, hl, q, Q0, al, which, o.v)
                        else:
                            e1 = oe.next()
                            e2 = oe.next()
                            block(h, hl, q, Q0, al, which, e1.v)
                            block(h, hl, q, Q0, [Q0 - 4 + 2 * t for t in range(5, -1, -1)], 0, e2.v)
                            self.ts("pool", e1.v, e1.v, self.cflag[0:64, 1:2])
                            self.stt("dve", o.v, e2.v, self.cflag[0:64, 0:1], e1.v, ALU.mult, ALU.add)
                        self.store(S["MIXT"][h * 64:(h + 1) * 64, Q0 * 64:Q0 * 64 + 256], o.v,
                                   writes=[self.P.dbuf("MIXT", (Q0 * 64) // self.TT)])

    def phase_rec(self, l, mixer):
        S, I, SEG, TS, TT, NT, L = self.S, self.I, self.SEG, self.TS, self.TT, self.NT, self.L
        P = self.P
        self.off = self.base_off
        CPT = TT // 64
        f3 = [64, 4, TT]
        qin = self.ring("r_q", 2, f3, F32)
        zin = self.ring("r_z", 2, f3, F32)
        gaf = self.ring("r_ga", 2, [16, TT], F32)
        gafb = self.ring("r_gab", 2, [16, TT], BF16)
        t1 = self.sb("r_t1", f3, F32)
        t2 = self.sb("r_t2", f3, F32)
        gl = self.sb("r_g", f3, F32)
        be = self.sb("r_be", f3, F32)
        eq = self.sb("r_eq", f3, F32)
        ek = self.sb("r_ek", f3, F32)
        qt = self.ring("r_qt", 2, f3, BF16)
        kt_ = self.ring("r_kt", 2, f3, BF16)
        fac = self.ring("r_fac", 2, [64, 3, 4, CPT], F32)
        vtok = self.ring("r_v", 2, [64, CPT, 256], BF16)
        ktok = self.ring("r_ktok", 3, [64, 4, 64], BF16)
        atm = self.ring("r_atm", 3, [64, 4, 64], BF16)
        T = self.sb("r_T", [64, 4, 64], F32)
        Sp = self.ring("r_Sp", 3, [64, 4, 64], BF16)
        Wc = self.ring("r_W", 3, [64, 4, 64], F32)
        osum = self.ring("r_os", 2, [64, 4, TT], F32)
        ofl = self.ring("r_of", 1, [64, 4, TT], F32)
        gate = self.ring("r_gate", 1, [64, 4, TT], F32)
        sqo = self.sb("r_sqo", [64, 4, TT], BF16)
        rs = self.sb("r_rs", [64, 4, TT], F32)
        og = self.ring("r_og", 2, [64, 4, TT], BF16)
        if mixer == 1:
            for b in qin.items + zin.items:
                self.memset("pool", b.v, 0.0)
        ZTF = S["ZTF"]
        VT = S["VTOK"]
        zdeps = lambda ti: [P.dbuf("ZTF", ti)]
        vcol = 512 if mixer == 0 else 768
        rq = R_HQ if mixer == 0 else R_GQ
        rgate = R_HGATE if mixer == 0 else R_GGATE
        moff = 4 * 64 * mixer + 512
        idn64 = self.ident[0:64, 0:64]

        def headload(dst, row0, ti):
            t0 = ti * TT
            if mixer == 0:
                P.dma("sp", [(dst.ap, ZTF[row0:row0 + 256, t0:t0 + TT].rearrange("(h k) t -> k h t", k=64))],
                      dst, reads=zdeps(ti), writes=[dst])
            else:
                P.dma("sp", [(dst.ap[0:32], ZTF[row0:row0 + 128, t0:t0 + TT].rearrange("(h k) t -> k h t", k=32))],
                      dst, reads=zdeps(ti), writes=[dst])

        for d in range(2):
            order = list(range(NT)) if d == 0 else list(range(NT - 1, -1, -1))
            self.memset("dve", T.v, 0.0)
            for idx, ti in enumerate(order):
                t0 = ti * TT
                seg_start = (idx > 0) and ((t0 // SEG) != ((order[idx - 1] * TT) // SEG))
                q = qin.next()
                headload(q, rq, ti)
                z = zin.next()
                if mixer == 0:
                    headload(z, R_HZF if d == 0 else R_HZB, ti)
                else:
                    headload(z, R_GK, ti)
                    ga = gaf.next()
                    self.load(ga.v, ZTF[R_GA + 16 * d:R_GA + 16 * d + 16, t0:t0 + TT], reads=zdeps(ti))
                vt = vtok.next()
                P.dma("sp", [(vt.ap, VT[t0:t0 + TT, vcol:vcol + 256].rearrange("(c s) f -> s c f", s=64))],
                      vt, reads=[P.dbuf("VTOK", ti)], writes=[vt])
                if d == 1:
                    of_ = ofl.next()
                    P.dma("sp", [(of_.ap, S["OF"][:, :, t0:t0 + TT].rearrange("h v t -> v h t"))], of_,
                          reads=[P.dbuf("OF", ti)], writes=[of_])
                    gt = gate.next()
                    P.dma("sp", [(gt.ap, ZTF[rgate:rgate + 256, t0:t0 + TT].rearrange("(h v) t -> v h t", v=64))],
                          gt, reads=zdeps(ti), writes=[gt])
                if mixer == 0:
                    self.act(t1.v, z.v, AF.Exp, scale=-1.0)
                    self.ts("pool", t1.v, t1.v, 1.0, None, ALU.add)
                    self.recip(t1.v, t1.v)
                    for hh in range(4):
                        c = (l * 2 + d) * 4 + hh
                        self.ts("dve", t2[:, hh, :], t1[:, hh, :], self.omlb[0:64, c:c + 1], self.lb[0:64, c:c + 1],
                                ALU.mult, ALU.add)
                    self.act(gl.v, t2.v, AF.Ln)
                    kk = t1
                    self.ts("pool", kk.v, t2.v, -1.0, 1.0, ALU.mult, ALU.add)
                    qq = q
                else:
                    gb = gafb.next()
                    self.copy("pool", gb.v, ga.v)
                    for hh in range(4):
                        c = (l * 2 + d) * 4 + hh
                        ps = self.psn()
                        self.mm([(ps[0:64, :], [(self.walphab[:, c * 64:(c + 1) * 64], gb.v)])], ps)
                        self.act(t1[:, hh, :], ps[0:64, :], AF.Exp, scale=-1.0, bias=self.nbalpha[0:64, c:c + 1])
                    self.act(t2.v, t1.v, AF.Ln, bias=self.one_f[0:64, 0:1])
                    self.ts("pool", gl.v, t2.v, -1.0 / 16.0)
                    kk = z
                    qq = t1
                    self.ts("pool", qq.v, q.v, 32.0 ** -0.5)
                g4 = gl.v.re("p a (c s) -> p a c s", s=64)
                b4 = be.v.re("p a (c s) -> p a c s", s=64)
                for hh in range(4):
                    self.scan(be[:, hh, :], self.rmask[0:64, :], gl[:, hh, :])
                if d == 0:
                    mpos, lpos = 31, 63
                    bt = be
                else:
                    self.tt("pool", gl.v, gl.v, be.v, ALU.subtract)
                    bt = t2
                    bt4 = bt.v.re("p a (c s) -> p a c s", s=64)
                    for hh in range(4):
                        self.tt("dve", bt4[:, hh], g4[:, hh], b4[:, hh, :, 63:64].bc([64, CPT, 64]), ALU.add)
                    mpos, lpos = 32, 0
                bt4 = bt.v.re("p a (c s) -> p a c s", s=64)
                e4 = eq.v.re("p a (c s) -> p a c s", s=64)
                fc = fac.next()
                f4 = fc.v
                for hh in range(4):
                    self.act(f4[:, 0, hh, :].re("p (c o) -> p c o", o=1), bt4[:, hh, :, mpos:mpos + 1], AF.Exp)
                    self.act(f4[:, 1, hh, :].re("p (c o) -> p c o", o=1), bt4[:, hh, :, lpos:lpos + 1], AF.Exp)
                    self.tt("dve", e4[:, hh], bt4[:, hh], bt4[:, hh, :, mpos:mpos + 1].bc([64, CPT, 64]), ALU.subtract)
                self.act(ek.v, eq.v, AF.Exp, scale=-1.0)
                self.act(eq.v, eq.v, AF.Exp)
                for hh in range(4):
                    self.copy("pool", f4[:, 2, hh, :].re("p (c o) -> p c o", o=1), e4[:, hh, :, lpos:lpos + 1])
                qt_ = qt.next()
                ktl = kt_.next()
                self.tt("dve", qt_.v, qq.v, eq.v, ALU.mult)
                self.tt("pool", ek.v, kk.v, ek.v, ALU.mult)
                self.copy("pool", ktl.v, ek.v)
                osm = osum.next()
                chunks = list(range(CPT)) if d == 0 else list(range(CPT - 1, -1, -1))
                for ci, c in enumerate(chunks):
                    cs = slice(c * 64, (c + 1) * 64)
                    pa = self.psn()
                    self.mm([(pa[0:64, hh * 64:(hh + 1) * 64], [(ktl[:, hh, cs], qt_[:, hh, cs])]) for hh in range(4)], pa)
                    am = atm.next()
                    self.tt("dve", am.v, pa[0:64, 0:256].re("p (h t) -> p h t", h=4),
                            self.tri[:, d, :].re("p (o t) -> p o t", o=1).bc([64, 4, 64]), ALU.mult)
                    ptr = self.psn()
                    self.transp([(ptr[0:64, hh * 64:(hh + 1) * 64], ek[:, hh, cs]) for hh in range(4)], ptr, idn64)
                    ktk = ktok.next()
                    self.copy("act", ktk.v, ptr[0:64, 0:256].re("p (h k) -> p h k", h=4))
                    pu = self.psn()
                    self.mm([(pu[0:64, hh * 64:(hh + 1) * 64], [(ktk[:, hh, :], vt[:, c, hh * 64:(hh + 1) * 64])])
                             for hh in range(4)], pu)
                    if seg_start and ci == 0:
                        self.ts("dve", T.v, T.v, self.cflag[0:64, 0:1])
                    sp = Sp.next()
                    self.tt("pool", sp.v, T.v, f4[:, 0, :, c:c + 1].bc([64, 4, 64]), ALU.mult)
                    w = Wc.next()
                    self.tt("dve", w.v, pu[0:64, 0:256].re("p (a v) -> p a v", a=4), f4[:, 2, :, c:c + 1].bc([64, 4, 64]),
                            ALU.mult)
                    self.tt("dve", T.v, T.v, f4[:, 1, :, c:c + 1].bc([64, 4, 64]), ALU.mult)
                    self.tt("dve", T.v, T.v, w.v, ALU.add)
                    po = self.psn()
                    self.mm([(po[0:64, hh * 64:(hh + 1) * 64],
                              [(vt[:, c, hh * 64:(hh + 1) * 64], am[:, hh, :]), (sp[:, hh, :], qt_[:, hh, cs])])
                             for hh in range(4)], po)
                    if d == 0:
                        self.copy("act", osm[:, :, cs], po[0:64, 0:256].re("p (h t) -> p h t", h=4))
                    else:
                        self.tt("dve", osm[:, :, cs], po[0:64, 0:256].re("p (h t) -> p h t", h=4), of_[:, :, cs], ALU.add)
                if d == 0:
                    P.dma("pool", [(S["OF"][:, :, t0:t0 + TT].rearrange("h v t -> v h t"), osm.ap)], osm,
                          reads=[osm], writes=[P.dbuf("OF", ti)])
                else:
                    self.tt("pool", sqo.v, osm.v, osm.v, ALU.mult)
                    for hh in range(4):
                        ps = self.psn()
                        self.mm([(ps[0:64, :], [(self.ones64[0:64, :], sqo[:, hh, :])])], ps)
                        self.act(rs[:, hh, :], ps[0:64, :], AF.Ln, bias=self.epsb[0:64, 0:1])
                    self.act(rs.v, rs.v, AF.Exp, scale=-0.5)
                    self.stt("dve", osm.v, osm.v, self.hnorm[0:64, mixer * L + l:mixer * L + l + 1], rs.v, ALU.mult, ALU.mult)
                    self.act(rs.v, gt.v, AF.Exp, scale=-1.0)
                    self.ts("pool", rs.v, rs.v, 1.0, None, ALU.add)
                    self.recip(rs.v, rs.v)
                    self.tt("pool", gt.v, gt.v, rs.v, ALU.mult)
                    o_ = og.next()
                    self.tt("dve", o_.v, osm.v, gt.v, ALU.mult)
                    P.dma("pool", [(S["MIXT"][moff:moff + 256, t0:t0 + TT].rearrange("(h v) t -> v h t", v=64), o_.ap)],
                          o_, reads=[o_], writes=[P.dbuf("MIXT", ti)])


class Ring:
    def __init__(self, items):
        self.items = items
        self.i = 0

    def next(self):
        b = self.items[self.i % len(self.items)]
        self.i += 1
        return b


def _host_consts(L, na_rpb, hg_lb, hg_norm, gla_norm, gla_w_alpha, gla_b_alpha, gains_list, g_final):
    out = {}
    NG = 5 * L + 1
    gains = np.zeros((128, NG * DT), np.float32)
    for w, g in enumerate(gains_list):
        for l in range(L):
            gains[:, (w * L + l) * DT:(w * L + l + 1) * DT] = g[l].reshape(DT, 128).T
    gains[:, 5 * L * DT:(5 * L + 1) * DT] = g_final.reshape(DT, 128).T
    out["gains"] = gains
    hglb = np.zeros((128, L * 8), np.float32)
    for l in range(L):
        for d in range(2):
            for hd in range(4):
                hglb[0:64, (l * 2 + d) * 4 + hd] = hg_lb[l, d, hd * 64:(hd + 1) * 64]
    out["hglb"] = hglb
    hn = np.zeros((128, 2 * L), np.float32)
    hn[0:64, 0:L] = hg_norm[:L].T
    hn[0:64, L:2 * L] = gla_norm[:L].T
    out["hnorm"] = hn
    wa = np.zeros((16, L * 8 * 64), np.float32)
    ba = np.zeros((128, L * 8), np.float32)
    for l in range(L):
        for d in range(2):
            for hd in range(4):
                c = (l * 2 + d) * 4 + hd
                wa[:, c * 64:c * 64 + 32] = gla_w_alpha[l, d, :, hd * 32:(hd + 1) * 32]
                ba[0:32, c] = gla_b_alpha[l, d, hd * 32:(hd + 1) * 32]
    out["walpha"] = wa
    out["balpha"] = ba
    p = np.arange(128)
    kc = p % 64
    half = p // 64
    n = np.arange(NTAB)
    qc = np.arange(64)
    dr = 6 - n[None, :] + half[:, None]
    dc = kc[:, None] - qc[None, :] + 15
    c0 = np.clip(qc - 8, 0, 48)
    colok = (kc[:, None] >= c0[None, :]) & (kc[:, None] < c0[None, :] + 16)
    dcc = np.clip(dc, 0, 30)
    drc = np.clip(dr + 7, 0, 14)
    g = na_rpb[:L][:, :, drc[:, :, None], dcc[:, None, :]]
    out["rpbg"] = np.ascontiguousarray(g.transpose(0, 2, 1, 3, 4)).reshape(L, 128, 8 * NTAB * 64).astype(np.float32)
    rowok_int = (dr >= -4) & (dr <= 3)
    rowok_all = (dr >= -7) & (dr <= 7)
    m = np.zeros((128, 2, NTAB, 64), np.float32)
    m[:, 0] = (rowok_int[:, :, None] & colok[:, None, :])
    m[:, 1] = (rowok_all[:, :, None] & colok[:, None, :])
    out["namask"] = m.reshape(128, 2 * NTAB * 64)
    out["ident"] = np.eye(128, dtype=np.float32)
    s = np.arange(64)
    tri = np.zeros((64, 2, 64), np.float32)
    tri[:, 0, :] = (s[None, :] >= s[:, None])
    tri[:, 1, :] = (s[None, :] <= s[:, None])
    out["tri"] = tri.reshape(64, 128)
    rm = np.ones((128, 512), np.float32)
    rm[:, ::64] = 0.0
    out["rmask"] = rm
    return out


_CACHE = {}


def run_slots(SEG, L, slots, weights, n_cores):
    key = (SEG, L)
    if key not in _CACHE:
        _CACHE[key] = K(SEG, L).build()
    nc = _CACHE[key]
    hc = _host_consts(L, weights["na_rpb"], weights["hg_lb"], weights["hg_norm"], weights["gla_norm"],
                      weights["gla_w_alpha"], weights["gla_b_alpha"],
                      [weights["g_ffn1"], weights["g_mix"], weights["g_cross"], weights["g_mem"], weights["g_ffn2"]],
                      weights["g_final"])
    in_maps = []
    for x, mem, c in slots:
        m = dict(hc)
        m["x"] = np.ascontiguousarray(x, dtype=np.float32)
        m["mem"] = np.ascontiguousarray(mem, dtype=np.float32)
        cf = np.zeros((128, 2), np.float32)
        cf[:, 0] = c
        cf[:, 1] = 1 - c
        m["cflag"] = cf
        for nm in ("w_ffn1_up", "w_ffn1_down", "w_ffn2_up", "w_ffn2_down", "w_in", "w_out", "ca_wq", "ca_wkv", "ca_wo"):
            m[nm] = np.ascontiguousarray(weights[nm][:L], dtype=np.float32)
        in_maps.append(m)
    res = run_bass_kernel_spmd(nc, in_maps, core_ids=list(range(n_cores)))
    return [r["y"] for r in res.results]


def kernel(**inputs):
    inputs = {k: np.asarray(v) for k, v in inputs.items()}
    xp, xs = inputs["x_prompt"], inputs["x_sample"]
    mp, ms = inputs["mem_prompt"], inputs["mem_sample"]
    SEG = xp.shape[1]
    L = inputs["w_in"].shape[0]
    slots = []
    for i in range(4):
        slots.append((xs[i], np.stack([ms[i], ms[i]]), 1.0))
    for k in range(2):
        slots.append((np.concatenate([xp[2 * k], xp[2 * k + 1]], axis=0), np.stack([mp[2 * k], mp[2 * k + 1]]), 0.0))
    slots.append(slots[4])
    slots.append(slots[5])
    ys = run_slots(SEG, L, slots, inputs, 8)
    y_sample = np.stack(ys[0:4]).astype(np.float32)
    y_prompt = np.stack([ys[4][:SEG], ys[4][SEG:], ys[5][:SEG], ys[5][SEG:]]).astype(np.float32)
    return (y_prompt, y_sample)
```

```python
import numpy as np
from contextlib import ExitStack
import concourse.bass as bass
import concourse.mybir as mybir
from concourse.bass_utils import run_bass_kernel_spmd

F32 = mybir.dt.float32
BF16 = mybir.dt.bfloat16
ALU = mybir.AluOpType
AF = mybir.ActivationFunctionType

ENGS = ("pe", "act", "dve", "pool", "sp")
D = 1024
DT = 8
DFF = 2816
NJ = 22
GW = 64
EPS = 1e-6
NTAB = 14
FM_BF = [0, 128, 256, 384, 512, 640, 768, 896]
FM_F = [1536, 1664, 1792, 1920, 2048, 2176, 2560, 2688, 2816, 2944, 3328, 3456, 3584]
R_HQ, R_HZF, R_HZB, R_HGATE, R_GQ, R_GK, R_GGATE, R_GA = 0, 256, 512, 768, 1024, 1152, 1280, 1536


class V:
    __slots__ = ("b", "ap")

    def __init__(self, b, ap):
        self.b = b
        self.ap = ap

    def __getitem__(self, k):
        return V(self.b, self.ap[k])

    def re(self, pat, **kw):
        return V(self.b, self.ap.rearrange(pat, **kw))

    def bc(self, shape):
        return V(self.b, self.ap.broadcast_to(list(shape)))

    def cast(self, dt):
        return V(self.b, self.ap.bitcast(dt))


class Buf:
    __slots__ = ("name", "ap", "w", "r", "sem", "semv")

    def __init__(self, name, ap=None):
        self.name = name
        self.ap = ap
        self.w = None
        self.r = {}
        self.sem = None
        self.semv = 0

    def __getitem__(self, k):
        return V(self, self.ap[k])

    @property
    def v(self):
        return V(self, self.ap)


class Prog:
    def __init__(self, nc, es):
        self.nc = nc
        self.es = es
        self.sems = []
        self.streams = {e: [] for e in ENGS}
        self.esem = {}
        self.ecnt = {e: 0 for e in ENGS}
        self.seen = {e: {} for e in ENGS}
        for e in ENGS:
            self.esem[e] = len(self.sems)
            self.sems.append(es.enter_context(nc.semaphore("s_" + e)))
        self.bufs = []
        self.ninstr = 0
        self.dbufs = {}
        self.sem_pool = []

    def buf(self, name, ap=None):
        b = Buf(name, ap)
        self.bufs.append(b)
        return b

    def dbuf(self, *key):
        b = self.dbufs.get(key)
        if b is None:
            b = self.buf(str(key))
            self.dbufs[key] = b
        return b

    def _waits(self, eng, reads, writes, extra=()):
        need = {}
        seen = self.seen[eng]

        def add(s, v):
            if seen.get(s, 0) >= v:
                return
            if need.get(s, 0) < v:
                need[s] = v
        for b in reads:
            if b.w is not None:
                add(*b.w)
        for b in writes:
            if b.w is not None:
                add(*b.w)
            for s, v in b.r.items():
                add(s, v)
        for s, v in extra:
            add(s, v)
        for s, v in need.items():
            seen[s] = v
        return list(need.items())

    def _mark(self, tok, reads, writes):
        s, v = tok
        for b in reads:
            if b.r.get(s, 0) < v:
                b.r[s] = v
        for b in writes:
            b.w = tok
            b.r = {}

    def op(self, eng, fn, reads=(), writes=()):
        waits = self._waits(eng, reads, writes)
        self.ecnt[eng] += 1
        tok = (self.esem[eng], self.ecnt[eng])
        self.streams[eng].append((waits, fn, self.esem[eng]))
        self._mark(tok, reads, writes)
        self.ninstr += 1

    def dma(self, eng, pairs, sembuf, reads=(), writes=()):
        if sembuf.sem is None:
            if self.sem_pool:
                sembuf.sem, sembuf.semv = self.sem_pool.pop()
            else:
                sembuf.sem = len(self.sems)
                sembuf.semv = 0
                self.sems.append(self.es.enter_context(self.nc.semaphore("d%d" % len(self.sems))))
        s = sembuf.sem
        waits = self._waits(eng, reads, writes, extra=((s, sembuf.semv),) if sembuf.semv else ())
        np_ = []
        for o, i in pairs:
            sh = tuple(o.shape)
            if len(sh) == 3 and sh[0] * sh[1] > 512 and tuple(i.shape) == sh:
                step = max(1, 512 // sh[0])
                for g0 in range(0, sh[1], step):
                    np_.append((o[:, g0:min(g0 + step, sh[1]), :], i[:, g0:min(g0 + step, sh[1]), :]))
            else:
                np_.append((o, i))
        pairs = np_
        sembuf.semv += 16 * len(pairs)
        tok = (s, sembuf.semv)
        sem = self.sems[s]

        def fn(e):
            for o, i in pairs:
                e.dma_start(out=o, in_=i).then_inc(sem, 16)
            return None
        self.streams[eng].append((waits, fn, None))
        self._mark(tok, reads, writes)
        self.ninstr += len(pairs)

    def barrier(self):
        toks = [(self.esem[e], self.ecnt[e]) for e in ENGS if self.ecnt[e]]
        toks += [(b.sem, b.semv) for b in self.bufs if b.sem is not None and b.semv]
        for e in ENGS:
            waits = self._waits(e, (), (), extra=toks)
            if waits:
                self.streams[e].append((waits, None, None))
        for b in self.bufs:
            if b.sem is not None:
                self.sem_pool.append((b.sem, b.semv))
                b.sem = None
                b.semv = 0

    def emit(self):
        sems = self.sems
        streams = self.streams

        def replay(name, e):
            for waits, fn, inc in streams[name]:
                for s, v in waits:
                    e.wait_ge(sems[s], v)
                if fn is not None:
                    ins = fn(e)
                    if inc is not None:
                        ins.then_inc(sems[inc], 1)
        with self.nc.Block() as block:
            @block.tensor
            def _(e):
                replay("pe", e)

            @block.scalar
            def _(e):
                replay("act", e)

            @block.vector
            def _(e):
                replay("dve", e)

            @block.gpsimd
            def _(e):
                replay("pool", e)

            @block.sync
            def _(e):
                replay("sp", e)


def _bufs(*vs):
    out = []
    for v in vs:
        if v is None or isinstance(v, (int, float)):
            continue
        if v.b not in out:
            out.append(v.b)
    return out


def _a(v):
    return v.ap if isinstance(v, V) else v


class K:
    def __init__(self, SEG, L):
        self.SEG = SEG
        self.L = L
        self.TS = 2 * SEG
        self.TT = 512
        self.NT = self.TS // self.TT
        self.ROWS = SEG // GW
        self.NCH = self.TS // 64

    def mm(self, groups, ps, extra_reads=()):
        reads = []
        gl = []
        for o, ops in groups:
            ol = []
            for l, r in ops:
                for b in (l.b, r.b):
                    if b not in reads:
                        reads.append(b)
                ol.append((l.ap, r.ap))
            gl.append((o.ap, ol))
        for b in extra_reads:
            if b not in reads:
                reads.append(b)

        def fn(e):
            ins = None
            for o, ol in gl:
                n = len(ol)
                for i, (l, r) in enumerate(ol):
                    ins = e.matmul(o, lhsT=l, rhs=r, start=(i == 0), stop=(i == n - 1))
            return ins
        self.P.op("pe", fn, reads=reads, writes=[ps])

    def transp(self, groups, ps, ident):
        reads = [ident.b]
        gl = []
        for o, i in groups:
            if i.b not in reads:
                reads.append(i.b)
            gl.append((o.ap, i.ap))
        ia = ident.ap

        def fn(e):
            ins = None
            for o, i in gl:
                ins = e.transpose(o, i, ia)
            return ins
        self.P.op("pe", fn, reads=reads, writes=[ps])

    def tt(self, eng, out, in0, in1, op):
        o, a, b = out.ap, in0.ap, in1.ap
        self.P.op(eng, lambda e: e.tensor_tensor(out=o, in0=a, in1=b, op=op),
                  reads=_bufs(in0, in1), writes=[out.b])

    def ts(self, eng, out, in0, s1, s2=None, op0=ALU.mult, op1=None):
        o, a = out.ap, in0.ap
        x1, x2 = _a(s1), _a(s2)
        if op1 is None:
            fn = lambda e: e.tensor_scalar(out=o, in0=a, scalar1=x1, scalar2=None, op0=op0)
        else:
            fn = lambda e: e.tensor_scalar(out=o, in0=a, scalar1=x1, scalar2=x2, op0=op0, op1=op1)
        self.P.op(eng, fn, reads=_bufs(in0, s1 if isinstance(s1, V) else None, s2 if isinstance(s2, V) else None),
                  writes=[out.b])

    def stt(self, eng, out, in0, sc, in1, op0, op1):
        o, a, b = out.ap, in0.ap, in1.ap
        s = _a(sc)
        self.P.op(eng, lambda e: e.scalar_tensor_tensor(out=o, in0=a, scalar=s, in1=b, op0=op0, op1=op1),
                  reads=_bufs(in0, in1, sc if isinstance(sc, V) else None), writes=[out.b])

    def act(self, out, in_, func, scale=1.0, bias=0.0):
        o, a = out.ap, in_.ap
        sc, bi = _a(scale), _a(bias)
        self.P.op("act", lambda e: e.activation(out=o, in_=a, func=func, bias=bi, scale=sc),
                  reads=_bufs(in_, scale if isinstance(scale, V) else None, bias if isinstance(bias, V) else None),
                  writes=[out.b])

    def copy(self, eng, out, in_):
        o, a = out.ap, in_.ap
        if eng == "act":
            self.P.op("act", lambda e: e.activation(out=o, in_=a, func=AF.Copy), reads=[in_.b], writes=[out.b])
        else:
            self.P.op(eng, lambda e: e.tensor_copy(out=o, in_=a), reads=[in_.b], writes=[out.b])

    def recip(self, out, in_):
        o, a = out.ap, in_.ap
        self.P.op("dve", lambda e: e.reciprocal(out=o, in_=a), reads=[in_.b], writes=[out.b])

    def memset(self, eng, out, val):
        o = out.ap
        self.P.op(eng, lambda e: e.memset(o, val), writes=[out.b])

    def scan(self, out, d0, d1):
        o, a, b = out.ap, d0.ap, d1.ap
        self.P.op("dve", lambda e: e.tensor_tensor_scan(out=o, data0=a, data1=b, initial=0.0,
                                                       op0=ALU.mult, op1=ALU.add),
                  reads=_bufs(d0, d1), writes=[out.b])

    def load(self, dst, src_ap, reads=(), eng="sp"):
        self.P.dma(eng, [(dst.ap, src_ap)], dst.b, reads=list(reads), writes=[dst.b])

    def store(self, dst_ap, src, writes=(), eng="pool"):
        self.P.dma(eng, [(dst_ap, src.ap)], src.b, reads=[src.b], writes=list(writes))

    def sb(self, name, shape, dt):
        n = int(np.prod(shape[1:])) * (4 if dt == F32 else 2)
        assert self.off % 4 == 0
        a = self.arena[:, self.off // 4:(self.off + n) // 4]
        if dt != F32:
            a = a.bitcast(dt)
        if len(shape) == 3:
            a = a.rearrange("p (a b) -> p a b", a=shape[1])
        elif len(shape) == 4:
            a = a.rearrange("p (a b c) -> p a b c", a=shape[1], b=shape[2])
        if shape[0] < 128:
            a = a[0:shape[0]]
        self.off += (n + 63) // 64 * 64
        assert self.off <= self.ARENA, (name, self.off)
        return self.P.buf(name, a)

    def ring(self, name, n, shape, dt):
        return Ring([self.sb("%s%d" % (name, i), shape, dt) for i in range(n)])

    ps_lo = 0

    def psn(self):
        lo = self.ps_lo
        b = self.psb[lo + self.psi % (8 - lo)]
        self.psi += 1
        return b

    def build(self):
        SEG, L, TS, TT, NT = self.SEG, self.L, self.TS, self.TT, self.NT
        nc = bass.Bass("TRN2", target_bir_lowering=False)
        self.nc = nc

        def din(name, shape, dt=F32):
            return nc.dram_tensor(name, list(shape), dt, kind="ExternalInput").ap()

        def dscr(name, shape, dt):
            return nc.dram_tensor(name, list(shape), dt, kind="Internal").ap()
        I = {}
        I["x"] = din("x", [TS, D])
        I["mem"] = din("mem", [2, 256, D])
        I["cflag"] = din("cflag", [128, 2])
        I["w_ffn1_up"] = din("w_ffn1_up", [L, D, 2 * DFF])
        I["w_ffn1_down"] = din("w_ffn1_down", [L, DFF, D])
        I["w_ffn2_up"] = din("w_ffn2_up", [L, D, 2 * DFF])
        I["w_ffn2_down"] = din("w_ffn2_down", [L, DFF, D])
        I["w_in"] = din("w_in", [L, D, 3616])
        I["w_out"] = din("w_out", [L, D, D])
        I["ca_wq"] = din("ca_wq", [L, D, D])
        I["ca_wkv"] = din("ca_wkv", [L, D, 2 * D])
        I["ca_wo"] = din("ca_wo", [L, D, D])
        NG = 5 * L + 1
        I["gains"] = din("gains", [128, NG * DT])
        I["hglb"] = din("hglb", [128, L * 8])
        I["hnorm"] = din("hnorm", [128, 2 * L])
        I["walpha"] = din("walpha", [16, L * 8 * 64])
        I["balpha"] = din("balpha", [128, L * 8])
        I["rpbg"] = din("rpbg", [L, 128, 8 * NTAB * 64])
        I["namask"] = din("namask", [128, 2 * NTAB * 64])
        I["ident"] = din("ident", [128, 128])
        I["tri"] = din("tri", [64, 2 * 64])
        I["rmask"] = din("rmask", [128, 512])
        self.I = I
        yout = nc.dram_tensor("y", [TS, D], F32, kind="ExternalOutput").ap()

        S = {}
        S["XT"] = dscr("XT", [D, TS], F32)
        S["ZTB"] = dscr("ZTB", [1024, TS], BF16)
        S["ZTF"] = dscr("ZTF", [1568, TS], F32)
        S["VTOK"] = dscr("VTOK", [TS, 1024], BF16)
        S["MIXT"] = dscr("MIXT", [1024, TS], BF16)
        S["OF"] = dscr("OF", [4, 64, TS], F32)
        for l in range(L):
            for f in (1, 2):
                S["UP%d_%d" % (f, l)] = dscr("UP%d_%d" % (f, l), [NJ, 128, 8 * 256], BF16)
                S["DN%d_%d" % (f, l)] = dscr("DN%d_%d" % (f, l), [8, 128, NJ * 128], BF16)
            S["WINF_%d" % l] = dscr("WINF_%d" % l, [21, 128, 8 * 128], BF16)
            S["WINT_%d" % l] = dscr("WINT_%d" % l, [128, 8 * 1024], BF16)
            S["WO_%d" % l] = dscr("WO_%d" % l, [8, 128, 8 * 128], BF16)
            S["WQ_%d" % l] = dscr("WQ_%d" % l, [8, 128, 8 * 128], BF16)
            S["WCO_%d" % l] = dscr("WCO_%d" % l, [8, 128, 8 * 128], BF16)
            S["WK_%d" % l] = dscr("WK_%d" % l, [8, 128, 8 * 128], BF16)
            S["WV_%d" % l] = dscr("WV_%d" % l, [128, 8 * 1024], BF16)
        self.S = S

        with ExitStack() as es:
            P = Prog(nc, es)
            self.P = P
            self.ARENA = 206000
            self.arena = nc.alloc_sbuf_tensor("arena", [128, self.ARENA // 4], F32).ap()
            self.psb = [P.buf("ps%d" % i, es.enter_context(nc.psum_tensor("ps%d" % i, [128, 512], F32))[:, :])
                        for i in range(8)]
            self.psi = 0
            self.off = 0
            self.ident = self.sb("ident", [128, 128], F32)
            self.identb = self.sb("identb", [128, 128], BF16)
            self.ones = self.sb("ones", [128, 128], BF16)
            self.ones64 = self.sb("ones64", [128, 64], BF16)
            self.ones1 = self.sb("ones1", [128, 128], BF16)
            self.gains = self.sb("gains", [128, NG * DT], F32)
            self.cflag = self.sb("cflag", [128, 2], F32)
            self.hglb = self.sb("hglb", [128, L * 8], F32)
            self.lb = self.sb("lb", [128, L * 8], F32)
            self.omlb = self.sb("omlb", [128, L * 8], F32)
            self.hnorm = self.sb("hnorm", [128, 2 * L], F32)
            self.walpha = self.sb("walpha", [16, L * 8 * 64], F32)
            self.walphab = self.sb("walphab", [16, L * 8 * 64], BF16)
            self.balpha = self.sb("balpha", [128, L * 8], F32)
            self.nbalpha = self.sb("nbalpha", [128, L * 8], F32)
            self.tri = self.sb("tri", [64, 2, 64], F32)
            self.rmask = self.sb("rmask", [128, 512], F32)
            self.memn = [self.sb("memn%d" % s, [128, 8, 256], BF16) for s in range(2)]
            self.epsb = self.sb("epsb", [128, 1], F32)
            self.one_f = self.sb("one_f", [128, 1], F32)
            self.base_off = self.off
            for nm in ("gains", "cflag", "hglb", "hnorm", "balpha", "ident", "rmask"):
                self.load(getattr(self, nm).v, I[nm])
            self.load(self.walpha.v, I["walpha"])
            self.load(self.tri.v, I["tri"].rearrange("p (a b) -> p a b", a=2))
            self.memset("pool", self.ones.v, 1.0 / 1024)
            self.memset("pool", self.ones64.v, 1.0 / 64)
            self.memset("pool", self.ones1.v, 1.0)
            self.memset("pool", self.epsb.v, EPS)
            self.memset("pool", self.one_f.v, 1.0)
            self.copy("dve", self.identb.v, self.ident.v)
            self.copy("dve", self.walphab.v, self.walpha.v)
            self.ts("dve", self.nbalpha.v, self.balpha.v, -1.0)
            import os
            stop = int(os.environ.get("KSTOP", "99"))
            stages = [self.lb_compute, self.phase_mem, self.phase_weights]
            for l in range(L):
                stages += [lambda l=l: self.phase_a(l), lambda l=l: self.phase_na(l), lambda l=l: self.phase_rec(l, 0),
                           lambda l=l: self.phase_rec(l, 1), lambda l=l: self.phase_c(l, yout if l == L - 1 else None)]
            for i, st in enumerate(stages):
                if i > stop:
                    break
                st()
                P.barrier()
            P.emit()
        return nc

    def gain(self, which, l, dt):
        c = (which * self.L + l) * DT + dt
        return self.gains[:, c:c + 1]

    def lb_compute(self):
        L = self.L
        e = self.sb("lb_e", [128, L * 8], F32)
        s = self.sb("lb_s", [128, 8], F32)
        self.act(e.v, self.hglb.v, AF.Exp)
        self.copy("dve", s.v, e[:, 0:8])
        for l in range(1, L):
            self.tt("dve", s.v, s.v, e[:, 8 * l:8 * l + 8], ALU.add)
        self.recip(s.v, s.v)
        self.memset("dve", self.lb[:, 0:8], 0.0)
        for l in range(1, L):
            self.tt("dve", e[:, 8 * l:8 * l + 8], e[:, 8 * l:8 * l + 8], s.v, ALU.mult)
            self.tt("dve", self.lb[:, 8 * l:8 * l + 8], self.lb[:, 8 * l - 8:8 * l], e[:, 8 * l:8 * l + 8], ALU.add)
        self.ts("dve", self.omlb.v, self.lb.v, -1.0, 1.0, ALU.mult, ALU.add)

    def rstd_from(self, sq, rstd, nfree, ones=None):
        ps = self.psn()
        o = ps[:, 0:nfree]
        ones = self.ones.v if ones is None else ones
        self.mm([(o, [(ones, sq[:, k, :]) for k in range(DT)])], ps)
        self.act(rstd, o, AF.Ln, bias=self.epsb[:, 0:1])
        self.act(rstd, rstd, AF.Exp, scale=-0.5)

    def phase_mem(self):
        self.off = self.base_off
        for s in range(2):
            memT = self.sb("memT", [128, 8, 256], F32)
            for mt in range(2):
                mt_in = self.sb("mem_in", [128, D], F32)
                self.load(mt_in.v, self.I["mem"][s, mt * 128:(mt + 1) * 128, :])
                for half in range(2):
                    ps = self.psn()
                    self.transp([(ps[:, k * 128:(k + 1) * 128], mt_in[:, (half * 4 + k) * 128:(half * 4 + k + 1) * 128])
                                 for k in range(4)], ps, self.ident.v)
                    self.copy("dve", memT[:, half * 4:half * 4 + 4, mt * 128:(mt + 1) * 128],
                              ps.v.re("p (a b) -> p a b", a=4))
            sq = self.sb("mem_sq", [128, 8, 256], BF16)
            rstd = self.sb("mem_rstd", [128, 256], F32)
            self.tt("pool", sq.v, memT.v, memT.v, ALU.mult)
            self.rstd_from(sq.v, rstd.v, 256)
            self.tt("dve", self.memn[s].v, memT.v, rstd.v.re("p (o b) -> p o b", o=1).bc([128, 8, 256]), ALU.mult)

    def phase_weights(self):
        I, S, L = self.I, self.S, self.L
        self.off = self.base_off
        stg = self.ring("wst", 3, [128, 2816], F32)
        stb = self.ring("wsb", 3, [128, 2816], BF16)
        self.wcnt = 0
        engs = ("act", "dve", "pool")

        def piece(src_ap, ncols, g, outs):
            a = stg.next()
            b = stb.next()
            self.load(a[:, 0:ncols], src_ap)
            eng = engs[self.wcnt % 3]
            self.wcnt += 1
            if g is None:
                self.copy(eng, b[:, 0:ncols], a[:, 0:ncols])
            elif eng == "act":
                self.act(b[:, 0:ncols], a[:, 0:ncols], AF.Copy, scale=g)
            else:
                self.ts(eng, b[:, 0:ncols], a[:, 0:ncols], g)
            prs = []
            for dst, c0, c1, w in outs:
                src = b.ap[:, c0:c1]
                if w is not None:
                    src = src.rearrange("p (c w) -> p c w", w=w)
                    nchk = (c1 - c0) // w
                    for g0 in range(0, nchk, 4):
                        prs.append((dst[:, g0:min(g0 + 4, nchk), :], src[:, g0:min(g0 + 4, nchk), :]))
                else:
                    prs.append((dst, src))
            self.P.dma("pool", prs, b, reads=[b], writes=[])

        for l in range(L):
            for f, wu, wd, gi in ((1, "w_ffn1_up", "w_ffn1_down", 0), (2, "w_ffn2_up", "w_ffn2_down", 4)):
                UP = S["UP%d_%d" % (f, l)].rearrange("c p (k w) -> p c k w", k=8)
                DN = S["DN%d_%d" % (f, l)].rearrange("c p (k w) -> p c k w", k=NJ)
                for kt in range(8):
                    g = self.gain(gi, l, kt)
                    rows = I[wu][l, kt * 128:(kt + 1) * 128, :]
                    piece(rows[:, 0:DFF], DFF, g, [(UP[:, :, kt, 0:128], 0, DFF, 128)])
                    piece(rows[:, DFF:2 * DFF], DFF, g, [(UP[:, :, kt, 128:256], 0, DFF, 128)])
                for kt in range(NJ):
                    rows = I[wd][l, kt * 128:(kt + 1) * 128, :]
                    piece(rows, D, None, [(DN[:, :, kt, :], 0, D, 128)])
            WINF = S["WINF_%d" % l].rearrange("c p (k w) -> p c k w", k=8)
            WINT = S["WINT_%d" % l].rearrange("p (k w) -> p k w", k=8)
            for kt in range(8):
                g = self.gain(1, l, kt)
                rows = I["w_in"][l, kt * 128:(kt + 1) * 128, :]
                piece(rows[:, 0:1024], 1024, g, [(WINF[:, 0:8, kt, :], 0, 1024, 128)])
                piece(rows[:, 1024:1536], 512, g, [(WINT[:, kt, 0:512], 0, 512, None)])
                b0 = 1536
                piece(rows[:, 1536:3616], 2080, g, [
                    (WINF[:, 8:14, kt, :], 1536 - b0, 2304 - b0, 128),
                    (WINT[:, kt, 512:768], 2304 - b0, 2560 - b0, None),
                    (WINF[:, 14:18, kt, :], 2560 - b0, 3072 - b0, 128),
                    (WINT[:, kt, 768:1024], 3072 - b0, 3328 - b0, None),
                    (WINF[:, 18:20, kt, :], 3328 - b0, 3584 - b0, 128),
                    (WINF[:, 20, kt, 0:32], 3584 - b0, 3616 - b0, None),
                ])
            for nm, src, gi in (("WO", "w_out", None), ("WQ", "ca_wq", 2), ("WCO", "ca_wo", None)):
                W = S["%s_%d" % (nm, l)].rearrange("c p (k w) -> p c k w", k=8)
                for kt in range(8):
                    g = None if gi is None else self.gain(gi, l, kt)
                    piece(I[src][l, kt * 128:(kt + 1) * 128, :], D, g, [(W[:, :, kt, :], 0, D, 128)])
            WK = S["WK_%d" % l].rearrange("c p (k w) -> p c k w", k=8)
            WV = S["WV_%d" % l].rearrange("p (k w) -> p k w", k=8)
            for kt in range(8):
                g = self.gain(3, l, kt)
                rows = I["ca_wkv"][l, kt * 128:(kt + 1) * 128, :]
                piece(rows[:, 0:1024], 1024, g, [(WK[:, :, kt, :], 0, 1024, 128)])
                piece(rows[:, 1024:2048], 1024, g, [(WV[:, kt, :], 0, 1024, None)])

    def wload(self, src_ap, shape):
        slot = self.wslots.next()
        n = shape[1] * shape[2]
        v = slot[:, 0:n].re("p (a b) -> p a b", a=shape[1])
        self.load(v, src_ap)
        return v

    def norm_h(self, x, h, sq, rstd):
        self.tt("pool", sq.v, x.v, x.v, ALU.mult)
        self.rstd_from(sq.v, rstd.v, self.TT)
        self.tt("dve", h.v, x.v, rstd.v.re("p (o b) -> p o b", o=1).bc([128, DT, self.TT]), ALU.mult)

    def proj_fm_resid(self, x, rhs, wname, l, scale):
        W = self.S["%s_%d" % (wname, l)]
        for dt in range(DT):
            w = self.wload(W[dt].rearrange("p (k w) -> p k w", k=8), [128, 8, 128])
            ps = self.psn()
            self.mm([(ps.v, [(w[:, k, :], rhs[:, k, :]) for k in range(8)])], ps)
            self.stt("dve", x[:, dt, :], ps.v, scale, x[:, dt, :], ALU.mult, ALU.add)

    def ffn(self, x, h, hid, f, l):
        UP = self.S["UP%d_%d" % (f, l)]
        DN = self.S["DN%d_%d" % (f, l)]
        for j in range(NJ):
            w = self.wload(UP[j].rearrange("p (k w) -> p k w", k=8), [128, 8, 256])
            pg = self.psn()
            pu = self.psn()
            self.mm([(pg.v, [(w[:, k, 0:128], h[:, k, :]) for k in range(8)])], pg)
            self.mm([(pu.v, [(w[:, k, 128:256], h[:, k, :]) for k in range(8)])], pu)
            t = self.silu_t.next()
            self.act(t.v, pg.v, AF.Silu)
            self.tt("dve", hid[:, j, :], t.v, pu.v, ALU.mult)
        for dt in range(DT):
            w = self.wload(DN[dt].rearrange("p (k w) -> p k w", k=NJ), [128, NJ, 128])
            ps = self.psn()
            self.mm([(ps.v, [(w[:, j, :], hid[:, j, :]) for j in range(NJ)])], ps)
            self.stt("dve", x[:, dt, :], ps.v, 0.5, x[:, dt, :], ALU.mult, ALU.add)

    def alloc_tok(self):
        TT = self.TT
        self.ps_lo = 0
        self.off = self.base_off
        self.xr = self.ring("x", 2, [128, DT, TT], F32)
        self.hr = self.ring("h", 2, [128, DT, TT], BF16)
        self.sq = self.sb("sq", [128, DT, TT], BF16)
        self.rstd = self.ring("rstd", 2, [128, TT], F32)
        self.hid = self.sb("hid", [128, NJ, TT], BF16)
        self.wslots = self.ring("wslot", 4, [128, 4096], BF16)
        self.silu_t = self.ring("silu", 2, [128, TT], F32)

    def phase_a(self, l):
        S, I, TT, NT = self.S, self.I, self.TT, self.NT
        self.alloc_tok()
        self.stg_f = self.ring("stgf", 3, [128, TT], F32)
        self.stg_b = self.ring("stgb", 3, [128, TT], BF16)
        self.tin = self.ring("tin", 2, [128, D], F32)
        XT = S["XT"].rearrange("(k p) t -> p k t", p=128)
        WINF = S["WINF_%d" % l]
        WINT = S["WINT_%d" % l].rearrange("p (k w) -> p k w", k=8)
        for ti in range(NT):
            t0 = ti * TT
            x = self.xr.next()
            if l == 0:
                for sub in range(TT // 128):
                    tin = self.tin.next()
                    self.load(tin.v, I["x"][t0 + sub * 128:t0 + (sub + 1) * 128, :])
                    for half in range(2):
                        ps = self.psn()
                        self.transp([(ps[:, k * 128:(k + 1) * 128],
                                      tin[:, (half * 4 + k) * 128:(half * 4 + k + 1) * 128]) for k in range(4)],
                                    ps, self.ident.v)
                        self.copy("act" if half else "dve", x[:, half * 4:half * 4 + 4, sub * 128:(sub + 1) * 128],
                                  ps.v.re("p (a b) -> p a b", a=4))
            else:
                self.load(x.v, XT[:, :, t0:t0 + TT], reads=[self.P.dbuf("XT", ti)])
            h = self.hr.next()
            self.norm_h(x, h, self.sq, self.rstd.next())
            self.ffn(x, h, self.hid, 1, l)
            h2 = self.hr.next()
            self.norm_h(x, h2, self.sq, self.rstd.next())
            self.store(XT[:, :, t0:t0 + TT], x.v, writes=[self.P.dbuf("XT", ti)])
            for ci in range(21):
                ncol = 128 if ci < 20 else 32
                w = self.wload(WINF[ci].rearrange("p (k w) -> p k w", k=8), [128, 8, 128])
                ps = self.psn()
                self.mm([(ps[0:ncol, :], [(w[:, k, 0:ncol], h2[:, k, :]) for k in range(8)])], ps)
                if ci < 8:
                    st = self.stg_b.next()
                    self.copy("act" if ci % 2 else "dve", st.v, ps.v)
                    self.store(S["ZTB"][ci * 128:(ci + 1) * 128, t0:t0 + TT], st.v, writes=[self.P.dbuf("ZTB", ti)])
                else:
                    st = self.stg_f.next()
                    self.copy("act" if ci % 2 else "dve", st[0:ncol, :], ps[0:ncol, :])
                    r0 = (ci - 8) * 128
                    self.store(S["ZTF"][r0:r0 + ncol, t0:t0 + TT], st[0:ncol, :], writes=[self.P.dbuf("ZTF", ti)])
            for half in range(2):
                w = self.wload(WINT[:, :, half * 512:(half + 1) * 512], [128, 8, 512])
                for sub in range(TT // 128):
                    ps = self.psn()
                    self.mm([(ps.v, [(h2[:, k, sub * 128:(sub + 1) * 128], w[:, k, :]) for k in range(8)])], ps)
                    st = self.stg_b.next()
                    self.copy("act" if sub % 2 else "dve", st.v, ps.v)
                    self.store(S["VTOK"][t0 + sub * 128:t0 + (sub + 1) * 128, half * 512:(half + 1) * 512], st.v,
                               writes=[self.P.dbuf("VTOK", ti)])

    def phase_c(self, l, yout):
        S, TT, NT = self.S, self.TT, self.NT
        self.alloc_tok()
        mixr = self.ring("mix", 1, [128, DT, TT], BF16)
        qT = self.sb("qT", [128, DT, TT], BF16)
        caT = self.sb("caT", [128, DT, TT], BF16)
        pT = self.ring("pT", 2, [128, 2, TT], BF16)
        rden = self.ring("rden", 2, [128, TT], F32)
        kmem = [self.sb("kmem%d" % s, [128, DT, 256], BF16) for s in range(2)]
        vmem = [self.sb("vmem%d" % s, [128, 2, D], BF16) for s in range(2)]
        yt = self.ring("yt", 2, [128, D], F32)
        XT = S["XT"].rearrange("(k p) t -> p k t", p=128)
        MIXT = S["MIXT"].rearrange("(k p) t -> p k t", p=128)
        WK = S["WK_%d" % l]
        WV = S["WV_%d" % l].rearrange("p (k w) -> p k w", k=8)
        for s in range(2):
            for dt in range(DT):
                w = self.wload(WK[dt].rearrange("p (k w) -> p k w", k=8), [128, 8, 128])
                ps = self.psn()
                self.mm([(ps[:, 0:256], [(w[:, k, :], self.memn[s][:, k, :]) for k in range(8)])], ps)
                self.copy("act" if dt % 2 else "dve", kmem[s][:, dt, :], ps[:, 0:256])
            for half in range(2):
                w = self.wload(WV[:, :, half * 512:(half + 1) * 512], [128, 8, 512])
                for mt in range(2):
                    ps = self.psn()
                    self.mm([(ps.v, [(self.memn[s][:, k, mt * 128:(mt + 1) * 128], w[:, k, :]) for k in range(8)])], ps)
                    self.copy("act" if mt else "dve", vmem[s][:, mt, half * 512:(half + 1) * 512], ps.v)
        for ti in range(NT):
            t0 = ti * TT
            s = t0 // self.SEG
            x = self.xr.next()
            self.load(x.v, XT[:, :, t0:t0 + TT], reads=[self.P.dbuf("XT", ti)])
            mix = mixr.next()
            self.load(mix.v, MIXT[:, :, t0:t0 + TT], reads=[self.P.dbuf("MIXT", ti)])
            self.proj_fm_resid(x, mix, "WO", l, 1.0)
            h = self.hr.next()
            self.norm_h(x, h, self.sq, self.rstd.next())
            WQ = S["WQ_%d" % l]
            for dt in range(DT):
                w = self.wload(WQ[dt].rearrange("p (k w) -> p k w", k=8), [128, 8, 128])
                ps = self.psn()
                self.mm([(ps.v, [(w[:, k, :], h[:, k, :]) for k in range(8)])], ps)
                self.copy("act" if dt % 2 else "dve", qT[:, dt, :], ps.v)
            for hd in range(4):
                p = pT.next()
                for mt in range(2):
                    ps = self.psn()
                    self.mm([(ps.v, [(kmem[s][:, 2 * hd + kk, mt * 128:(mt + 1) * 128], qT[:, 2 * hd + kk, :])
                                     for kk in range(2)])], ps)
                    self.act(p[:, mt, :], ps.v, AF.Exp, scale=1.0 / 16.0)
                psd = self.psn()
                self.mm([(psd.v, [(self.ones1.v, p[:, mt, :]) for mt in range(2)])], psd)
                rd = rden.next()
                self.recip(rd.v, psd.v)
                for dd in range(2):
                    ps = self.psn()
                    c0 = (2 * hd + dd) * 128
                    self.mm([(ps.v, [(vmem[s][:, mt, c0:c0 + 128], p[:, mt, :]) for mt in range(2)])], ps)
                    self.tt("dve", caT[:, 2 * hd + dd, :], ps.v, rd.v, ALU.mult)
            self.proj_fm_resid(x, caT, "WCO", l, 1.0)
            h2 = self.hr.next()
            self.norm_h(x, h2, self.sq, self.rstd.next())
            self.ffn(x, h2, self.hid, 2, l)
            if yout is None:
                self.store(XT[:, :, t0:t0 + TT], x.v, writes=[self.P.dbuf("XT", ti)])
            else:
                rs = self.rstd.next()
                self.tt("pool", self.sq.v, x.v, x.v, ALU.mult)
                self.rstd_from(self.sq.v, rs.v, TT)
                for dt in range(DT):
                    self.stt("dve", x[:, dt, :], x[:, dt, :], self.gain(0, 0, 5 * self.L * DT + dt - 0) if False else
                             self.gains[:, 5 * self.L * DT + dt:5 * self.L * DT + dt + 1], rs.v, ALU.mult, ALU.mult)
                for sub in range(TT // 128):
                    y = yt.next()
                    for half in range(2):
                        ps = self.psn()
                        self.transp([(ps[:, k * 128:(k + 1) * 128], x[:, half * 4 + k, sub * 128:(sub + 1) * 128])
                                     for k in range(4)], ps, self.ident.v)
                        self.copy("act" if half else "dve", y[:, half * 512:(half + 1) * 512], ps.v)
                    self.store(yout[t0 + sub * 128:t0 + (sub + 1) * 128, :], y.v)

    def phase_na(self, l):
        S, I, SEG, TS, ROWS = self.S, self.I, self.SEG, self.TS, self.ROWS
        self.off = self.base_off
        self.ps_lo = 0
        NKT = TS // 128
        tabs = self.sb("tabs", [128, 2, 8, NTAB * 64], BF16)
        msk = self.sb("msk", [128, 2, NTAB * 64], F32)
        gst = self.ring("gst", 2, [128, NTAB * 64], F32)
        qk = self.ring("qk", 2, [64, 2, TS], BF16)
        vg = self.sb("vg", [128, NKT, 256], BF16)
        pf = self.ring("pf", 6, [128, 512], F32)
        pt = self.ring("pt", 14, [128, 256], BF16)
        rd = self.ring("nrd", 2, [64, 256], F32)
        ob = self.ring("ob", 4, [64, 256], BF16)
        oe = self.ring("oe", 4, [64, 256], F32)
        self.load(msk.v, I["namask"].rearrange("p (a b) -> p a b", a=2))
        for h in range(8):
            g = gst.next()
            self.load(g.v, I["rpbg"][l, :, h * NTAB * 64:(h + 1) * NTAB * 64])
            self.act(g.v, g.v, AF.Exp)
            self.tt("dve", tabs[:, 0, h, :], g.v, msk[:, 0, :], ALU.mult)
            self.tt("pool", tabs[:, 1, h, :], g.v, msk[:, 1, :], ALU.mult)
        VT = S["VTOK"].rearrange("(k p) c -> p k c", p=128)
        zdeps = [self.P.dbuf("ZTB", ti) for ti in range(self.NT)]
        vdeps = [self.P.dbuf("VTOK", ti) for ti in range(self.NT)]

        state = {"i": 0, "pend": None}

        def scores(h, q, Q0, a_list, which):
            banks = [self.psb[(state["i"] % 2) * 3 + j] for j in range(3)]
            qs = q[:, 0, Q0 * 64:Q0 * 64 + 256]
            nk = len(a_list)
            for u, a in enumerate(a_list):
                bk = banks[u // 2]
                self.mm([(bk[:, (u % 2) * 256:(u % 2 + 1) * 256], [(q[:, 1, a * 64:a * 64 + 128], qs)])], bk)
            pts = []
            for bi in range((nk + 1) // 2):
                bk = banks[bi]
                nt_ = min(2, nk - 2 * bi)
                p_f = pf.next()
                self.act(p_f[:, 0:256 * nt_], bk[:, 0:256 * nt_], AF.Exp, scale=0.125)
                for j in range(nt_):
                    a = a_list[2 * bi + j]
                    n0 = 6 - (a - Q0)
                    p_t = pt.next()
                    self.tt("dve", p_t.v, p_f[:, j * 256:(j + 1) * 256], tabs[:, which, h, n0 * 64:(n0 + 4) * 64], ALU.mult)
                    pts.append((p_t, a // 2))
            acc = self.psb[6 + state["i"] % 2]
            state["i"] += 1
            return pts, acc

        def pv(pts, acc, hl, out_v):
            nk = len(pts)
            kts = [kt for _, kt in pts]
            pas = [p.ap for p, _ in pts]

            def fn(e, o1=acc.ap[0:64, 0:256], o2=acc.ap[0:64, 256:512], on=self.ones1.ap[:, 0:64]):
                ins = None
                for u in range(nk):
                    e.matmul(o1, lhsT=vg.ap[:, kts[u], hl * 64:(hl + 1) * 64], rhs=pas[u], start=(u == 0), stop=(u == nk - 1))
                for u in range(nk):
                    ins = e.matmul(o2, lhsT=on, rhs=pas[u], start=(u == 0), stop=(u == nk - 1))
                return ins
            self.P.op("pe", fn, reads=[vg, self.ones1] + [p for p, _ in pts], writes=[acc])
            r = rd.next()
            self.recip(r.v, acc[0:64, 256:512])
            self.tt("dve", out_v, acc[0:64, 0:256], r.v, ALU.mult)

        def submit(h, hl, q, Q0, al, which, out_v, post):
            cur = (scores(h, q, Q0, al, which), hl, out_v, post)
            flush()
            state["pend"] = cur

        def flush():
            p = state["pend"]
            if p is not None:
                (pts, acc), hl_, out_v, post = p
                pv(pts, acc, hl_, out_v)
                if post is not None:
                    post()
                state["pend"] = None

        for hg in range(2):
            self.P.dma("sp", [(vg.ap[:, k0:k0 + 16, :], VT[:, k0:k0 + 16, hg * 256:(hg + 1) * 256])
                              for k0 in range(0, NKT, 16)], vg, reads=vdeps, writes=[vg])
            for hl in range(4):
                h = hg * 4 + hl
                q = qk.next()
                self.P.dma("sp", [(q.ap[:, 0, :], S["ZTB"][h * 64:(h + 1) * 64, :]),
                                  (q.ap[:, 1, :], S["ZTB"][512 + h * 64:512 + (h + 1) * 64, :])],
                           q, reads=zdeps, writes=[q])
                for s in range(2):
                    R0 = s * ROWS
                    for jb in range(ROWS // 4):
                        Q0 = R0 + 4 * jb
                        if jb == 0:
                            al, which = [R0 + 6, R0 + 4, R0 + 2, R0], 1
                        elif jb == ROWS // 4 - 1:
                            al, which = [R0 + ROWS - 2, R0 + ROWS - 4, R0 + ROWS - 6, R0 + ROWS - 8], 1
                        else:
                            al, which = [Q0 - 4 + 2 * t for t in range(5, -1, -1)], 0
                        o = ob.next()
                        mid = (s == 0 and jb == ROWS // 4 - 1) or (s == 1 and jb == 0)

                        def st(o=o, h=h, Q0=Q0):
                            self.store(S["MIXT"][h * 64:(h + 1) * 64, Q0 * 64:Q0 * 64 + 256], o.v,
                                       writes=[self.P.dbuf("MIXT", (Q0 * 64) // self.TT)])
                        if not mid:
                            submit(h, hl, q, Q0, al, which, o.v, st)
                        else:
                            e1 = oe.next()
                            e2 = oe.next()

                            def blend(o=o, e1=e1, e2=e2, st=st):
                                self.ts("pool", e1.v, e1.v, self.cflag[0:64, 1:2])
                                self.stt("dve", o.v, e2.v, self.cflag[0:64, 0:1], e1.v, ALU.mult, ALU.add)
                                st()
                            submit(h, hl, q, Q0, al, which, e1.v, None)
                            submit(h, hl, q, Q0, [Q0 - 4 + 2 * t for t in range(5, -1, -1)], 0, e2.v, blend)
            flush()

    def phase_rec(self, l, mixer):
        S, I, SEG, TS, TT, NT, L = self.S, self.I, self.SEG, self.TS, self.TT, self.NT, self.L
        P = self.P
        self.off = self.base_off
        CPT = TT // 64
        f3 = [64, 4, TT]
        qin = self.ring("r_q", 2, f3, F32)
        zin = self.ring("r_z", 2, f3, F32)
        gaf = self.ring("r_ga", 2, [16, TT], F32)
        gafb = self.ring("r_gab", 2, [16, TT], BF16)
        t1 = self.sb("r_t1", f3, F32)
        t2 = self.sb("r_t2", f3, F32)
        gl = self.sb("r_g", f3, F32)
        be = self.sb("r_be", f3, F32)
        eq = self.sb("r_eq", f3, F32)
        ek = self.sb("r_ek", f3, F32)
        qt = self.ring("r_qt", 2, f3, BF16)
        kt_ = self.ring("r_kt", 2, f3, BF16)
        fac = self.ring("r_fac", 2, [64, 3, 4, CPT], F32)
        vtok = self.ring("r_v", 2, [64, CPT, 256], BF16)
        ktok = self.ring("r_ktok", 4, [64, 4, 64], BF16)
        atm = self.ring("r_atm", 5, [64, 4, 64], BF16)
        T = self.sb("r_T", [64, 4, 64], F32)
        Sp = self.ring("r_Sp", 4, [64, 4, 64], BF16)
        Wc = self.ring("r_W", 3, [64, 4, 64], F32)
        osum = self.ring("r_os", 2, [64, 4, TT], F32)
        ofl = self.ring("r_of", 1, [64, 4, TT], F32)
        gate = self.ring("r_gate", 1, [64, 4, TT], F32)
        sqo = self.sb("r_sqo", [64, 4, TT], BF16)
        rs = self.sb("r_rs", [64, 4, TT], F32)
        og = self.ring("r_og", 2, [64, 4, TT], BF16)
        if mixer == 1:
            for b in qin.items + zin.items:
                self.memset("pool", b.v, 0.0)
        ZTF = S["ZTF"]
        VT = S["VTOK"]
        zdeps = lambda ti: [P.dbuf("ZTF", ti)]
        vcol = 512 if mixer == 0 else 768
        rq = R_HQ if mixer == 0 else R_GQ
        rgate = R_HGATE if mixer == 0 else R_GGATE
        moff = 4 * 64 * mixer + 512
        idn64 = self.ident[0:64, 0:64]
        self.ps_lo = 6

        def headload(dst, row0, ti):
            t0 = ti * TT
            if mixer == 0:
                P.dma("sp", [(dst.ap, ZTF[row0:row0 + 256, t0:t0 + TT].rearrange("(h k) t -> k h t", k=64))],
                      dst, reads=zdeps(ti), writes=[dst])
            else:
                P.dma("sp", [(dst.ap[0:32], ZTF[row0:row0 + 128, t0:t0 + TT].rearrange("(h k) t -> k h t", k=32))],
                      dst, reads=zdeps(ti), writes=[dst])

        for d in range(2):
            order = list(range(NT)) if d == 0 else list(range(NT - 1, -1, -1))
            self.memset("dve", T.v, 0.0)
            for idx, ti in enumerate(order):
                t0 = ti * TT
                seg_start = (idx > 0) and ((t0 // SEG) != ((order[idx - 1] * TT) // SEG))
                q = qin.next()
                headload(q, rq, ti)
                z = zin.next()
                if mixer == 0:
                    headload(z, R_HZF if d == 0 else R_HZB, ti)
                else:
                    headload(z, R_GK, ti)
                    ga = gaf.next()
                    self.load(ga.v, ZTF[R_GA + 16 * d:R_GA + 16 * d + 16, t0:t0 + TT], reads=zdeps(ti))
                vt = vtok.next()
                P.dma("sp", [(vt.ap, VT[t0:t0 + TT, vcol:vcol + 256].rearrange("(c s) f -> s c f", s=64))],
                      vt, reads=[P.dbuf("VTOK", ti)], writes=[vt])
                if d == 1:
                    of_ = ofl.next()
                    P.dma("sp", [(of_.ap, S["OF"][:, :, t0:t0 + TT].rearrange("h v t -> v h t"))], of_,
                          reads=[P.dbuf("OF", ti)], writes=[of_])
                    gt = gate.next()
                    P.dma("sp", [(gt.ap, ZTF[rgate:rgate + 256, t0:t0 + TT].rearrange("(h v) t -> v h t", v=64))],
                          gt, reads=zdeps(ti), writes=[gt])
                if mixer == 0:
                    self.act(t1.v, z.v, AF.Exp, scale=-1.0)
                    self.ts("pool", t1.v, t1.v, 1.0, None, ALU.add)
                    self.recip(t1.v, t1.v)
                    for hh in range(4):
                        c = (l * 2 + d) * 4 + hh
                        self.ts("dve", t2[:, hh, :], t1[:, hh, :], self.omlb[0:64, c:c + 1], self.lb[0:64, c:c + 1],
                                ALU.mult, ALU.add)
                    self.act(gl.v, t2.v, AF.Ln)
                    kk = t1
                    self.ts("pool", kk.v, t2.v, -1.0, 1.0, ALU.mult, ALU.add)
                    qq = q
                else:
                    gb = gafb.next()
                    self.copy("pool", gb.v, ga.v)
                    for hh in range(4):
                        c = (l * 2 + d) * 4 + hh
                        ps = self.psn()
                        self.mm([(ps[0:64, :], [(self.walphab[:, c * 64:(c + 1) * 64], gb.v)])], ps)
                        self.act(t1[:, hh, :], ps[0:64, :], AF.Exp, scale=-1.0, bias=self.nbalpha[0:64, c:c + 1])
                    self.act(t2.v, t1.v, AF.Ln, bias=self.one_f[0:64, 0:1])
                    self.ts("pool", gl.v, t2.v, -1.0 / 16.0)
                    kk = z
                    qq = t1
                    self.ts("pool", qq.v, q.v, 32.0 ** -0.5)
                g4 = gl.v.re("p a (c s) -> p a c s", s=64)
                b4 = be.v.re("p a (c s) -> p a c s", s=64)
                for hh in range(4):
                    self.scan(be[:, hh, :], self.rmask[0:64, :], gl[:, hh, :])
                if d == 0:
                    mpos, lpos = 31, 63
                    bt = be
                else:
                    self.tt("pool", gl.v, gl.v, be.v, ALU.subtract)
                    bt = t2
                    bt4 = bt.v.re("p a (c s) -> p a c s", s=64)
                    for hh in range(4):
                        self.tt("dve", bt4[:, hh], g4[:, hh], b4[:, hh, :, 63:64].bc([64, CPT, 64]), ALU.add)
                    mpos, lpos = 32, 0
                bt4 = bt.v.re("p a (c s) -> p a c s", s=64)
                e4 = eq.v.re("p a (c s) -> p a c s", s=64)
                fc = fac.next()
                f4 = fc.v
                for hh in range(4):
                    self.act(f4[:, 0, hh, :].re("p (c o) -> p c o", o=1), bt4[:, hh, :, mpos:mpos + 1], AF.Exp)
                    self.act(f4[:, 1, hh, :].re("p (c o) -> p c o", o=1), bt4[:, hh, :, lpos:lpos + 1], AF.Exp)
                    self.tt("dve", e4[:, hh], bt4[:, hh], bt4[:, hh, :, mpos:mpos + 1].bc([64, CPT, 64]), ALU.subtract)
                self.act(ek.v, eq.v, AF.Exp, scale=-1.0)
                self.act(eq.v, eq.v, AF.Exp)
                for hh in range(4):
                    self.copy("pool", f4[:, 2, hh, :].re("p (c o) -> p c o", o=1), e4[:, hh, :, lpos:lpos + 1])
                qt_ = qt.next()
                ktl = kt_.next()
                self.tt("dve", qt_.v, qq.v, eq.v, ALU.mult)
                self.tt("pool", ek.v, kk.v, ek.v, ALU.mult)
                self.copy("pool", ktl.v, ek.v)
                osm = osum.next()
                chunks = list(range(CPT)) if d == 0 else list(range(CPT - 1, -1, -1))
                st_ = {}

                def stage1(ci, c):
                    cs = slice(c * 64, (c + 1) * 64)
                    bk = self.psb[ci % 2]
                    self.mm([(bk[0:64, hh * 64:(hh + 1) * 64], [(ktl[:, hh, cs], qt_[:, hh, cs])]) for hh in range(4)], bk)
                    am = atm.next()
                    self.tt("dve", am.v, bk[0:64, 0:256].re("p (h t) -> p h t", h=4),
                            self.tri[:, d, :].re("p (o t) -> p o t", o=1).bc([64, 4, 64]), ALU.mult)
                    self.transp([(bk[0:64, 256 + hh * 64:256 + (hh + 1) * 64], ek[:, hh, cs]) for hh in range(4)], bk, idn64)
                    ktk = ktok.next()
                    self.copy("act", ktk.v, bk[0:64, 256:512].re("p (h k) -> p h k", h=4))
                    st_[ci] = {"am": am, "ktk": ktk}

                def stage2(ci, c):
                    ktk = st_[ci]["ktk"]
                    pu = self.psb[2 + ci % 2]
                    self.mm([(pu[0:64, hh * 64:(hh + 1) * 64], [(ktk[:, hh, :], vt[:, c, hh * 64:(hh + 1) * 64])])
                             for hh in range(4)], pu)
                    if seg_start and ci == 0:
                        self.ts("dve", T.v, T.v, self.cflag[0:64, 0:1])
                    sp = Sp.next()
                    self.tt("pool", sp.v, T.v, f4[:, 0, :, c:c + 1].bc([64, 4, 64]), ALU.mult)
                    w = Wc.next()
                    self.tt("dve", w.v, pu[0:64, 0:256].re("p (a v) -> p a v", a=4), f4[:, 2, :, c:c + 1].bc([64, 4, 64]),
                            ALU.mult)
                    self.tt("dve", T.v, T.v, f4[:, 1, :, c:c + 1].bc([64, 4, 64]), ALU.mult)
                    self.tt("dve", T.v, T.v, w.v, ALU.add)
                    st_[ci]["sp"] = sp

                def stage3(ci, c):
                    cs = slice(c * 64, (c + 1) * 64)
                    am, sp = st_[ci]["am"], st_[ci]["sp"]
                    po = self.psb[4 + ci % 2]
                    self.mm([(po[0:64, hh * 64:(hh + 1) * 64],
                              [(vt[:, c, hh * 64:(hh + 1) * 64], am[:, hh, :]), (sp[:, hh, :], qt_[:, hh, cs])])
                             for hh in range(4)], po)
                    if d == 0:
                        self.copy("act", osm[:, :, cs], po[0:64, 0:256].re("p (h t) -> p h t", h=4))
                    else:
                        self.tt("dve", osm[:, :, cs], po[0:64, 0:256].re("p (h t) -> p h t", h=4), of_[:, :, cs], ALU.add)

                nch = len(chunks)
                for it in range(nch + 2):
                    if it < nch:
                        stage1(it, chunks[it])
                    if 0 <= it - 1 < nch:
                        stage2(it - 1, chunks[it - 1])
                    if 0 <= it - 2 < nch:
                        stage3(it - 2, chunks[it - 2])
                if d == 0:
                    P.dma("pool", [(S["OF"][:, :, t0:t0 + TT].rearrange("h v t -> v h t"), osm.ap)], osm,
                          reads=[osm], writes=[P.dbuf("OF", ti)])
                else:
                    self.tt("pool", sqo.v, osm.v, osm.v, ALU.mult)
                    for hh in range(4):
                        ps = self.psn()
                        self.mm([(ps[0:64, :], [(self.ones64[0:64, :], sqo[:, hh, :])])], ps)
                        self.act(rs[:, hh, :], ps[0:64, :], AF.Ln, bias=self.epsb[0:64, 0:1])
                    self.act(rs.v, rs.v, AF.Exp, scale=-0.5)
                    self.stt("dve", osm.v, osm.v, self.hnorm[0:64, mixer * L + l:mixer * L + l + 1], rs.v, ALU.mult, ALU.mult)
                    self.act(rs.v, gt.v, AF.Exp, scale=-1.0)
                    self.ts("pool", rs.v, rs.v, 1.0, None, ALU.add)
                    self.recip(rs.v, rs.v)
                    self.tt("pool", gt.v, gt.v, rs.v, ALU.mult)
                    o_ = og.next()
                    self.tt("dve", o_.v, osm.v, gt.v, ALU.mult)
                    P.dma("pool", [(S["MIXT"][moff:moff + 256, t0:t0 + TT].rearrange("(h v) t -> v h t", v=64), o_.ap)],
                          o_, reads=[o_], writes=[P.dbuf("MIXT", ti)])


class Ring:
    def __init__(self, items):
        self.items = items
        self.i = 0

    def next(self):
        b = self.items[self.i % len(self.items)]
        self.i += 1
        return b


def _host_consts(L, na_rpb, hg_lb, hg_norm, gla_norm, gla_w_alpha, gla_b_alpha, gains_list, g_final):
    out = {}
    NG = 5 * L + 1
    gains = np.zeros((128, NG * DT), np.float32)
    for w, g in enumerate(gains_list):
        for l in range(L):
            gains[:, (w * L + l) * DT:(w * L + l + 1) * DT] = g[l].reshape(DT, 128).T
    gains[:, 5 * L * DT:(5 * L + 1) * DT] = g_final.reshape(DT, 128).T
    out["gains"] = gains
    hglb = np.zeros((128, L * 8), np.float32)
    for l in range(L):
        for d in range(2):
            for hd in range(4):
                hglb[0:64, (l * 2 + d) * 4 + hd] = hg_lb[l, d, hd * 64:(hd + 1) * 64]
    out["hglb"] = hglb
    hn = np.zeros((128, 2 * L), np.float32)
    hn[0:64, 0:L] = hg_norm[:L].T
    hn[0:64, L:2 * L] = gla_norm[:L].T
    out["hnorm"] = hn
    wa = np.zeros((16, L * 8 * 64), np.float32)
    ba = np.zeros((128, L * 8), np.float32)
    for l in range(L):
        for d in range(2):
            for hd in range(4):
                c = (l * 2 + d) * 4 + hd
                wa[:, c * 64:c * 64 + 32] = gla_w_alpha[l, d, :, hd * 32:(hd + 1) * 32]
                ba[0:32, c] = gla_b_alpha[l, d, hd * 32:(hd + 1) * 32]
    out["walpha"] = wa
    out["balpha"] = ba
    p = np.arange(128)
    kc = p % 64
    half = p // 64
    n = np.arange(NTAB)
    qc = np.arange(64)
    dr = 6 - n[None, :] + half[:, None]
    dc = kc[:, None] - qc[None, :] + 15
    c0 = np.clip(qc - 8, 0, 48)
    colok = (kc[:, None] >= c0[None, :]) & (kc[:, None] < c0[None, :] + 16)
    dcc = np.clip(dc, 0, 30)
    drc = np.clip(dr + 7, 0, 14)
    g = na_rpb[:L][:, :, drc[:, :, None], dcc[:, None, :]]
    out["rpbg"] = np.ascontiguousarray(g.transpose(0, 2, 1, 3, 4)).reshape(L, 128, 8 * NTAB * 64).astype(np.float32)
    rowok_int = (dr >= -4) & (dr <= 3)
    rowok_all = (dr >= -7) & (dr <= 7)
    m = np.zeros((128, 2, NTAB, 64), np.float32)
    m[:, 0] = (rowok_int[:, :, None] & colok[:, None, :])
    m[:, 1] = (rowok_all[:, :, None] & colok[:, None, :])
    out["namask"] = m.reshape(128, 2 * NTAB * 64)
    out["ident"] = np.eye(128, dtype=np.float32)
    s = np.arange(64)
    tri = np.zeros((64, 2, 64), np.float32)
    tri[:, 0, :] = (s[None, :] >= s[:, None])
    tri[:, 1, :] = (s[None, :] <= s[:, None])
    out["tri"] = tri.reshape(64, 128)
    rm = np.ones((128, 512), np.float32)
    rm[:, ::64] = 0.0
    out["rmask"] = rm
    return out


_CACHE = {}


def run_slots(SEG, L, slots, weights, n_cores):
    key = (SEG, L)
    if key not in _CACHE:
        _CACHE[key] = K(SEG, L).build()
    nc = _CACHE[key]
    hc = _host_consts(L, weights["na_rpb"], weights["hg_lb"], weights["hg_norm"], weights["gla_norm"],
                      weights["gla_w_alpha"], weights["gla_b_alpha"],
                      [weights["g_ffn1"], weights["g_mix"], weights["g_cross"], weights["g_mem"], weights["g_ffn2"]],
                      weights["g_final"])
    in_maps = []
    for x, mem, c in slots:
        m = dict(hc)
        m["x"] = np.ascontiguousarray(x, dtype=np.float32)
        m["mem"] = np.ascontiguousarray(mem, dtype=np.float32)
        cf = np.zeros((128, 2), np.float32)
        cf[:, 0] = c
        cf[:, 1] = 1 - c
        m["cflag"] = cf
        for nm in ("w_ffn1_up", "w_ffn1_down", "w_ffn2_up", "w_ffn2_down", "w_in", "w_out", "ca_wq", "ca_wkv", "ca_wo"):
            m[nm] = np.ascontiguousarray(weights[nm][:L], dtype=np.float32)
        in_maps.append(m)
    res = run_bass_kernel_spmd(nc, in_maps, core_ids=list(range(n_cores)))
    return [r["y"] for r in res.results]


def kernel(**inputs):
    inputs = {k: np.asarray(v) for k, v in inputs.items()}
    xp, xs = inputs["x_prompt"], inputs["x_sample"]
    mp, ms = inputs["mem_prompt"], inputs["mem_sample"]
    SEG = xp.shape[1]
    L = inputs["w_in"].shape[0]
    slots = []
    for i in range(4):
        slots.append((xs[i], np.stack([ms[i], ms[i]]), 1.0))
    for k in range(2):
        slots.append((np.concatenate([xp[2 * k], xp[2 * k + 1]], axis=0), np.stack([mp[2 * k], mp[2 * k + 1]]), 0.0))
    slots.append(slots[4])
    slots.append(slots[5])
    ys = run_slots(SEG, L, slots, inputs, 8)
    y_sample = np.stack(ys[0:4]).astype(np.float32)
    y_prompt = np.stack([ys[4][:SEG], ys[4][SEG:], ys[5][:SEG], ys[5][SEG:]]).astype(np.float32)
    return (y_prompt, y_sample)
```

```python
import numpy as np
from contextlib import ExitStack
import concourse.bass as bass
import concourse.mybir as mybir
from concourse.bass_utils import run_bass_kernel_spmd

F32 = mybir.dt.float32
BF16 = mybir.dt.bfloat16
ALU = mybir.AluOpType
AF = mybir.ActivationFunctionType

ENGS = ("pe", "act", "dve", "pool", "sp")
D = 1024
DT = 8
DFF = 2816
NJ = 22
GW = 64
EPS = 1e-6
NTAB = 14
FM_BF = [0, 128, 256, 384, 512, 640, 768, 896]
FM_F = [1536, 1664, 1792, 1920, 2048, 2176, 2560, 2688, 2816, 2944, 3328, 3456, 3584]
R_HQ, R_HZF, R_HZB, R_HGATE, R_GQ, R_GK, R_GGATE, R_GA = 0, 256, 512, 768, 1024, 1152, 1280, 1536


class V:
    __slots__ = ("b", "ap")

    def __init__(self, b, ap):
        self.b = b
        self.ap = ap

    def __getitem__(self, k):
        return V(self.b, self.ap[k])

    def re(self, pat, **kw):
        return V(self.b, self.ap.rearrange(pat, **kw))

    def bc(self, shape):
        return V(self.b, self.ap.broadcast_to(list(shape)))

    def cast(self, dt):
        return V(self.b, self.ap.bitcast(dt))


class Buf:
    __slots__ = ("name", "ap", "w", "r", "sem", "semv")

    def __init__(self, name, ap=None):
        self.name = name
        self.ap = ap
        self.w = None
        self.r = {}
        self.sem = None
        self.semv = 0

    def __getitem__(self, k):
        return V(self, self.ap[k])

    @property
    def v(self):
        return V(self, self.ap)


class Prog:
    def __init__(self, nc, es):
        self.nc = nc
        self.es = es
        self.sems = []
        self.streams = {e: [] for e in ENGS}
        self.esem = {}
        self.ecnt = {e: 0 for e in ENGS}
        self.seen = {e: {} for e in ENGS}
        for e in ENGS:
            self.esem[e] = len(self.sems)
            self.sems.append(es.enter_context(nc.semaphore("s_" + e)))
        self.bufs = []
        self.ninstr = 0
        self.dbufs = {}
        self.sem_pool = []

    def buf(self, name, ap=None):
        b = Buf(name, ap)
        self.bufs.append(b)
        return b

    def dbuf(self, *key):
        b = self.dbufs.get(key)
        if b is None:
            b = self.buf(str(key))
            self.dbufs[key] = b
        return b

    def _waits(self, eng, reads, writes, extra=()):
        need = {}
        seen = self.seen[eng]

        def add(s, v):
            if seen.get(s, 0) >= v:
                return
            if need.get(s, 0) < v:
                need[s] = v
        for b in reads:
            if b.w is not None:
                add(*b.w)
        for b in writes:
            if b.w is not None:
                add(*b.w)
            for s, v in b.r.items():
                add(s, v)
        for s, v in extra:
            add(s, v)
        for s, v in need.items():
            seen[s] = v
        return list(need.items())

    def _mark(self, tok, reads, writes):
        s, v = tok
        for b in reads:
            if b.r.get(s, 0) < v:
                b.r[s] = v
        for b in writes:
            b.w = tok
            b.r = {}

    def op(self, eng, fn, reads=(), writes=()):
        waits = self._waits(eng, reads, writes)
        self.ecnt[eng] += 1
        tok = (self.esem[eng], self.ecnt[eng])
        self.streams[eng].append((waits, fn, self.esem[eng]))
        self._mark(tok, reads, writes)
        self.ninstr += 1

    def dma(self, eng, pairs, sembuf, reads=(), writes=()):
        if sembuf.sem is None:
            if self.sem_pool:
                sembuf.sem, sembuf.semv = self.sem_pool.pop()
            else:
                sembuf.sem = len(self.sems)
                sembuf.semv = 0
                self.sems.append(self.es.enter_context(self.nc.semaphore("d%d" % len(self.sems))))
        s = sembuf.sem
        waits = self._waits(eng, reads, writes, extra=((s, sembuf.semv),) if sembuf.semv else ())
        np_ = []
        for o, i in pairs:
            sh = tuple(o.shape)
            if len(sh) == 3 and sh[0] * sh[1] > 512 and tuple(i.shape) == sh:
                step = max(1, 512 // sh[0])
                for g0 in range(0, sh[1], step):
                    np_.append((o[:, g0:min(g0 + step, sh[1]), :], i[:, g0:min(g0 + step, sh[1]), :]))
            else:
                np_.append((o, i))
        pairs = np_
        sembuf.semv += 16 * len(pairs)
        tok = (s, sembuf.semv)
        sem = self.sems[s]

        def fn(e):
            for o, i in pairs:
                e.dma_start(out=o, in_=i).then_inc(sem, 16)
            return None
        self.streams[eng].append((waits, fn, None))
        self._mark(tok, reads, writes)
        self.ninstr += len(pairs)

    def barrier(self):
        toks = [(self.esem[e], self.ecnt[e]) for e in ENGS if self.ecnt[e]]
        toks += [(b.sem, b.semv) for b in self.bufs if b.sem is not None and b.semv]
        for e in ENGS:
            waits = self._waits(e, (), (), extra=toks)
            if waits:
                self.streams[e].append((waits, None, None))
        for b in self.bufs:
            if b.sem is not None:
                self.sem_pool.append((b.sem, b.semv))
                b.sem = None
                b.semv = 0

    def emit(self):
        sems = self.sems
        streams = self.streams

        def replay(name, e):
            for waits, fn, inc in streams[name]:
                for s, v in waits:
                    e.wait_ge(sems[s], v)
                if fn is not None:
                    ins = fn(e)
                    if inc is not None:
                        ins.then_inc(sems[inc], 1)
        with self.nc.Block() as block:
            @block.tensor
            def _(e):
                replay("pe", e)

            @block.scalar
            def _(e):
                replay("act", e)

            @block.vector
            def _(e):
                replay("dve", e)

            @block.gpsimd
            def _(e):
                replay("pool", e)

            @block.sync
            def _(e):
                replay("sp", e)


def _bufs(*vs):
    out = []
    for v in vs:
        if v is None or isinstance(v, (int, float)):
            continue
        if v.b not in out:
            out.append(v.b)
    return out


def _a(v):
    return v.ap if isinstance(v, V) else v


class K:
    def __init__(self, SEG, L):
        self.SEG = SEG
        self.L = L
        self.TS = 2 * SEG
        self.TT = 512
        self.NT = self.TS // self.TT
        self.ROWS = SEG // GW
        self.NCH = self.TS // 64

    def mm(self, groups, ps, extra_reads=()):
        reads = []
        gl = []
        for o, ops in groups:
            ol = []
            for l, r in ops:
                for b in (l.b, r.b):
                    if b not in reads:
                        reads.append(b)
                ol.append((l.ap, r.ap))
            gl.append((o.ap, ol))
        for b in extra_reads:
            if b not in reads:
                reads.append(b)

        def fn(e):
            ins = None
            for o, ol in gl:
                n = len(ol)
                for i, (l, r) in enumerate(ol):
                    ins = e.matmul(o, lhsT=l, rhs=r, start=(i == 0), stop=(i == n - 1))
            return ins
        self.P.op("pe", fn, reads=reads, writes=[ps])

    def transp(self, groups, ps, ident):
        reads = [ident.b]
        gl = []
        for o, i in groups:
            if i.b not in reads:
                reads.append(i.b)
            gl.append((o.ap, i.ap))
        ia = ident.ap

        def fn(e):
            ins = None
            for o, i in gl:
                ins = e.transpose(o, i, ia)
            return ins
        self.P.op("pe", fn, reads=reads, writes=[ps])

    def tt(self, eng, out, in0, in1, op):
        o, a, b = out.ap, in0.ap, in1.ap
        self.P.op(eng, lambda e: e.tensor_tensor(out=o, in0=a, in1=b, op=op),
                  reads=_bufs(in0, in1), writes=[out.b])

    def ts(self, eng, out, in0, s1, s2=None, op0=ALU.mult, op1=None):
        o, a = out.ap, in0.ap
        x1, x2 = _a(s1), _a(s2)
        if op1 is None:
            fn = lambda e: e.tensor_scalar(out=o, in0=a, scalar1=x1, scalar2=None, op0=op0)
        else:
            fn = lambda e: e.tensor_scalar(out=o, in0=a, scalar1=x1, scalar2=x2, op0=op0, op1=op1)
        self.P.op(eng, fn, reads=_bufs(in0, s1 if isinstance(s1, V) else None, s2 if isinstance(s2, V) else None),
                  writes=[out.b])

    def stt(self, eng, out, in0, sc, in1, op0, op1):
        o, a, b = out.ap, in0.ap, in1.ap
        s = _a(sc)
        self.P.op(eng, lambda e: e.scalar_tensor_tensor(out=o, in0=a, scalar=s, in1=b, op0=op0, op1=op1),
                  reads=_bufs(in0, in1, sc if isinstance(sc, V) else None), writes=[out.b])

    def act(self, out, in_, func, scale=1.0, bias=0.0):
        o, a = out.ap, in_.ap
        sc, bi = _a(scale), _a(bias)
        self.P.op("act", lambda e: e.activation(out=o, in_=a, func=func, bias=bi, scale=sc),
                  reads=_bufs(in_, scale if isinstance(scale, V) else None, bias if isinstance(bias, V) else None),
                  writes=[out.b])

    def copy(self, eng, out, in_):
        o, a = out.ap, in_.ap
        if eng == "act":
            self.P.op("act", lambda e: e.activation(out=o, in_=a, func=AF.Copy), reads=[in_.b], writes=[out.b])
        else:
            self.P.op(eng, lambda e: e.tensor_copy(out=o, in_=a), reads=[in_.b], writes=[out.b])

    def recip(self, out, in_):
        o, a = out.ap, in_.ap
        self.P.op("dve", lambda e: e.reciprocal(out=o, in_=a), reads=[in_.b], writes=[out.b])

    def memset(self, eng, out, val):
        o = out.ap
        self.P.op(eng, lambda e: e.memset(o, val), writes=[out.b])

    def scan(self, out, d0, d1):
        o, a, b = out.ap, d0.ap, d1.ap
        self.P.op("dve", lambda e: e.tensor_tensor_scan(out=o, data0=a, data1=b, initial=0.0,
                                                       op0=ALU.mult, op1=ALU.add),
                  reads=_bufs(d0, d1), writes=[out.b])

    def load(self, dst, src_ap, reads=(), eng="sp"):
        self.P.dma(eng, [(dst.ap, src_ap)], dst.b, reads=list(reads), writes=[dst.b])

    def store(self, dst_ap, src, writes=(), eng="pool"):
        self.P.dma(eng, [(dst_ap, src.ap)], src.b, reads=[src.b], writes=list(writes))

    def sb(self, name, shape, dt):
        n = int(np.prod(shape[1:])) * (4 if dt == F32 else 2)
        assert self.off % 4 == 0
        a = self.arena[:, self.off // 4:(self.off + n) // 4]
        if dt != F32:
            a = a.bitcast(dt)
        if len(shape) == 3:
            a = a.rearrange("p (a b) -> p a b", a=shape[1])
        elif len(shape) == 4:
            a = a.rearrange("p (a b c) -> p a b c", a=shape[1], b=shape[2])
        if shape[0] < 128:
            a = a[0:shape[0]]
        self.off += (n + 63) // 64 * 64
        assert self.off <= self.ARENA, (name, self.off)
        return self.P.buf(name, a)

    def ring(self, name, n, shape, dt):
        return Ring([self.sb("%s%d" % (name, i), shape, dt) for i in range(n)])

    ps_lo = 0

    def psn(self):
        lo = self.ps_lo
        b = self.psb[lo + self.psi % (8 - lo)]
        self.psi += 1
        return b

    def build(self):
        SEG, L, TS, TT, NT = self.SEG, self.L, self.TS, self.TT, self.NT
        nc = bass.Bass("TRN2", target_bir_lowering=False)
        self.nc = nc

        def din(name, shape, dt=F32):
            return nc.dram_tensor(name, list(shape), dt, kind="ExternalInput").ap()

        def dscr(name, shape, dt):
            return nc.dram_tensor(name, list(shape), dt, kind="Internal").ap()
        I = {}
        I["x"] = din("x", [TS, D])
        I["mem"] = din("mem", [2, 256, D])
        I["cflag"] = din("cflag", [128, 2])
        I["w_ffn1_up"] = din("w_ffn1_up", [L, D, 2 * DFF])
        I["w_ffn1_down"] = din("w_ffn1_down", [L, DFF, D])
        I["w_ffn2_up"] = din("w_ffn2_up", [L, D, 2 * DFF])
        I["w_ffn2_down"] = din("w_ffn2_down", [L, DFF, D])
        I["w_in"] = din("w_in", [L, D, 3616])
        I["w_out"] = din("w_out", [L, D, D])
        I["ca_wq"] = din("ca_wq", [L, D, D])
        I["ca_wkv"] = din("ca_wkv", [L, D, 2 * D])
        I["ca_wo"] = din("ca_wo", [L, D, D])
        NG = 5 * L + 1
        I["gains"] = din("gains", [128, NG * DT])
        I["hglb"] = din("hglb", [128, L * 8])
        I["hnorm"] = din("hnorm", [128, 2 * L])
        I["walpha"] = din("walpha", [16, L * 8 * 64])
        I["balpha"] = din("balpha", [128, L * 8])
        I["rpbg"] = din("rpbg", [L, 128, 8 * NTAB * 64])
        I["namask"] = din("namask", [128, 2 * NTAB * 64])
        I["ident"] = din("ident", [128, 128])
        I["tri"] = din("tri", [64, 2 * 64])
        I["rmask"] = din("rmask", [128, 512])
        self.I = I
        yout = nc.dram_tensor("y", [TS, D], F32, kind="ExternalOutput").ap()

        S = {}
        S["XT"] = dscr("XT", [D, TS], F32)
        S["ZTB"] = dscr("ZTB", [1024, TS], BF16)
        S["ZTF"] = dscr("ZTF", [1568, TS], F32)
        S["VTOK"] = dscr("VTOK", [TS, 1024], BF16)
        S["MIXT"] = dscr("MIXT", [1024, TS], BF16)
        S["OF"] = dscr("OF", [4, 64, TS], F32)
        for l in range(L):
            for f in (1, 2):
                S["UP%d_%d" % (f, l)] = dscr("UP%d_%d" % (f, l), [NJ, 128, 8 * 256], BF16)
                S["DN%d_%d" % (f, l)] = dscr("DN%d_%d" % (f, l), [8, 128, NJ * 128], BF16)
            S["WINF_%d" % l] = dscr("WINF_%d" % l, [21, 128, 8 * 128], BF16)
            S["WINT_%d" % l] = dscr("WINT_%d" % l, [128, 8 * 1024], BF16)
            S["WO_%d" % l] = dscr("WO_%d" % l, [8, 128, 8 * 128], BF16)
            S["WQ_%d" % l] = dscr("WQ_%d" % l, [8, 128, 8 * 128], BF16)
            S["WCO_%d" % l] = dscr("WCO_%d" % l, [8, 128, 8 * 128], BF16)
            S["WK_%d" % l] = dscr("WK_%d" % l, [8, 128, 8 * 128], BF16)
            S["WV_%d" % l] = dscr("WV_%d" % l, [128, 8 * 1024], BF16)
        self.S = S

        with ExitStack() as es:
            P = Prog(nc, es)
            self.P = P
            self.ARENA = 208000
            self.arena = nc.alloc_sbuf_tensor("arena", [128, self.ARENA // 4], F32).ap()
            self.psb = [P.buf("ps%d" % i, es.enter_context(nc.psum_tensor("ps%d" % i, [128, 512], F32))[:, :])
                        for i in range(8)]
            self.psi = 0
            self.off = 0
            self.ident = self.sb("ident", [128, 128], F32)
            self.identb = self.sb("identb", [128, 128], BF16)
            self.ones = self.sb("ones", [128, 128], BF16)
            self.ones64 = self.sb("ones64", [128, 64], BF16)
            self.ones1 = self.sb("ones1", [128, 128], BF16)
            self.gains = self.sb("gains", [128, NG * DT], F32)
            self.cflag = self.sb("cflag", [128, 2], F32)
            self.hglb = self.sb("hglb", [128, L * 8], F32)
            self.lb = self.sb("lb", [128, L * 8], F32)
            self.omlb = self.sb("omlb", [128, L * 8], F32)
            self.hnorm = self.sb("hnorm", [128, 2 * L], F32)
            self.walpha = self.sb("walpha", [16, L * 8 * 64], F32)
            self.walphab = self.sb("walphab", [16, L * 8 * 64], BF16)
            self.balpha = self.sb("balpha", [128, L * 8], F32)
            self.nbalpha = self.sb("nbalpha", [128, L * 8], F32)
            self.tri = self.sb("tri", [64, 2, 64], F32)
            self.rmask = self.sb("rmask", [128, 512], F32)
            self.memn = [self.sb("memn%d" % s, [128, 8, 256], BF16) for s in range(2)]
            self.epsb = self.sb("epsb", [128, 1], F32)
            self.one_f = self.sb("one_f", [128, 1], F32)
            self.base_off = self.off
            for nm in ("gains", "cflag", "hglb", "hnorm", "balpha", "ident", "rmask"):
                self.load(getattr(self, nm).v, I[nm])
            self.load(self.walpha.v, I["walpha"])
            self.load(self.tri.v, I["tri"].rearrange("p (a b) -> p a b", a=2))
            self.memset("pool", self.ones.v, 1.0 / 1024)
            self.memset("pool", self.ones64.v, 1.0 / 64)
            self.memset("pool", self.ones1.v, 1.0)
            self.memset("pool", self.epsb.v, EPS)
            self.memset("pool", self.one_f.v, 1.0)
            self.copy("dve", self.identb.v, self.ident.v)
            self.copy("dve", self.walphab.v, self.walpha.v)
            self.ts("dve", self.nbalpha.v, self.balpha.v, -1.0)
            import os
            stop = int(os.environ.get("KSTOP", "99"))
            stages = [self.lb_compute, self.phase_mem, self.phase_weights]
            for l in range(L):
                stages += [lambda l=l: self.phase_a(l), lambda l=l: self.phase_na(l), lambda l=l: self.phase_rec(l, 0),
                           lambda l=l: self.phase_rec(l, 1), lambda l=l: self.phase_c(l, yout if l == L - 1 else None)]
            for i, st in enumerate(stages):
                if i > stop:
                    break
                st()
                P.barrier()
            P.emit()
        return nc

    def gain(self, which, l, dt):
        c = (which * self.L + l) * DT + dt
        return self.gains[:, c:c + 1]

    def lb_compute(self):
        L = self.L
        e = self.sb("lb_e", [128, L * 8], F32)
        s = self.sb("lb_s", [128, 8], F32)
        self.act(e.v, self.hglb.v, AF.Exp)
        self.copy("dve", s.v, e[:, 0:8])
        for l in range(1, L):
            self.tt("dve", s.v, s.v, e[:, 8 * l:8 * l + 8], ALU.add)
        self.recip(s.v, s.v)
        self.memset("dve", self.lb[:, 0:8], 0.0)
        for l in range(1, L):
            self.tt("dve", e[:, 8 * l:8 * l + 8], e[:, 8 * l:8 * l + 8], s.v, ALU.mult)
            self.tt("dve", self.lb[:, 8 * l:8 * l + 8], self.lb[:, 8 * l - 8:8 * l], e[:, 8 * l:8 * l + 8], ALU.add)
        self.ts("dve", self.omlb.v, self.lb.v, -1.0, 1.0, ALU.mult, ALU.add)

    def rstd_from(self, sq, rstd, nfree, ones=None):
        ps = self.psn()
        o = ps[:, 0:nfree]
        ones = self.ones.v if ones is None else ones
        self.mm([(o, [(ones, sq[:, k, :]) for k in range(DT)])], ps)
        self.act(rstd, o, AF.Ln, bias=self.epsb[:, 0:1])
        self.act(rstd, rstd, AF.Exp, scale=-0.5)

    def phase_mem(self):
        self.off = self.base_off
        for s in range(2):
            memT = self.sb("memT", [128, 8, 256], F32)
            for mt in range(2):
                mt_in = self.sb("mem_in", [128, D], F32)
                self.load(mt_in.v, self.I["mem"][s, mt * 128:(mt + 1) * 128, :])
                for half in range(2):
                    ps = self.psn()
                    self.transp([(ps[:, k * 128:(k + 1) * 128], mt_in[:, (half * 4 + k) * 128:(half * 4 + k + 1) * 128])
                                 for k in range(4)], ps, self.ident.v)
                    self.copy("dve", memT[:, half * 4:half * 4 + 4, mt * 128:(mt + 1) * 128],
                              ps.v.re("p (a b) -> p a b", a=4))
            sq = self.sb("mem_sq", [128, 8, 256], BF16)
            rstd = self.sb("mem_rstd", [128, 256], F32)
            self.tt("pool", sq.v, memT.v, memT.v, ALU.mult)
            self.rstd_from(sq.v, rstd.v, 256)
            self.tt("dve", self.memn[s].v, memT.v, rstd.v.re("p (o b) -> p o b", o=1).bc([128, 8, 256]), ALU.mult)

    def phase_weights(self):
        I, S, L = self.I, self.S, self.L
        self.off = self.base_off
        stg = self.ring("wst", 3, [128, 2816], F32)
        stb = self.ring("wsb", 3, [128, 2816], BF16)
        self.wcnt = 0
        engs = ("act", "dve", "pool")

        def piece(src_ap, ncols, g, outs):
            a = stg.next()
            b = stb.next()
            self.load(a[:, 0:ncols], src_ap)
            eng = engs[self.wcnt % 3]
            self.wcnt += 1
            if g is None:
                self.copy(eng, b[:, 0:ncols], a[:, 0:ncols])
            elif eng == "act":
                self.act(b[:, 0:ncols], a[:, 0:ncols], AF.Copy, scale=g)
            else:
                self.ts(eng, b[:, 0:ncols], a[:, 0:ncols], g)
            prs = []
            for dst, c0, c1, w in outs:
                src = b.ap[:, c0:c1]
                if w is not None:
                    src = src.rearrange("p (c w) -> p c w", w=w)
                    nchk = (c1 - c0) // w
                    for g0 in range(0, nchk, 4):
                        prs.append((dst[:, g0:min(g0 + 4, nchk), :], src[:, g0:min(g0 + 4, nchk), :]))
                else:
                    prs.append((dst, src))
            self.P.dma("pool", prs, b, reads=[b], writes=[])

        for l in range(L):
            for f, wu, wd, gi in ((1, "w_ffn1_up", "w_ffn1_down", 0), (2, "w_ffn2_up", "w_ffn2_down", 4)):
                UP = S["UP%d_%d" % (f, l)].rearrange("c p (k w) -> p c k w", k=8)
                DN = S["DN%d_%d" % (f, l)].rearrange("c p (k w) -> p c k w", k=NJ)
                for kt in range(8):
                    g = self.gain(gi, l, kt)
                    rows = I[wu][l, kt * 128:(kt + 1) * 128, :]
                    piece(rows[:, 0:DFF], DFF, g, [(UP[:, :, kt, 0:128], 0, DFF, 128)])
                    piece(rows[:, DFF:2 * DFF], DFF, g, [(UP[:, :, kt, 128:256], 0, DFF, 128)])
                for kt in range(NJ):
                    rows = I[wd][l, kt * 128:(kt + 1) * 128, :]
                    piece(rows, D, None, [(DN[:, :, kt, :], 0, D, 128)])
            WINF = S["WINF_%d" % l].rearrange("c p (k w) -> p c k w", k=8)
            WINT = S["WINT_%d" % l].rearrange("p (k w) -> p k w", k=8)
            for kt in range(8):
                g = self.gain(1, l, kt)
                rows = I["w_in"][l, kt * 128:(kt + 1) * 128, :]
                piece(rows[:, 0:1024], 1024, g, [(WINF[:, 0:8, kt, :], 0, 1024, 128)])
                piece(rows[:, 1024:1536], 512, g, [(WINT[:, kt, 0:512], 0, 512, None)])
                b0 = 1536
                piece(rows[:, 1536:3616], 2080, g, [
                    (WINF[:, 8:14, kt, :], 1536 - b0, 2304 - b0, 128),
                    (WINT[:, kt, 512:768], 2304 - b0, 2560 - b0, None),
                    (WINF[:, 14:18, kt, :], 2560 - b0, 3072 - b0, 128),
                    (WINT[:, kt, 768:1024], 3072 - b0, 3328 - b0, None),
                    (WINF[:, 18:20, kt, :], 3328 - b0, 3584 - b0, 128),
                    (WINF[:, 20, kt, 0:32], 3584 - b0, 3616 - b0, None),
                ])
            for nm, src, gi in (("WO", "w_out", None), ("WQ", "ca_wq", 2), ("WCO", "ca_wo", None)):
                W = S["%s_%d" % (nm, l)].rearrange("c p (k w) -> p c k w", k=8)
                for kt in range(8):
                    g = None if gi is None else self.gain(gi, l, kt)
                    piece(I[src][l, kt * 128:(kt + 1) * 128, :], D, g, [(W[:, :, kt, :], 0, D, 128)])
            WK = S["WK_%d" % l].rearrange("c p (k w) -> p c k w", k=8)
            WV = S["WV_%d" % l].rearrange("p (k w) -> p k w", k=8)
            for kt in range(8):
                g = self.gain(3, l, kt)
                rows = I["ca_wkv"][l, kt * 128:(kt + 1) * 128, :]
                piece(rows[:, 0:1024], 1024, g, [(WK[:, :, kt, :], 0, 1024, 128)])
                piece(rows[:, 1024:2048], 1024, g, [(WV[:, kt, :], 0, 1024, None)])

    def wload(self, src_ap, shape):
        slot = self.wslots.next()
        n = shape[1] * shape[2]
        v = slot[:, 0:n].re("p (a b) -> p a b", a=shape[1])
        self.load(v, src_ap)
        return v

    def norm_h(self, x, h, sq, rstd):
        self.tt("pool", sq.v, x.v, x.v, ALU.mult)
        self.rstd_from(sq.v, rstd.v, self.TT)
        self.tt("dve", h.v, x.v, rstd.v.re("p (o b) -> p o b", o=1).bc([128, DT, self.TT]), ALU.mult)

    def proj_fm_resid(self, x, rhs, wname, l, scale):
        W = self.S["%s_%d" % (wname, l)]
        for dt in range(DT):
            w = self.wload(W[dt].rearrange("p (k w) -> p k w", k=8), [128, 8, 128])
            ps = self.psn()
            self.mm([(ps.v, [(w[:, k, :], rhs[:, k, :]) for k in range(8)])], ps)
            self.stt("dve", x[:, dt, :], ps.v, scale, x[:, dt, :], ALU.mult, ALU.add)

    def ffn(self, x, h, hid, f, l):
        UP = self.S["UP%d_%d" % (f, l)]
        DN = self.S["DN%d_%d" % (f, l)]
        for j in range(NJ):
            w = self.wload(UP[j].rearrange("p (k w) -> p k w", k=8), [128, 8, 256])
            pg = self.psn()
            pu = self.psn()
            self.mm([(pg.v, [(w[:, k, 0:128], h[:, k, :]) for k in range(8)])], pg)
            self.mm([(pu.v, [(w[:, k, 128:256], h[:, k, :]) for k in range(8)])], pu)
            t = self.silu_t.next()
            self.act(t.v, pg.v, AF.Silu)
            self.tt("dve", hid[:, j, :], t.v, pu.v, ALU.mult)
        for dt in range(DT):
            w = self.wload(DN[dt].rearrange("p (k w) -> p k w", k=NJ), [128, NJ, 128])
            ps = self.psn()
            self.mm([(ps.v, [(w[:, j, :], hid[:, j, :]) for j in range(NJ)])], ps)
            self.stt("dve", x[:, dt, :], ps.v, 0.5, x[:, dt, :], ALU.mult, ALU.add)

    def alloc_tok(self):
        TT = self.TT
        self.ps_lo = 0
        self.off = self.base_off
        self.xr = self.ring("x", 2, [128, DT, TT], F32)
        self.hr = self.ring("h", 2, [128, DT, TT], BF16)
        self.sq = self.sb("sq", [128, DT, TT], BF16)
        self.rstd = self.ring("rstd", 2, [128, TT], F32)
        self.hid = self.sb("hid", [128, NJ, TT], BF16)
        self.wslots = self.ring("wslot", 4, [128, 4096], BF16)
        self.silu_t = self.ring("silu", 2, [128, TT], F32)

    def phase_a(self, l):
        S, I, TT, NT = self.S, self.I, self.TT, self.NT
        self.alloc_tok()
        self.stg_f = self.ring("stgf", 3, [128, TT], F32)
        self.stg_b = self.ring("stgb", 3, [128, TT], BF16)
        self.tin = self.ring("tin", 2, [128, D], F32)
        XT = S["XT"].rearrange("(k p) t -> p k t", p=128)
        WINF = S["WINF_%d" % l]
        WINT = S["WINT_%d" % l].rearrange("p (k w) -> p k w", k=8)
        for ti in range(NT):
            t0 = ti * TT
            x = self.xr.next()
            if l == 0:
                for sub in range(TT // 128):
                    tin = self.tin.next()
                    self.load(tin.v, I["x"][t0 + sub * 128:t0 + (sub + 1) * 128, :])
                    for half in range(2):
                        ps = self.psn()
                        self.transp([(ps[:, k * 128:(k + 1) * 128],
                                      tin[:, (half * 4 + k) * 128:(half * 4 + k + 1) * 128]) for k in range(4)],
                                    ps, self.ident.v)
                        self.copy("act" if half else "dve", x[:, half * 4:half * 4 + 4, sub * 128:(sub + 1) * 128],
                                  ps.v.re("p (a b) -> p a b", a=4))
            else:
                self.load(x.v, XT[:, :, t0:t0 + TT], reads=[self.P.dbuf("XT", ti)])
            h = self.hr.next()
            self.norm_h(x, h, self.sq, self.rstd.next())
            self.ffn(x, h, self.hid, 1, l)
            h2 = self.hr.next()
            self.norm_h(x, h2, self.sq, self.rstd.next())
            self.store(XT[:, :, t0:t0 + TT], x.v, writes=[self.P.dbuf("XT", ti)])
            for ci in range(21):
                ncol = 128 if ci < 20 else 32
                w = self.wload(WINF[ci].rearrange("p (k w) -> p k w", k=8), [128, 8, 128])
                ps = self.psn()
                self.mm([(ps[0:ncol, :], [(w[:, k, 0:ncol], h2[:, k, :]) for k in range(8)])], ps)
                if ci < 8:
                    st = self.stg_b.next()
                    self.copy("act" if ci % 2 else "dve", st.v, ps.v)
                    self.store(S["ZTB"][ci * 128:(ci + 1) * 128, t0:t0 + TT], st.v, writes=[self.P.dbuf("ZTB", ti)])
                else:
                    st = self.stg_f.next()
                    self.copy("act" if ci % 2 else "dve", st[0:ncol, :], ps[0:ncol, :])
                    r0 = (ci - 8) * 128
                    self.store(S["ZTF"][r0:r0 + ncol, t0:t0 + TT], st[0:ncol, :], writes=[self.P.dbuf("ZTF", ti)])
            for half in range(2):
                w = self.wload(WINT[:, :, half * 512:(half + 1) * 512], [128, 8, 512])
                for sub in range(TT // 128):
                    ps = self.psn()
                    self.mm([(ps.v, [(h2[:, k, sub * 128:(sub + 1) * 128], w[:, k, :]) for k in range(8)])], ps)
                    st = self.stg_b.next()
                    self.copy("act" if sub % 2 else "dve", st.v, ps.v)
                    self.store(S["VTOK"][t0 + sub * 128:t0 + (sub + 1) * 128, half * 512:(half + 1) * 512], st.v,
                               writes=[self.P.dbuf("VTOK", ti)])

    def phase_c(self, l, yout):
        S, TT, NT = self.S, self.TT, self.NT
        self.alloc_tok()
        mixr = self.ring("mix", 1, [128, DT, TT], BF16)
        qT = self.sb("qT", [128, DT, TT], BF16)
        caT = self.sb("caT", [128, DT, TT], BF16)
        pT = self.ring("pT", 2, [128, 2, TT], BF16)
        rden = self.ring("rden", 2, [128, TT], F32)
        kmem = [self.sb("kmem%d" % s, [128, DT, 256], BF16) for s in range(2)]
        vmem = [self.sb("vmem%d" % s, [128, 2, D], BF16) for s in range(2)]
        yt = self.ring("yt", 2, [128, D], F32)
        XT = S["XT"].rearrange("(k p) t -> p k t", p=128)
        MIXT = S["MIXT"].rearrange("(k p) t -> p k t", p=128)
        WK = S["WK_%d" % l]
        WV = S["WV_%d" % l].rearrange("p (k w) -> p k w", k=8)
        for s in range(2):
            for dt in range(DT):
                w = self.wload(WK[dt].rearrange("p (k w) -> p k w", k=8), [128, 8, 128])
                ps = self.psn()
                self.mm([(ps[:, 0:256], [(w[:, k, :], self.memn[s][:, k, :]) for k in range(8)])], ps)
                self.copy("act" if dt % 2 else "dve", kmem[s][:, dt, :], ps[:, 0:256])
            for half in range(2):
                w = self.wload(WV[:, :, half * 512:(half + 1) * 512], [128, 8, 512])
                for mt in range(2):
                    ps = self.psn()
                    self.mm([(ps.v, [(self.memn[s][:, k, mt * 128:(mt + 1) * 128], w[:, k, :]) for k in range(8)])], ps)
                    self.copy("act" if mt else "dve", vmem[s][:, mt, half * 512:(half + 1) * 512], ps.v)
        for ti in range(NT):
            t0 = ti * TT
            s = t0 // self.SEG
            x = self.xr.next()
            self.load(x.v, XT[:, :, t0:t0 + TT], reads=[self.P.dbuf("XT", ti)])
            mix = mixr.next()
            self.load(mix.v, MIXT[:, :, t0:t0 + TT], reads=[self.P.dbuf("MIXT", ti)])
            self.proj_fm_resid(x, mix, "WO", l, 1.0)
            h = self.hr.next()
            self.norm_h(x, h, self.sq, self.rstd.next())
            WQ = S["WQ_%d" % l]
            for dt in range(DT):
                w = self.wload(WQ[dt].rearrange("p (k w) -> p k w", k=8), [128, 8, 128])
                ps = self.psn()
                self.mm([(ps.v, [(w[:, k, :], h[:, k, :]) for k in range(8)])], ps)
                self.copy("act" if dt % 2 else "dve", qT[:, dt, :], ps.v)
            for hd in range(4):
                p = pT.next()
                for mt in range(2):
                    ps = self.psn()
                    self.mm([(ps.v, [(kmem[s][:, 2 * hd + kk, mt * 128:(mt + 1) * 128], qT[:, 2 * hd + kk, :])
                                     for kk in range(2)])], ps)
                    self.act(p[:, mt, :], ps.v, AF.Exp, scale=1.0 / 16.0)
                psd = self.psn()
                self.mm([(psd.v, [(self.ones1.v, p[:, mt, :]) for mt in range(2)])], psd)
                rd = rden.next()
                self.recip(rd.v, psd.v)
                for dd in range(2):
                    ps = self.psn()
                    c0 = (2 * hd + dd) * 128
                    self.mm([(ps.v, [(vmem[s][:, mt, c0:c0 + 128], p[:, mt, :]) for mt in range(2)])], ps)
                    self.tt("dve", caT[:, 2 * hd + dd, :], ps.v, rd.v, ALU.mult)
            self.proj_fm_resid(x, caT, "WCO", l, 1.0)
            h2 = self.hr.next()
            self.norm_h(x, h2, self.sq, self.rstd.next())
            self.ffn(x, h2, self.hid, 2, l)
            if yout is None:
                self.store(XT[:, :, t0:t0 + TT], x.v, writes=[self.P.dbuf("XT", ti)])
            else:
                rs = self.rstd.next()
                self.tt("pool", self.sq.v, x.v, x.v, ALU.mult)
                self.rstd_from(self.sq.v, rs.v, TT)
                for dt in range(DT):
                    self.stt("dve", x[:, dt, :], x[:, dt, :], self.gain(0, 0, 5 * self.L * DT + dt - 0) if False else
                             self.gains[:, 5 * self.L * DT + dt:5 * self.L * DT + dt + 1], rs.v, ALU.mult, ALU.mult)
                for sub in range(TT // 128):
                    y = yt.next()
                    for half in range(2):
                        ps = self.psn()
                        self.transp([(ps[:, k * 128:(k + 1) * 128], x[:, half * 4 + k, sub * 128:(sub + 1) * 128])
                                     for k in range(4)], ps, self.ident.v)
                        self.copy("act" if half else "dve", y[:, half * 512:(half + 1) * 512], ps.v)
                    self.store(yout[t0 + sub * 128:t0 + (sub + 1) * 128, :], y.v)

    def phase_na(self, l):
        S, I, SEG, TS, ROWS = self.S, self.I, self.SEG, self.TS, self.ROWS
        self.off = self.base_off
        self.ps_lo = 0
        NKT = TS // 128
        tabs = self.sb("tabs", [128, 2, 8, NTAB * 64], BF16)
        msk = self.sb("msk", [128, 2, NTAB * 64], F32)
        gst = self.ring("gst", 2, [128, NTAB * 64], F32)
        qk = self.ring("qk", 2, [64, 2, TS], BF16)
        vg = self.sb("vg", [128, NKT, 256], BF16)
        pf = self.ring("pf", 6, [128, 512], F32)
        pt = self.ring("pt", 14, [128, 256], BF16)
        rd = self.ring("nrd", 2, [64, 256], F32)
        ob = self.ring("ob", 4, [64, 256], BF16)
        oe = self.ring("oe", 4, [64, 256], F32)
        self.load(msk.v, I["namask"].rearrange("p (a b) -> p a b", a=2))
        for h in range(8):
            g = gst.next()
            self.load(g.v, I["rpbg"][l, :, h * NTAB * 64:(h + 1) * NTAB * 64])
            self.act(g.v, g.v, AF.Exp)
            self.tt("dve", tabs[:, 0, h, :], g.v, msk[:, 0, :], ALU.mult)
            self.tt("pool", tabs[:, 1, h, :], g.v, msk[:, 1, :], ALU.mult)
        VT = S["VTOK"].rearrange("(k p) c -> p k c", p=128)
        zdeps = [self.P.dbuf("ZTB", ti) for ti in range(self.NT)]
        vdeps = [self.P.dbuf("VTOK", ti) for ti in range(self.NT)]

        state = {"i": 0, "pend": None}

        def scores(h, q, Q0, a_list, which):
            banks = [self.psb[(state["i"] % 2) * 3 + j] for j in range(3)]
            qs = q[:, 0, Q0 * 64:Q0 * 64 + 256]
            nk = len(a_list)
            for u, a in enumerate(a_list):
                bk = banks[u // 2]
                self.mm([(bk[:, (u % 2) * 256:(u % 2 + 1) * 256], [(q[:, 1, a * 64:a * 64 + 128], qs)])], bk)
            pts = []
            for bi in range((nk + 1) // 2):
                bk = banks[bi]
                nt_ = min(2, nk - 2 * bi)
                p_f = pf.next()
                self.act(p_f[:, 0:256 * nt_], bk[:, 0:256 * nt_], AF.Exp, scale=0.125)
                for j in range(nt_):
                    a = a_list[2 * bi + j]
                    n0 = 6 - (a - Q0)
                    p_t = pt.next()
                    self.tt("dve", p_t.v, p_f[:, j * 256:(j + 1) * 256], tabs[:, which, h, n0 * 64:(n0 + 4) * 64], ALU.mult)
                    pts.append((p_t, a // 2))
            acc = self.psb[6 + state["i"] % 2]
            state["i"] += 1
            return pts, acc

        def pv(pts, acc, hl, out_v):
            nk = len(pts)
            kts = [kt for _, kt in pts]
            pas = [p.ap for p, _ in pts]

            def fn(e, o1=acc.ap[0:64, 0:256], o2=acc.ap[0:64, 256:512], on=self.ones1.ap[:, 0:64]):
                ins = None
                for u in range(nk):
                    e.matmul(o1, lhsT=vg.ap[:, kts[u], hl * 64:(hl + 1) * 64], rhs=pas[u], start=(u == 0), stop=(u == nk - 1))
                for u in range(nk):
                    ins = e.matmul(o2, lhsT=on, rhs=pas[u], start=(u == 0), stop=(u == nk - 1))
                return ins
            self.P.op("pe", fn, reads=[vg, self.ones1] + [p for p, _ in pts], writes=[acc])
            r = rd.next()
            self.recip(r.v, acc[0:64, 256:512])
            self.tt("dve", out_v, acc[0:64, 0:256], r.v, ALU.mult)

        def submit(h, hl, q, Q0, al, which, out_v, post):
            cur = (scores(h, q, Q0, al, which), hl, out_v, post)
            flush()
            state["pend"] = cur

        def flush():
            p = state["pend"]
            if p is not None:
                (pts, acc), hl_, out_v, post = p
                pv(pts, acc, hl_, out_v)
                if post is not None:
                    post()
                state["pend"] = None

        for hg in range(2):
            self.P.dma("sp", [(vg.ap[:, k0:k0 + 16, :], VT[:, k0:k0 + 16, hg * 256:(hg + 1) * 256])
                              for k0 in range(0, NKT, 16)], vg, reads=vdeps, writes=[vg])
            for hl in range(4):
                h = hg * 4 + hl
                q = qk.next()
                self.P.dma("sp", [(q.ap[:, 0, :], S["ZTB"][h * 64:(h + 1) * 64, :]),
                                  (q.ap[:, 1, :], S["ZTB"][512 + h * 64:512 + (h + 1) * 64, :])],
                           q, reads=zdeps, writes=[q])
                for s in range(2):
                    R0 = s * ROWS
                    for jb in range(ROWS // 4):
                        Q0 = R0 + 4 * jb
                        if jb == 0:
                            al, which = [R0 + 6, R0 + 4, R0 + 2, R0], 1
                        elif jb == ROWS // 4 - 1:
                            al, which = [R0 + ROWS - 2, R0 + ROWS - 4, R0 + ROWS - 6, R0 + ROWS - 8], 1
                        else:
                            al, which = [Q0 - 4 + 2 * t for t in range(5, -1, -1)], 0
                        o = ob.next()
                        mid = (s == 0 and jb == ROWS // 4 - 1) or (s == 1 and jb == 0)

                        def st(o=o, h=h, Q0=Q0):
                            self.store(S["MIXT"][h * 64:(h + 1) * 64, Q0 * 64:Q0 * 64 + 256], o.v,
                                       writes=[self.P.dbuf("MIXT", (Q0 * 64) // self.TT)])
                        if not mid:
                            submit(h, hl, q, Q0, al, which, o.v, st)
                        else:
                            e1 = oe.next()
                            e2 = oe.next()

                            def blend(o=o, e1=e1, e2=e2, st=st):
                                self.ts("pool", e1.v, e1.v, self.cflag[0:64, 1:2])
                                self.stt("dve", o.v, e2.v, self.cflag[0:64, 0:1], e1.v, ALU.mult, ALU.add)
                                st()
                            submit(h, hl, q, Q0, al, which, e1.v, None)
                            submit(h, hl, q, Q0, [Q0 - 4 + 2 * t for t in range(5, -1, -1)], 0, e2.v, blend)
            flush()

    def phase_rec(self, l, mixer):
        S, I, SEG, TS, TT, NT, L = self.S, self.I, self.SEG, self.TS, self.TT, self.NT, self.L
        P = self.P
        self.off = self.base_off
        CPT = TT // 64
        f3 = [64, 4, TT]
        qin = self.ring("r_q", 2, f3, F32)
        zin = self.ring("r_z", 2, f3, F32)
        gaf = self.ring("r_ga", 2, [16, TT], F32)
        gafb = self.ring("r_gab", 2, [16, TT], BF16)
        t1 = self.sb("r_t1", f3, F32)
        t2 = self.sb("r_t2", f3, F32)
        gl = self.sb("r_g", f3, F32)
        be = self.sb("r_be", f3, F32)
        eq = self.sb("r_eq", f3, F32)
        ekring = self.ring("r_ekr", 2, f3, F32)
        qt = self.ring("r_qt", 2, f3, BF16)
        kt_ = self.ring("r_kt", 2, f3, BF16)
        fac = self.ring("r_fac", 2, [64, 3, 4, CPT], F32)
        vtok = self.ring("r_v", 2, [64, CPT, 256], BF16)
        ktok = self.ring("r_ktok", 4, [64, 4, 64], BF16)
        atm = self.ring("r_atm", 4, [64, 4, 64], BF16)
        T = self.sb("r_T", [64, 4, 64], F32)
        Sp = self.ring("r_Sp", 4, [64, 4, 64], BF16)
        Wc = self.ring("r_W", 2, [64, 4, 64], F32)
        osum = self.ring("r_os", 2, [64, 4, TT], F32)
        ofl = self.ring("r_of", 1, [64, 4, TT], F32)
        gate = self.ring("r_gate", 1, [64, 4, TT], F32)
        sqo = self.sb("r_sqo", [64, 4, TT], BF16)
        rs = self.sb("r_rs", [64, 4, TT], F32)
        og = self.ring("r_og", 1, [64, 4, TT], BF16)
        if mixer == 1:
            for b in qin.items + zin.items:
                self.memset("pool", b.v, 0.0)
        ZTF = S["ZTF"]
        VT = S["VTOK"]
        zdeps = lambda ti: [P.dbuf("ZTF", ti)]
        vcol = 512 if mixer == 0 else 768
        rq = R_HQ if mixer == 0 else R_GQ
        rgate = R_HGATE if mixer == 0 else R_GGATE
        moff = 4 * 64 * mixer + 512
        idn64 = self.ident[0:64, 0:64]
        self.ps_lo = 6

        def headload(dst, row0, ti):
            t0 = ti * TT
            if mixer == 0:
                P.dma("sp", [(dst.ap, ZTF[row0:row0 + 256, t0:t0 + TT].rearrange("(h k) t -> k h t", k=64))],
                      dst, reads=zdeps(ti), writes=[dst])
            else:
                P.dma("sp", [(dst.ap[0:32], ZTF[row0:row0 + 128, t0:t0 + TT].rearrange("(h k) t -> k h t", k=32))],
                      dst, reads=zdeps(ti), writes=[dst])

        for d in range(2):
            order = list(range(NT)) if d == 0 else list(range(NT - 1, -1, -1))
            self.memset("dve", T.v, 0.0)
            def prep(idx, d=d, order=order):
                ti = order[idx]
                t0 = ti * TT
                seg_start = (idx > 0) and ((t0 // SEG) != ((order[idx - 1] * TT) // SEG))
                q = qin.next()
                headload(q, rq, ti)
                z = zin.next()
                if mixer == 0:
                    headload(z, R_HZF if d == 0 else R_HZB, ti)
                else:
                    headload(z, R_GK, ti)
                    ga = gaf.next()
                    self.load(ga.v, ZTF[R_GA + 16 * d:R_GA + 16 * d + 16, t0:t0 + TT], reads=zdeps(ti))
                vt = vtok.next()
                P.dma("sp", [(vt.ap, VT[t0:t0 + TT, vcol:vcol + 256].rearrange("(c s) f -> s c f", s=64))],
                      vt, reads=[P.dbuf("VTOK", ti)], writes=[vt])
                if mixer == 0:
                    self.act(t1.v, z.v, AF.Exp, scale=-1.0)
                    self.ts("pool", t1.v, t1.v, 1.0, None, ALU.add)
                    self.recip(t1.v, t1.v)
                    for hh in range(4):
                        c = (l * 2 + d) * 4 + hh
                        self.ts("dve", t2[:, hh, :], t1[:, hh, :], self.omlb[0:64, c:c + 1], self.lb[0:64, c:c + 1],
                                ALU.mult, ALU.add)
                    self.act(gl.v, t2.v, AF.Ln)
                    kk = t1
                    self.ts("pool", kk.v, t2.v, -1.0, 1.0, ALU.mult, ALU.add)
                    qq = q
                else:
                    gb = gafb.next()
                    self.copy("pool", gb.v, ga.v)
                    for hh in range(4):
                        c = (l * 2 + d) * 4 + hh
                        ps = self.psn()
                        self.mm([(ps[0:64, :], [(self.walphab[:, c * 64:(c + 1) * 64], gb.v)])], ps)
                        self.act(t1[:, hh, :], ps[0:64, :], AF.Exp, scale=-1.0, bias=self.nbalpha[0:64, c:c + 1])
                    self.act(t2.v, t1.v, AF.Ln, bias=self.one_f[0:64, 0:1])
                    self.ts("pool", gl.v, t2.v, -1.0 / 16.0)
                    kk = z
                    qq = t1
                    self.ts("pool", qq.v, q.v, 32.0 ** -0.5)
                g4 = gl.v.re("p a (c s) -> p a c s", s=64)
                b4 = be.v.re("p a (c s) -> p a c s", s=64)
                for hh in range(4):
                    self.scan(be[:, hh, :], self.rmask[0:64, :], gl[:, hh, :])
                if d == 0:
                    mpos, lpos = 31, 63
                    bt = be
                else:
                    self.tt("pool", gl.v, gl.v, be.v, ALU.subtract)
                    bt = t2
                    bt4 = bt.v.re("p a (c s) -> p a c s", s=64)
                    for hh in range(4):
                        self.tt("dve", bt4[:, hh], g4[:, hh], b4[:, hh, :, 63:64].bc([64, CPT, 64]), ALU.add)
                    mpos, lpos = 32, 0
                bt4 = bt.v.re("p a (c s) -> p a c s", s=64)
                e4 = eq.v.re("p a (c s) -> p a c s", s=64)
                fc = fac.next()
                f4 = fc.v
                for hh in range(4):
                    self.act(f4[:, 0, hh, :].re("p (c o) -> p c o", o=1), bt4[:, hh, :, mpos:mpos + 1], AF.Exp)
                    self.act(f4[:, 1, hh, :].re("p (c o) -> p c o", o=1), bt4[:, hh, :, lpos:lpos + 1], AF.Exp)
                    self.tt("dve", e4[:, hh], bt4[:, hh], bt4[:, hh, :, mpos:mpos + 1].bc([64, CPT, 64]), ALU.subtract)
                self.act(gl.v, eq.v, AF.Exp, scale=-1.0)
                self.act(eq.v, eq.v, AF.Exp)
                for hh in range(4):
                    self.copy("pool", f4[:, 2, hh, :].re("p (c o) -> p c o", o=1), e4[:, hh, :, lpos:lpos + 1])
                qt_ = qt.next()
                ktl = kt_.next()
                self.tt("dve", qt_.v, qq.v, eq.v, ALU.mult)
                ekr = ekring.next()
                self.tt("pool", ekr.v, kk.v, gl.v, ALU.mult)
                self.copy("pool", ktl.v, ekr.v)
                return dict(ti=ti, t0=t0, seg_start=seg_start, vt=vt, f4=f4, qt_=qt_, ktl=ktl, ek=ekr)

            def run(ctx, d=d):
                ti, t0, seg_start, vt, f4, qt_, ktl, ek = (ctx[k] for k in ("ti", "t0", "seg_start", "vt", "f4", "qt_", "ktl", "ek"))
                if d == 1:
                    of_ = ofl.next()
                    P.dma("sp", [(of_.ap, S["OF"][:, :, t0:t0 + TT].rearrange("h v t -> v h t"))], of_,
                          reads=[P.dbuf("OF", ti)], writes=[of_])
                    gt = gate.next()
                    P.dma("sp", [(gt.ap, ZTF[rgate:rgate + 256, t0:t0 + TT].rearrange("(h v) t -> v h t", v=64))],
                          gt, reads=zdeps(ti), writes=[gt])
                osm = osum.next()
                chunks = list(range(CPT)) if d == 0 else list(range(CPT - 1, -1, -1))
                st_ = {}

                def stage1(ci, c):
                    cs = slice(c * 64, (c + 1) * 64)
                    bk = self.psb[ci % 2]
                    self.mm([(bk[0:64, hh * 64:(hh + 1) * 64], [(ktl[:, hh, cs], qt_[:, hh, cs])]) for hh in range(4)], bk)
                    am = atm.next()
                    self.tt("dve", am.v, bk[0:64, 0:256].re("p (h t) -> p h t", h=4),
                            self.tri[:, d, :].re("p (o t) -> p o t", o=1).bc([64, 4, 64]), ALU.mult)
                    self.transp([(bk[0:64, 256 + hh * 64:256 + (hh + 1) * 64], ek[:, hh, cs]) for hh in range(4)], bk, idn64)
                    ktk = ktok.next()
                    self.copy("act", ktk.v, bk[0:64, 256:512].re("p (h k) -> p h k", h=4))
                    st_[ci] = {"am": am, "ktk": ktk}

                def stage2(ci, c):
                    ktk = st_[ci]["ktk"]
                    pu = self.psb[2 + ci % 2]
                    self.mm([(pu[0:64, hh * 64:(hh + 1) * 64], [(ktk[:, hh, :], vt[:, c, hh * 64:(hh + 1) * 64])])
                             for hh in range(4)], pu)
                    if seg_start and ci == 0:
                        self.ts("dve", T.v, T.v, self.cflag[0:64, 0:1])
                    sp = Sp.next()
                    self.tt("pool", sp.v, T.v, f4[:, 0, :, c:c + 1].bc([64, 4, 64]), ALU.mult)
                    w = Wc.next()
                    self.tt("dve", w.v, pu[0:64, 0:256].re("p (a v) -> p a v", a=4), f4[:, 2, :, c:c + 1].bc([64, 4, 64]),
                            ALU.mult)
                    self.tt("dve", T.v, T.v, f4[:, 1, :, c:c + 1].bc([64, 4, 64]), ALU.mult)
                    self.tt("dve", T.v, T.v, w.v, ALU.add)
                    st_[ci]["sp"] = sp

                def stage3(ci, c):
                    cs = slice(c * 64, (c + 1) * 64)
                    am, sp = st_[ci]["am"], st_[ci]["sp"]
                    po = self.psb[4 + ci % 2]
                    self.mm([(po[0:64, hh * 64:(hh + 1) * 64],
                              [(vt[:, c, hh * 64:(hh + 1) * 64], am[:, hh, :]), (sp[:, hh, :], qt_[:, hh, cs])])
                             for hh in range(4)], po)
                    if d == 0:
                        self.copy("act", osm[:, :, cs], po[0:64, 0:256].re("p (h t) -> p h t", h=4))
                    else:
                        self.tt("dve", osm[:, :, cs], po[0:64, 0:256].re("p (h t) -> p h t", h=4), of_[:, :, cs], ALU.add)

                nch = len(chunks)
                for it in range(nch + 2):
                    if it < nch:
                        stage1(it, chunks[it])
                    if 0 <= it - 1 < nch:
                        stage2(it - 1, chunks[it - 1])
                    if 0 <= it - 2 < nch:
                        stage3(it - 2, chunks[it - 2])
                if d == 0:
                    P.dma("pool", [(S["OF"][:, :, t0:t0 + TT].rearrange("h v t -> v h t"), osm.ap)], osm,
                          reads=[osm], writes=[P.dbuf("OF", ti)])
                else:
                    self.tt("pool", sqo.v, osm.v, osm.v, ALU.mult)
                    for hh in range(4):
                        ps = self.psn()
                        self.mm([(ps[0:64, :], [(self.ones64[0:64, :], sqo[:, hh, :])])], ps)
                        self.act(rs[:, hh, :], ps[0:64, :], AF.Ln, bias=self.epsb[0:64, 0:1])
                    self.act(rs.v, rs.v, AF.Exp, scale=-0.5)
                    self.stt("dve", osm.v, osm.v, self.hnorm[0:64, mixer * L + l:mixer * L + l + 1], rs.v, ALU.mult, ALU.mult)
                    self.act(rs.v, gt.v, AF.Exp, scale=-1.0)
                    self.ts("pool", rs.v, rs.v, 1.0, None, ALU.add)
                    self.recip(rs.v, rs.v)
                    self.tt("pool", gt.v, gt.v, rs.v, ALU.mult)
                    o_ = og.next()
                    self.tt("dve", o_.v, osm.v, gt.v, ALU.mult)
                    P.dma("pool", [(S["MIXT"][moff:moff + 256, t0:t0 + TT].rearrange("(h v) t -> v h t", v=64), o_.ap)],
                          o_, reads=[o_], writes=[P.dbuf("MIXT", ti)])

            nxt = prep(0)
            for idx in range(len(order)):
                cur = nxt
                if idx + 1 < len(order):
                    nxt = prep(idx + 1)
                run(cur)


class Ring:
    def __init__(self, items):
        self.items = items
        self.i = 0

    def next(self):
        b = self.items[self.i % len(self.items)]
        self.i += 1
        return b


def _host_consts(L, na_rpb, hg_lb, hg_norm, gla_norm, gla_w_alpha, gla_b_alpha, gains_list, g_final):
    out = {}
    NG = 5 * L + 1
    gains = np.zeros((128, NG * DT), np.float32)
    for w, g in enumerate(gains_list):
        for l in range(L):
            gains[:, (w * L + l) * DT:(w * L + l + 1) * DT] = g[l].reshape(DT, 128).T
    gains[:, 5 * L * DT:(5 * L + 1) * DT] = g_final.reshape(DT, 128).T
    out["gains"] = gains
    hglb = np.zeros((128, L * 8), np.float32)
    for l in range(L):
        for d in range(2):
            for hd in range(4):
                hglb[0:64, (l * 2 + d) * 4 + hd] = hg_lb[l, d, hd * 64:(hd + 1) * 64]
    out["hglb"] = hglb
    hn = np.zeros((128, 2 * L), np.float32)
    hn[0:64, 0:L] = hg_norm[:L].T
    hn[0:64, L:2 * L] = gla_norm[:L].T
    out["hnorm"] = hn
    wa = np.zeros((16, L * 8 * 64), np.float32)
    ba = np.zeros((128, L * 8), np.float32)
    for l in range(L):
        for d in range(2):
            for hd in range(4):
                c = (l * 2 + d) * 4 + hd
                wa[:, c * 64:c * 64 + 32] = gla_w_alpha[l, d, :, hd * 32:(hd + 1) * 32]
                ba[0:32, c] = gla_b_alpha[l, d, hd * 32:(hd + 1) * 32]
    out["walpha"] = wa
    out["balpha"] = ba
    p = np.arange(128)
    kc = p % 64
    half = p // 64
    n = np.arange(NTAB)
    qc = np.arange(64)
    dr = 6 - n[None, :] + half[:, None]
    dc = kc[:, None] - qc[None, :] + 15
    c0 = np.clip(qc - 8, 0, 48)
    colok = (kc[:, None] >= c0[None, :]) & (kc[:, None] < c0[None, :] + 16)
    dcc = np.clip(dc, 0, 30)
    drc = np.clip(dr + 7, 0, 14)
    g = na_rpb[:L][:, :, drc[:, :, None], dcc[:, None, :]]
    out["rpbg"] = np.ascontiguousarray(g.transpose(0, 2, 1, 3, 4)).reshape(L, 128, 8 * NTAB * 64).astype(np.float32)
    rowok_int = (dr >= -4) & (dr <= 3)
    rowok_all = (dr >= -7) & (dr <= 7)
    m = np.zeros((128, 2, NTAB, 64), np.float32)
    m[:, 0] = (rowok_int[:, :, None] & colok[:, None, :])
    m[:, 1] = (rowok_all[:, :, None] & colok[:, None, :])
    out["namask"] = m.reshape(128, 2 * NTAB * 64)
    out["ident"] = np.eye(128, dtype=np.float32)
    s = np.arange(64)
    tri = np.zeros((64, 2, 64), np.float32)
    tri[:, 0, :] = (s[None, :] >= s[:, None])
    tri[:, 1, :] = (s[None, :] <= s[:, None])
    out["tri"] = tri.reshape(64, 128)
    rm = np.ones((128, 512), np.float32)
    rm[:, ::64] = 0.0
    out["rmask"] = rm
    return out


_CACHE = {}


def run_slots(SEG, L, slots, weights, n_cores):
    key = (SEG, L)
    if key not in _CACHE:
        _CACHE[key] = K(SEG, L).build()
    nc = _CACHE[key]
    hc = _host_consts(L, weights["na_rpb"], weights["hg_lb"], weights["hg_norm"], weights["gla_norm"],
                      weights["gla_w_alpha"], weights["gla_b_alpha"],
                      [weights["g_ffn1"], weights["g_mix"], weights["g_cross"], weights["g_mem"], weights["g_ffn2"]],
                      weights["g_final"])
    in_maps = []
    for x, mem, c in slots:
        m = dict(hc)
        m["x"] = np.ascontiguousarray(x, dtype=np.float32)
        m["mem"] = np.ascontiguousarray(mem, dtype=np.float32)
        cf = np.zeros((128, 2), np.float32)
        cf[:, 0] = c
        cf[:, 1] = 1 - c
        m["cflag"] = cf
        for nm in ("w_ffn1_up", "w_ffn1_down", "w_ffn2_up", "w_ffn2_down", "w_in", "w_out", "ca_wq", "ca_wkv", "ca_wo"):
            m[nm] = np.ascontiguousarray(weights[nm][:L], dtype=np.float32)
        in_maps.append(m)
    res = run_bass_kernel_spmd(nc, in_maps, core_ids=list(range(n_cores)))
    return [r["y"] for r in res.results]


def kernel(**inputs):
    inputs = {k: np.asarray(v) for k, v in inputs.items()}
    xp, xs = inputs["x_prompt"], inputs["x_sample"]
    mp, ms = inputs["mem_prompt"], inputs["mem_sample"]
    SEG = xp.shape[1]
    L = inputs["w_in"].shape[0]
    slots = []
    for i in range(4):
        slots.append((xs[i], np.stack([ms[i], ms[i]]), 1.0))
    for k in range(2):
        slots.append((np.concatenate([xp[2 * k], xp[2 * k + 1]], axis=0), np.stack([mp[2 * k], mp[2 * k + 1]]), 0.0))
    slots.append(slots[4])
    slots.append(slots[5])
    ys = run_slots(SEG, L, slots, inputs, 8)
    y_sample = np.stack(ys[0:4]).astype(np.float32)
    y_prompt = np.stack([ys[4][:SEG], ys[4][SEG:], ys[5][:SEG], ys[5][SEG:]]).astype(np.float32)
    return (y_prompt, y_sample)
```

```python
import numpy as np
from contextlib import ExitStack
import concourse.bass as bass
import concourse.mybir as mybir
from concourse.bass_utils import run_bass_kernel_spmd

F32 = mybir.dt.float32
BF16 = mybir.dt.bfloat16
ALU = mybir.AluOpType
AF = mybir.ActivationFunctionType

ENGS = ("pe", "act", "dve", "pool", "sp")
D = 1024
DT = 8
DFF = 2816
NJ = 22
GW = 64
EPS = 1e-6
NTAB = 14
FM_BF = [0, 128, 256, 384, 512, 640, 768, 896]
FM_F = [1536, 1664, 1792, 1920, 2048, 2176, 2560, 2688, 2816, 2944, 3328, 3456, 3584]
R_HQ, R_HZF, R_HZB, R_HGATE, R_GQ, R_GK, R_GGATE, R_GA = 0, 256, 512, 768, 1024, 1152, 1280, 1536


class V:
    __slots__ = ("b", "ap")

    def __init__(self, b, ap):
        self.b = b
        self.ap = ap

    def __getitem__(self, k):
        return V(self.b, self.ap[k])

    def re(self, pat, **kw):
        return V(self.b, self.ap.rearrange(pat, **kw))

    def bc(self, shape):
        return V(self.b, self.ap.broadcast_to(list(shape)))

    def cast(self, dt):
        return V(self.b, self.ap.bitcast(dt))


class Buf:
    __slots__ = ("name", "ap", "w", "r", "sem", "semv")

    def __init__(self, name, ap=None):
        self.name = name
        self.ap = ap
        self.w = None
        self.r = {}
        self.sem = None
        self.semv = 0

    def __getitem__(self, k):
        return V(self, self.ap[k])

    @property
    def v(self):
        return V(self, self.ap)


class Prog:
    def __init__(self, nc, es):
        self.nc = nc
        self.es = es
        self.sems = []
        self.streams = {e: [] for e in ENGS}
        self.esem = {}
        self.ecnt = {e: 0 for e in ENGS}
        self.seen = {e: {} for e in ENGS}
        for e in ENGS:
            self.esem[e] = len(self.sems)
            self.sems.append(es.enter_context(nc.semaphore("s_" + e)))
        self.bufs = []
        self.ninstr = 0
        self.dbufs = {}
        self.sem_pool = []

    def buf(self, name, ap=None):
        b = Buf(name, ap)
        self.bufs.append(b)
        return b

    def dbuf(self, *key):
        b = self.dbufs.get(key)
        if b is None:
            b = self.buf(str(key))
            self.dbufs[key] = b
        return b

    def _waits(self, eng, reads, writes, extra=()):
        need = {}
        seen = self.seen[eng]

        def add(s, v):
            if seen.get(s, 0) >= v:
                return
            if need.get(s, 0) < v:
                need[s] = v
        for b in reads:
            if b.w is not None:
                add(*b.w)
        for b in writes:
            if b.w is not None:
                add(*b.w)
            for s, v in b.r.items():
                add(s, v)
        for s, v in extra:
            add(s, v)
        for s, v in need.items():
            seen[s] = v
        return list(need.items())

    def _mark(self, tok, reads, writes):
        s, v = tok
        for b in reads:
            if b.r.get(s, 0) < v:
                b.r[s] = v
        for b in writes:
            b.w = tok
            b.r = {}

    def op(self, eng, fn, reads=(), writes=()):
        waits = self._waits(eng, reads, writes)
        self.ecnt[eng] += 1
        tok = (self.esem[eng], self.ecnt[eng])
        self.streams[eng].append((waits, fn, self.esem[eng]))
        self._mark(tok, reads, writes)
        self.ninstr += 1

    def dma(self, eng, pairs, sembuf, reads=(), writes=()):
        if sembuf.sem is None:
            if self.sem_pool:
                sembuf.sem, sembuf.semv = self.sem_pool.pop()
            else:
                sembuf.sem = len(self.sems)
                sembuf.semv = 0
                self.sems.append(self.es.enter_context(self.nc.semaphore("d%d" % len(self.sems))))
        s = sembuf.sem
        waits = self._waits(eng, reads, writes, extra=((s, sembuf.semv),) if sembuf.semv else ())
        np_ = []
        for o, i in pairs:
            sh = tuple(o.shape)
            if len(sh) == 3 and sh[0] * sh[1] > 512 and tuple(i.shape) == sh:
                step = max(1, 512 // sh[0])
                for g0 in range(0, sh[1], step):
                    np_.append((o[:, g0:min(g0 + step, sh[1]), :], i[:, g0:min(g0 + step, sh[1]), :]))
            else:
                np_.append((o, i))
        pairs = np_
        sembuf.semv += 16 * len(pairs)
        tok = (s, sembuf.semv)
        sem = self.sems[s]

        def fn(e):
            for o, i in pairs:
                e.dma_start(out=o, in_=i).then_inc(sem, 16)
            return None
        self.streams[eng].append((waits, fn, None))
        self._mark(tok, reads, writes)
        self.ninstr += len(pairs)

    def barrier(self):
        toks = [(self.esem[e], self.ecnt[e]) for e in ENGS if self.ecnt[e]]
        toks += [(b.sem, b.semv) for b in self.bufs if b.sem is not None and b.semv]
        for e in ENGS:
            waits = self._waits(e, (), (), extra=toks)
            if waits:
                self.streams[e].append((waits, None, None))
        for b in self.bufs:
            if b.sem is not None:
                self.sem_pool.append((b.sem, b.semv))
                b.sem = None
                b.semv = 0

    def emit(self):
        sems = self.sems
        streams = self.streams

        def replay(name, e):
            for waits, fn, inc in streams[name]:
                for s, v in waits:
                    e.wait_ge(sems[s], v)
                if fn is not None:
                    ins = fn(e)
                    if inc is not None:
                        ins.then_inc(sems[inc], 1)
        with self.nc.Block() as block:
            @block.tensor
            def _(e):
                replay("pe", e)

            @block.scalar
            def _(e):
                replay("act", e)

            @block.vector
            def _(e):
                replay("dve", e)

            @block.gpsimd
            def _(e):
                replay("pool", e)

            @block.sync
            def _(e):
                replay("sp", e)


def _bufs(*vs):
    out = []
    for v in vs:
        if v is None or isinstance(v, (int, float)):
            continue
        if v.b not in out:
            out.append(v.b)
    return out


def _a(v):
    return v.ap if isinstance(v, V) else v


class K:
    def __init__(self, SEG, L):
        self.SEG = SEG
        self.L = L
        self.TS = 2 * SEG
        self.TT = 512
        self.NT = self.TS // self.TT
        self.ROWS = SEG // GW
        self.NCH = self.TS // 64

    def mm(self, groups, ps, extra_reads=()):
        reads = []
        gl = []
        for o, ops in groups:
            ol = []
            for l, r in ops:
                for b in (l.b, r.b):
                    if b not in reads:
                        reads.append(b)
                ol.append((l.ap, r.ap))
            gl.append((o.ap, ol))
        for b in extra_reads:
            if b not in reads:
                reads.append(b)

        def fn(e):
            ins = None
            for o, ol in gl:
                n = len(ol)
                for i, (l, r) in enumerate(ol):
                    ins = e.matmul(o, lhsT=l, rhs=r, start=(i == 0), stop=(i == n - 1))
            return ins
        self.P.op("pe", fn, reads=reads, writes=[ps])

    def transp(self, groups, ps, ident):
        reads = [ident.b]
        gl = []
        for o, i in groups:
            if i.b not in reads:
                reads.append(i.b)
            gl.append((o.ap, i.ap))
        ia = ident.ap

        def fn(e):
            ins = None
            for o, i in gl:
                ins = e.transpose(o, i, ia)
            return ins
        self.P.op("pe", fn, reads=reads, writes=[ps])

    def tt(self, eng, out, in0, in1, op):
        o, a, b = out.ap, in0.ap, in1.ap
        self.P.op(eng, lambda e: e.tensor_tensor(out=o, in0=a, in1=b, op=op),
                  reads=_bufs(in0, in1), writes=[out.b])

    def ts(self, eng, out, in0, s1, s2=None, op0=ALU.mult, op1=None):
        o, a = out.ap, in0.ap
        x1, x2 = _a(s1), _a(s2)
        if op1 is None:
            fn = lambda e: e.tensor_scalar(out=o, in0=a, scalar1=x1, scalar2=None, op0=op0)
        else:
            fn = lambda e: e.tensor_scalar(out=o, in0=a, scalar1=x1, scalar2=x2, op0=op0, op1=op1)
        self.P.op(eng, fn, reads=_bufs(in0, s1 if isinstance(s1, V) else None, s2 if isinstance(s2, V) else None),
                  writes=[out.b])

    def stt(self, eng, out, in0, sc, in1, op0, op1):
        o, a, b = out.ap, in0.ap, in1.ap
        s = _a(sc)
        self.P.op(eng, lambda e: e.scalar_tensor_tensor(out=o, in0=a, scalar=s, in1=b, op0=op0, op1=op1),
                  reads=_bufs(in0, in1, sc if isinstance(sc, V) else None), writes=[out.b])

    def act(self, out, in_, func, scale=1.0, bias=0.0):
        o, a = out.ap, in_.ap
        sc, bi = _a(scale), _a(bias)
        self.P.op("act", lambda e: e.activation(out=o, in_=a, func=func, bias=bi, scale=sc),
                  reads=_bufs(in_, scale if isinstance(scale, V) else None, bias if isinstance(bias, V) else None),
                  writes=[out.b])

    def copy(self, eng, out, in_):
        o, a = out.ap, in_.ap
        if eng == "act":
            self.P.op("act", lambda e: e.activation(out=o, in_=a, func=AF.Copy), reads=[in_.b], writes=[out.b])
        else:
            self.P.op(eng, lambda e: e.tensor_copy(out=o, in_=a), reads=[in_.b], writes=[out.b])

    def recip(self, out, in_):
        o, a = out.ap, in_.ap
        self.P.op("dve", lambda e: e.reciprocal(out=o, in_=a), reads=[in_.b], writes=[out.b])

    def memset(self, eng, out, val):
        o = out.ap
        self.P.op(eng, lambda e: e.memset(o, val), writes=[out.b])

    def scan(self, out, d0, d1):
        o, a, b = out.ap, d0.ap, d1.ap
        self.P.op("dve", lambda e: e.tensor_tensor_scan(out=o, data0=a, data1=b, initial=0.0,
                                                       op0=ALU.mult, op1=ALU.add),
                  reads=_bufs(d0, d1), writes=[out.b])

    def load(self, dst, src_ap, reads=(), eng="sp"):
        self.P.dma(eng, [(dst.ap, src_ap)], dst.b, reads=list(reads), writes=[dst.b])

    def store(self, dst_ap, src, writes=(), eng="pool"):
        self.P.dma(eng, [(dst_ap, src.ap)], src.b, reads=[src.b], writes=list(writes))

    def sb(self, name, shape, dt):
        n = int(np.prod(shape[1:])) * (4 if dt == F32 else 2)
        assert self.off % 4 == 0
        a = self.arena[:, self.off // 4:(self.off + n) // 4]
        if dt != F32:
            a = a.bitcast(dt)
        if len(shape) == 3:
            a = a.rearrange("p (a b) -> p a b", a=shape[1])
        elif len(shape) == 4:
            a = a.rearrange("p (a b c) -> p a b c", a=shape[1], b=shape[2])
        if shape[0] < 128:
            a = a[0:shape[0]]
        self.off += (n + 63) // 64 * 64
        assert self.off <= self.ARENA, (name, self.off)
        return self.P.buf(name, a)

    def ring(self, name, n, shape, dt):
        return Ring([self.sb("%s%d" % (name, i), shape, dt) for i in range(n)])

    ps_lo = 0

    def psn(self):
        lo = self.ps_lo
        b = self.psb[lo + self.psi % (8 - lo)]
        self.psi += 1
        return b

    def build(self):
        SEG, L, TS, TT, NT = self.SEG, self.L, self.TS, self.TT, self.NT
        nc = bass.Bass("TRN2", target_bir_lowering=False)
        self.nc = nc

        def din(name, shape, dt=F32):
            return nc.dram_tensor(name, list(shape), dt, kind="ExternalInput").ap()

        def dscr(name, shape, dt):
            return nc.dram_tensor(name, list(shape), dt, kind="Internal").ap()
        I = {}
        I["x"] = din("x", [TS, D])
        I["mem"] = din("mem", [2, 256, D])
        I["cflag"] = din("cflag", [128, 2])
        I["w_ffn1_up"] = din("w_ffn1_up", [L, D, 2 * DFF])
        I["w_ffn1_down"] = din("w_ffn1_down", [L, DFF, D])
        I["w_ffn2_up"] = din("w_ffn2_up", [L, D, 2 * DFF])
        I["w_ffn2_down"] = din("w_ffn2_down", [L, DFF, D])
        I["w_in"] = din("w_in", [L, D, 3616])
        I["w_out"] = din("w_out", [L, D, D])
        I["ca_wq"] = din("ca_wq", [L, D, D])
        I["ca_wkv"] = din("ca_wkv", [L, D, 2 * D])
        I["ca_wo"] = din("ca_wo", [L, D, D])
        NG = 5 * L + 1
        I["gains"] = din("gains", [128, NG * DT])
        I["hglb"] = din("hglb", [128, L * 8])
        I["hnorm"] = din("hnorm", [128, 2 * L])
        I["walpha"] = din("walpha", [16, L * 8 * 64])
        I["balpha"] = din("balpha", [128, L * 8])
        I["rpbg"] = din("rpbg", [L, 128, 8 * NTAB * 64])
        I["namask"] = din("namask", [128, 2 * NTAB * 64])
        I["ident"] = din("ident", [128, 128])
        I["tri"] = din("tri", [64, 2 * 64])
        I["rmask"] = din("rmask", [128, 512])
        self.I = I
        yout = nc.dram_tensor("y", [TS, D], F32, kind="ExternalOutput").ap()

        S = {}
        S["XT"] = dscr("XT", [D, TS], F32)
        S["ZTB"] = dscr("ZTB", [1024, TS], BF16)
        S["ZTF"] = dscr("ZTF", [1568, TS], F32)
        S["VTOK"] = dscr("VTOK", [TS, 1024], BF16)
        S["MIXT"] = dscr("MIXT", [1024, TS], BF16)
        S["OF"] = dscr("OF", [4, 64, TS], F32)
        for l in range(L):
            for f in (1, 2):
                S["UP%d_%d" % (f, l)] = dscr("UP%d_%d" % (f, l), [NJ, 128, 8 * 256], BF16)
                S["DN%d_%d" % (f, l)] = dscr("DN%d_%d" % (f, l), [8, 128, NJ * 128], BF16)
            S["WINF_%d" % l] = dscr("WINF_%d" % l, [21, 128, 8 * 128], BF16)
            S["WINT_%d" % l] = dscr("WINT_%d" % l, [128, 8 * 1024], BF16)
            S["WO_%d" % l] = dscr("WO_%d" % l, [8, 128, 8 * 128], BF16)
            S["WQ_%d" % l] = dscr("WQ_%d" % l, [8, 128, 8 * 128], BF16)
            S["WCO_%d" % l] = dscr("WCO_%d" % l, [8, 128, 8 * 128], BF16)
            S["WK_%d" % l] = dscr("WK_%d" % l, [8, 128, 8 * 128], BF16)
            S["WV_%d" % l] = dscr("WV_%d" % l, [128, 8 * 1024], BF16)
        self.S = S

        with ExitStack() as es:
            P = Prog(nc, es)
            self.P = P
            self.ARENA = 206000
            self.arena = nc.alloc_sbuf_tensor("arena", [128, self.ARENA // 4], F32).ap()
            self.psb = [P.buf("ps%d" % i, es.enter_context(nc.psum_tensor("ps%d" % i, [128, 512], F32))[:, :])
                        for i in range(8)]
            self.psi = 0
            self.off = 0
            self.ident = self.sb("ident", [128, 128], F32)
            self.identb = self.sb("identb", [128, 128], BF16)
            self.ones = self.sb("ones", [128, 128], BF16)
            self.ones64 = self.sb("ones64", [128, 64], BF16)
            self.ones1 = self.sb("ones1", [128, 128], BF16)
            self.gains = self.sb("gains", [128, NG * DT], F32)
            self.cflag = self.sb("cflag", [128, 2], F32)
            self.hglb = self.sb("hglb", [128, L * 8], F32)
            self.lb = self.sb("lb", [128, L * 8], F32)
            self.omlb = self.sb("omlb", [128, L * 8], F32)
            self.hnorm = self.sb("hnorm", [128, 2 * L], F32)
            self.walpha = self.sb("walpha", [16, L * 8 * 64], F32)
            self.walphab = self.sb("walphab", [16, L * 8 * 64], BF16)
            self.balpha = self.sb("balpha", [128, L * 8], F32)
            self.nbalpha = self.sb("nbalpha", [128, L * 8], F32)
            self.tri = self.sb("tri", [64, 2, 64], F32)
            self.rmask = self.sb("rmask", [128, 512], F32)
            self.memn = [self.sb("memn%d" % s, [128, 8, 256], BF16) for s in range(2)]
            self.epsb = self.sb("epsb", [128, 1], F32)
            self.one_f = self.sb("one_f", [128, 1], F32)
            self.base_off = self.off
            for nm in ("gains", "cflag", "hglb", "hnorm", "balpha", "ident", "rmask"):
                self.load(getattr(self, nm).v, I[nm])
            self.load(self.walpha.v, I["walpha"])
            self.load(self.tri.v, I["tri"].rearrange("p (a b) -> p a b", a=2))
            self.memset("pool", self.ones.v, 1.0 / 1024)
            self.memset("pool", self.ones64.v, 1.0 / 64)
            self.memset("pool", self.ones1.v, 1.0)
            self.memset("pool", self.epsb.v, EPS)
            self.memset("pool", self.one_f.v, 1.0)
            self.copy("dve", self.identb.v, self.ident.v)
            self.copy("dve", self.walphab.v, self.walpha.v)
            self.ts("dve", self.nbalpha.v, self.balpha.v, -1.0)
            import os
            stop = int(os.environ.get("KSTOP", "99"))
            stages = [self.lb_compute, self.phase_mem, self.phase_weights]
            for l in range(L):
                stages += [lambda l=l: self.phase_a(l), lambda l=l: self.phase_na(l), lambda l=l: self.phase_rec(l, 0),
                           lambda l=l: self.phase_rec(l, 1), lambda l=l: self.phase_c(l, yout if l == L - 1 else None)]
            for i, st in enumerate(stages):
                if i > stop:
                    break
                st()
                P.barrier()
            P.emit()
        return nc

    def gain(self, which, l, dt):
        c = (which * self.L + l) * DT + dt
        return self.gains[:, c:c + 1]

    def lb_compute(self):
        L = self.L
        e = self.sb("lb_e", [128, L * 8], F32)
        s = self.sb("lb_s", [128, 8], F32)
        self.act(e.v, self.hglb.v, AF.Exp)
        self.copy("dve", s.v, e[:, 0:8])
        for l in range(1, L):
            self.tt("dve", s.v, s.v, e[:, 8 * l:8 * l + 8], ALU.add)
        self.recip(s.v, s.v)
        self.memset("dve", self.lb[:, 0:8], 0.0)
        for l in range(1, L):
            self.tt("dve", e[:, 8 * l:8 * l + 8], e[:, 8 * l:8 * l + 8], s.v, ALU.mult)
            self.tt("dve", self.lb[:, 8 * l:8 * l + 8], self.lb[:, 8 * l - 8:8 * l], e[:, 8 * l:8 * l + 8], ALU.add)
        self.ts("dve", self.omlb.v, self.lb.v, -1.0, 1.0, ALU.mult, ALU.add)

    def rstd_from(self, sq, rstd, nfree, ones=None):
        ps = self.psn()
        o = ps[:, 0:nfree]
        ones = self.ones.v if ones is None else ones
        self.mm([(o, [(ones, sq[:, k, :]) for k in range(DT)])], ps)
        self.act(rstd, o, AF.Ln, bias=self.epsb[:, 0:1])
        self.act(rstd, rstd, AF.Exp, scale=-0.5)

    def phase_mem(self):
        self.off = self.base_off
        for s in range(2):
            memT = self.sb("memT", [128, 8, 256], F32)
            for mt in range(2):
                mt_in = self.sb("mem_in", [128, D], F32)
                self.load(mt_in.v, self.I["mem"][s, mt * 128:(mt + 1) * 128, :])
                for half in range(2):
                    ps = self.psn()
                    self.transp([(ps[:, k * 128:(k + 1) * 128], mt_in[:, (half * 4 + k) * 128:(half * 4 + k + 1) * 128])
                                 for k in range(4)], ps, self.ident.v)
                    self.copy("dve", memT[:, half * 4:half * 4 + 4, mt * 128:(mt + 1) * 128],
                              ps.v.re("p (a b) -> p a b", a=4))
            sq = self.sb("mem_sq", [128, 8, 256], BF16)
            rstd = self.sb("mem_rstd", [128, 256], F32)
            self.tt("pool", sq.v, memT.v, memT.v, ALU.mult)
            self.rstd_from(sq.v, rstd.v, 256)
            self.tt("dve", self.memn[s].v, memT.v, rstd.v.re("p (o b) -> p o b", o=1).bc([128, 8, 256]), ALU.mult)

    def phase_weights(self):
        I, S, L = self.I, self.S, self.L
        self.off = self.base_off
        stg = self.ring("wst", 3, [128, 2816], F32)
        stb = self.ring("wsb", 3, [128, 2816], BF16)
        self.wcnt = 0
        engs = ("act", "dve", "act", "dve", "pool")
        pend = []

        def piece(src_ap, ncols, g, outs):
            a = stg.next()
            b = stb.next()
            self.load(a[:, 0:ncols], src_ap)
            eng = engs[self.wcnt % 5]
            self.wcnt += 1
            if g is None:
                self.copy(eng, b[:, 0:ncols], a[:, 0:ncols])
            elif eng == "act":
                self.act(b[:, 0:ncols], a[:, 0:ncols], AF.Copy, scale=g)
            else:
                self.ts(eng, b[:, 0:ncols], a[:, 0:ncols], g)
            prs = []
            for dst, c0, c1, w in outs:
                src = b.ap[:, c0:c1]
                if w is not None:
                    src = src.rearrange("p (c w) -> p c w", w=w)
                    nchk = (c1 - c0) // w
                    for g0 in range(0, nchk, 4):
                        prs.append((dst[:, g0:min(g0 + 4, nchk), :], src[:, g0:min(g0 + 4, nchk), :]))
                else:
                    prs.append((dst, src))
            pend.append((prs, b))
            if len(pend) > 2:
                p0, b0 = pend.pop(0)
                self.P.dma("sp", p0, b0, reads=[b0], writes=[])

        for l in range(L):
            for f, wu, wd, gi in ((1, "w_ffn1_up", "w_ffn1_down", 0), (2, "w_ffn2_up", "w_ffn2_down", 4)):
                UP = S["UP%d_%d" % (f, l)].rearrange("c p (k w) -> p c k w", k=8)
                DN = S["DN%d_%d" % (f, l)].rearrange("c p (k w) -> p c k w", k=NJ)
                for kt in range(8):
                    g = self.gain(gi, l, kt)
                    rows = I[wu][l, kt * 128:(kt + 1) * 128, :]
                    piece(rows[:, 0:DFF], DFF, g, [(UP[:, :, kt, 0:128], 0, DFF, 128)])
                    piece(rows[:, DFF:2 * DFF], DFF, g, [(UP[:, :, kt, 128:256], 0, DFF, 128)])
                for kt in range(NJ):
                    rows = I[wd][l, kt * 128:(kt + 1) * 128, :]
                    piece(rows, D, None, [(DN[:, :, kt, :], 0, D, 128)])
            WINF = S["WINF_%d" % l].rearrange("c p (k w) -> p c k w", k=8)
            WINT = S["WINT_%d" % l].rearrange("p (k w) -> p k w", k=8)
            for kt in range(8):
                g = self.gain(1, l, kt)
                rows = I["w_in"][l, kt * 128:(kt + 1) * 128, :]
                piece(rows[:, 0:1024], 1024, g, [(WINF[:, 0:8, kt, :], 0, 1024, 128)])
                piece(rows[:, 1024:1536], 512, g, [(WINT[:, kt, 0:512], 0, 512, None)])
                b0 = 1536
                piece(rows[:, 1536:3616], 2080, g, [
                    (WINF[:, 8:14, kt, :], 1536 - b0, 2304 - b0, 128),
                    (WINT[:, kt, 512:768], 2304 - b0, 2560 - b0, None),
                    (WINF[:, 14:18, kt, :], 2560 - b0, 3072 - b0, 128),
                    (WINT[:, kt, 768:1024], 3072 - b0, 3328 - b0, None),
                    (WINF[:, 18:20, kt, :], 3328 - b0, 3584 - b0, 128),
                    (WINF[:, 20, kt, 0:32], 3584 - b0, 3616 - b0, None),
                ])
            for nm, src, gi in (("WO", "w_out", None), ("WQ", "ca_wq", 2), ("WCO", "ca_wo", None)):
                W = S["%s_%d" % (nm, l)].rearrange("c p (k w) -> p c k w", k=8)
                for kt in range(8):
                    g = None if gi is None else self.gain(gi, l, kt)
                    piece(I[src][l, kt * 128:(kt + 1) * 128, :], D, g, [(W[:, :, kt, :], 0, D, 128)])
            WK = S["WK_%d" % l].rearrange("c p (k w) -> p c k w", k=8)
            WV = S["WV_%d" % l].rearrange("p (k w) -> p k w", k=8)
            for kt in range(8):
                g = self.gain(3, l, kt)
                rows = I["ca_wkv"][l, kt * 128:(kt + 1) * 128, :]
                piece(rows[:, 0:1024], 1024, g, [(WK[:, :, kt, :], 0, 1024, 128)])
                piece(rows[:, 1024:2048], 1024, g, [(WV[:, kt, :], 0, 1024, None)])
        for p0, b0 in pend:
            self.P.dma("sp", p0, b0, reads=[b0], writes=[])

    def wload(self, src_ap, shape):
        slot = self.wslots.next()
        n = shape[1] * shape[2]
        v = slot[:, 0:n].re("p (a b) -> p a b", a=shape[1])
        self.load(v, src_ap)
        return v

    def norm_h(self, x, h, sq, rstd):
        self.tt("pool", sq.v, x.v, x.v, ALU.mult)
        self.rstd_from(sq.v, rstd.v, self.TT)
        self.tt("dve", h.v, x.v, rstd.v.re("p (o b) -> p o b", o=1).bc([128, DT, self.TT]), ALU.mult)

    def proj_fm_resid(self, x, rhs, wname, l, scale):
        W = self.S["%s_%d" % (wname, l)]
        for dt in range(DT):
            w = self.wload(W[dt].rearrange("p (k w) -> p k w", k=8), [128, 8, 128])
            ps = self.psn()
            self.mm([(ps.v, [(w[:, k, :], rhs[:, k, :]) for k in range(8)])], ps)
            self.stt("dve", x[:, dt, :], ps.v, scale, x[:, dt, :], ALU.mult, ALU.add)

    def ffn(self, x, h, hid, f, l):
        UP = self.S["UP%d_%d" % (f, l)]
        DN = self.S["DN%d_%d" % (f, l)]
        for j in range(NJ):
            w = self.wload(UP[j].rearrange("p (k w) -> p k w", k=8), [128, 8, 256])
            pg = self.psn()
            pu = self.psn()
            self.mm([(pg.v, [(w[:, k, 0:128], h[:, k, :]) for k in range(8)])], pg)
            self.mm([(pu.v, [(w[:, k, 128:256], h[:, k, :]) for k in range(8)])], pu)
            t = self.silu_t.next()
            self.act(t.v, pg.v, AF.Silu)
            self.tt("dve", hid[:, j, :], t.v, pu.v, ALU.mult)
        for dt in range(DT):
            w = self.wload(DN[dt].rearrange("p (k w) -> p k w", k=NJ), [128, NJ, 128])
            ps = self.psn()
            self.mm([(ps.v, [(w[:, j, :], hid[:, j, :]) for j in range(NJ)])], ps)
            self.stt("dve", x[:, dt, :], ps.v, 0.5, x[:, dt, :], ALU.mult, ALU.add)

    def alloc_tok(self):
        TT = self.TT
        self.ps_lo = 0
        self.off = self.base_off
        self.xr = self.ring("x", 2, [128, DT, TT], F32)
        self.hr = self.ring("h", 2, [128, DT, TT], BF16)
        self.sq = self.sb("sq", [128, DT, TT], BF16)
        self.rstd = self.ring("rstd", 2, [128, TT], F32)
        self.hid = self.sb("hid", [128, NJ, TT], BF16)
        self.wslots = self.ring("wslot", 4, [128, 4096], BF16)
        self.silu_t = self.ring("silu", 2, [128, TT], F32)

    def phase_a(self, l):
        S, I, TT, NT = self.S, self.I, self.TT, self.NT
        self.alloc_tok()
        self.stg_f = self.ring("stgf", 3, [128, TT], F32)
        self.stg_b = self.ring("stgb", 3, [128, TT], BF16)
        self.tin = self.ring("tin", 2, [128, D], F32)
        XT = S["XT"].rearrange("(k p) t -> p k t", p=128)
        WINF = S["WINF_%d" % l]
        WINT = S["WINT_%d" % l].rearrange("p (k w) -> p k w", k=8)
        for ti in range(NT):
            t0 = ti * TT
            x = self.xr.next()
            if l == 0:
                for sub in range(TT // 128):
                    tin = self.tin.next()
                    self.load(tin.v, I["x"][t0 + sub * 128:t0 + (sub + 1) * 128, :])
                    for half in range(2):
                        ps = self.psn()
                        self.transp([(ps[:, k * 128:(k + 1) * 128],
                                      tin[:, (half * 4 + k) * 128:(half * 4 + k + 1) * 128]) for k in range(4)],
                                    ps, self.ident.v)
                        self.copy("act" if half else "dve", x[:, half * 4:half * 4 + 4, sub * 128:(sub + 1) * 128],
                                  ps.v.re("p (a b) -> p a b", a=4))
            else:
                self.load(x.v, XT[:, :, t0:t0 + TT], reads=[self.P.dbuf("XT", ti)])
            h = self.hr.next()
            self.norm_h(x, h, self.sq, self.rstd.next())
            self.ffn(x, h, self.hid, 1, l)
            h2 = self.hr.next()
            self.norm_h(x, h2, self.sq, self.rstd.next())
            self.store(XT[:, :, t0:t0 + TT], x.v, writes=[self.P.dbuf("XT", ti)])
            for ci in range(21):
                ncol = 128 if ci < 20 else 32
                w = self.wload(WINF[ci].rearrange("p (k w) -> p k w", k=8), [128, 8, 128])
                ps = self.psn()
                self.mm([(ps[0:ncol, :], [(w[:, k, 0:ncol], h2[:, k, :]) for k in range(8)])], ps)
                if ci < 8:
                    st = self.stg_b.next()
                    self.copy("act" if ci % 2 else "dve", st.v, ps.v)
                    self.store(S["ZTB"][ci * 128:(ci + 1) * 128, t0:t0 + TT], st.v, writes=[self.P.dbuf("ZTB", ti)])
                else:
                    st = self.stg_f.next()
                    self.copy("act" if ci % 2 else "dve", st[0:ncol, :], ps[0:ncol, :])
                    r0 = (ci - 8) * 128
                    self.store(S["ZTF"][r0:r0 + ncol, t0:t0 + TT], st[0:ncol, :], writes=[self.P.dbuf("ZTF", ti)])
            for half in range(2):
                w = self.wload(WINT[:, :, half * 512:(half + 1) * 512], [128, 8, 512])
                for sub in range(TT // 128):
                    ps = self.psn()
                    self.mm([(ps.v, [(h2[:, k, sub * 128:(sub + 1) * 128], w[:, k, :]) for k in range(8)])], ps)
                    st = self.stg_b.next()
                    self.copy("act" if sub % 2 else "dve", st.v, ps.v)
                    self.store(S["VTOK"][t0 + sub * 128:t0 + (sub + 1) * 128, half * 512:(half + 1) * 512], st.v,
                               writes=[self.P.dbuf("VTOK", ti)])

    def phase_c(self, l, yout):
        S, TT, NT = self.S, self.TT, self.NT
        self.alloc_tok()
        mixr = self.ring("mix", 1, [128, DT, TT], BF16)
        qT = self.sb("qT", [128, DT, TT], BF16)
        caT = self.sb("caT", [128, DT, TT], BF16)
        pT = self.ring("pT", 2, [128, 2, TT], BF16)
        rden = self.ring("rden", 2, [128, TT], F32)
        kmem = [self.sb("kmem%d" % s, [128, DT, 256], BF16) for s in range(2)]
        vmem = [self.sb("vmem%d" % s, [128, 2, D], BF16) for s in range(2)]
        yt = self.ring("yt", 2, [128, D], F32)
        XT = S["XT"].rearrange("(k p) t -> p k t", p=128)
        MIXT = S["MIXT"].rearrange("(k p) t -> p k t", p=128)
        WK = S["WK_%d" % l]
        WV = S["WV_%d" % l].rearrange("p (k w) -> p k w", k=8)
        for s in range(2):
            for dt in range(DT):
                w = self.wload(WK[dt].rearrange("p (k w) -> p k w", k=8), [128, 8, 128])
                ps = self.psn()
                self.mm([(ps[:, 0:256], [(w[:, k, :], self.memn[s][:, k, :]) for k in range(8)])], ps)
                self.copy("act" if dt % 2 else "dve", kmem[s][:, dt, :], ps[:, 0:256])
            for half in range(2):
                w = self.wload(WV[:, :, half * 512:(half + 1) * 512], [128, 8, 512])
                for mt in range(2):
                    ps = self.psn()
                    self.mm([(ps.v, [(self.memn[s][:, k, mt * 128:(mt + 1) * 128], w[:, k, :]) for k in range(8)])], ps)
                    self.copy("act" if mt else "dve", vmem[s][:, mt, half * 512:(half + 1) * 512], ps.v)
        for ti in range(NT):
            t0 = ti * TT
            s = t0 // self.SEG
            x = self.xr.next()
            self.load(x.v, XT[:, :, t0:t0 + TT], reads=[self.P.dbuf("XT", ti)])
            mix = mixr.next()
            self.load(mix.v, MIXT[:, :, t0:t0 + TT], reads=[self.P.dbuf("MIXT", ti)])
            self.proj_fm_resid(x, mix, "WO", l, 1.0)
            h = self.hr.next()
            self.norm_h(x, h, self.sq, self.rstd.next())
            WQ = S["WQ_%d" % l]
            for dt in range(DT):
                w = self.wload(WQ[dt].rearrange("p (k w) -> p k w", k=8), [128, 8, 128])
                ps = self.psn()
                self.mm([(ps.v, [(w[:, k, :], h[:, k, :]) for k in range(8)])], ps)
                self.copy("act" if dt % 2 else "dve", qT[:, dt, :], ps.v)
            for hd in range(4):
                p = pT.next()
                for mt in range(2):
                    ps = self.psn()
                    self.mm([(ps.v, [(kmem[s][:, 2 * hd + kk, mt * 128:(mt + 1) * 128], qT[:, 2 * hd + kk, :])
                                     for kk in range(2)])], ps)
                    self.act(p[:, mt, :], ps.v, AF.Exp, scale=1.0 / 16.0)
                psd = self.psn()
                self.mm([(psd.v, [(self.ones1.v, p[:, mt, :]) for mt in range(2)])], psd)
                rd = rden.next()
                self.recip(rd.v, psd.v)
                for dd in range(2):
                    ps = self.psn()
                    c0 = (2 * hd + dd) * 128
                    self.mm([(ps.v, [(vmem[s][:, mt, c0:c0 + 128], p[:, mt, :]) for mt in range(2)])], ps)
                    self.tt("dve", caT[:, 2 * hd + dd, :], ps.v, rd.v, ALU.mult)
            self.proj_fm_resid(x, caT, "WCO", l, 1.0)
            h2 = self.hr.next()
            self.norm_h(x, h2, self.sq, self.rstd.next())
            self.ffn(x, h2, self.hid, 2, l)
            if yout is None:
                self.store(XT[:, :, t0:t0 + TT], x.v, writes=[self.P.dbuf("XT", ti)])
            else:
                rs = self.rstd.next()
                self.tt("pool", self.sq.v, x.v, x.v, ALU.mult)
                self.rstd_from(self.sq.v, rs.v, TT)
                for dt in range(DT):
                    self.stt("dve", x[:, dt, :], x[:, dt, :], self.gain(0, 0, 5 * self.L * DT + dt - 0) if False else
                             self.gains[:, 5 * self.L * DT + dt:5 * self.L * DT + dt + 1], rs.v, ALU.mult, ALU.mult)
                for sub in range(TT // 128):
                    y = yt.next()
                    for half in range(2):
                        ps = self.psn()
                        self.transp([(ps[:, k * 128:(k + 1) * 128], x[:, half * 4 + k, sub * 128:(sub + 1) * 128])
                                     for k in range(4)], ps, self.ident.v)
                        self.copy("act" if half else "dve", y[:, half * 512:(half + 1) * 512], ps.v)
                    self.store(yout[t0 + sub * 128:t0 + (sub + 1) * 128, :], y.v)

    def phase_na(self, l):
        S, I, SEG, TS, ROWS = self.S, self.I, self.SEG, self.TS, self.ROWS
        self.off = self.base_off
        self.ps_lo = 0
        NKT = TS // 128
        tabs = self.sb("tabs", [128, 2, 8, NTAB * 64], BF16)
        msk = self.sb("msk", [128, 2, NTAB * 64], F32)
        gst = self.ring("gst", 2, [128, NTAB * 64], F32)
        qk = self.ring("qk", 2, [64, 2, TS], BF16)
        vg = self.sb("vg", [128, NKT, 256], BF16)
        pf = self.ring("pf", 6, [128, 512], F32)
        pt = self.ring("pt", 14, [128, 256], BF16)
        rd = self.ring("nrd", 2, [64, 256], F32)
        ob = self.ring("ob", 4, [64, 256], BF16)
        oe = self.ring("oe", 4, [64, 256], F32)
        self.load(msk.v, I["namask"].rearrange("p (a b) -> p a b", a=2))
        for h in range(8):
            g = gst.next()
            self.load(g.v, I["rpbg"][l, :, h * NTAB * 64:(h + 1) * NTAB * 64])
            self.act(g.v, g.v, AF.Exp)
            self.tt("dve", tabs[:, 0, h, :], g.v, msk[:, 0, :], ALU.mult)
            self.tt("pool", tabs[:, 1, h, :], g.v, msk[:, 1, :], ALU.mult)
        VT = S["VTOK"].rearrange("(k p) c -> p k c", p=128)
        zdeps = [self.P.dbuf("ZTB", ti) for ti in range(self.NT)]
        vdeps = [self.P.dbuf("VTOK", ti) for ti in range(self.NT)]

        state = {"i": 0, "pend": None}

        def scores(h, q, Q0, a_list, which):
            banks = [self.psb[(state["i"] % 2) * 3 + j] for j in range(3)]
            qs = q[:, 0, Q0 * 64:Q0 * 64 + 256]
            nk = len(a_list)
            for u, a in enumerate(a_list):
                bk = banks[u // 2]
                self.mm([(bk[:, (u % 2) * 256:(u % 2 + 1) * 256], [(q[:, 1, a * 64:a * 64 + 128], qs)])], bk)
            pts = []
            for bi in range((nk + 1) // 2):
                bk = banks[bi]
                nt_ = min(2, nk - 2 * bi)
                p_f = pf.next()
                self.act(p_f[:, 0:256 * nt_], bk[:, 0:256 * nt_], AF.Exp, scale=0.125)
                for j in range(nt_):
                    a = a_list[2 * bi + j]
                    n0 = 6 - (a - Q0)
                    p_t = pt.next()
                    self.tt("dve", p_t.v, p_f[:, j * 256:(j + 1) * 256], tabs[:, which, h, n0 * 64:(n0 + 4) * 64], ALU.mult)
                    pts.append((p_t, a // 2))
            acc = self.psb[6 + state["i"] % 2]
            state["i"] += 1
            return pts, acc

        def pv(pts, acc, hl, out_v):
            nk = len(pts)
            kts = [kt for _, kt in pts]
            pas = [p.ap for p, _ in pts]

            def fn(e, o1=acc.ap[0:64, 0:256], o2=acc.ap[0:64, 256:512], on=self.ones1.ap[:, 0:64]):
                ins = None
                for u in range(nk):
                    e.matmul(o1, lhsT=vg.ap[:, kts[u], hl * 64:(hl + 1) * 64], rhs=pas[u], start=(u == 0), stop=(u == nk - 1))
                for u in range(nk):
                    ins = e.matmul(o2, lhsT=on, rhs=pas[u], start=(u == 0), stop=(u == nk - 1))
                return ins
            self.P.op("pe", fn, reads=[vg, self.ones1] + [p for p, _ in pts], writes=[acc])
            r = rd.next()
            self.recip(r.v, acc[0:64, 256:512])
            self.tt("dve", out_v, acc[0:64, 0:256], r.v, ALU.mult)

        def submit(h, hl, q, Q0, al, which, out_v, post):
            cur = (scores(h, q, Q0, al, which), hl, out_v, post)
            flush()
            state["pend"] = cur

        def flush():
            p = state["pend"]
            if p is not None:
                (pts, acc), hl_, out_v, post = p
                pv(pts, acc, hl_, out_v)
                if post is not None:
                    post()
                state["pend"] = None

        for hg in range(2):
            self.P.dma("sp", [(vg.ap[:, k0:k0 + 16, :], VT[:, k0:k0 + 16, hg * 256:(hg + 1) * 256])
                              for k0 in range(0, NKT, 16)], vg, reads=vdeps, writes=[vg])
            for hl in range(4):
                h = hg * 4 + hl
                q = qk.next()
                self.P.dma("sp", [(q.ap[:, 0, :], S["ZTB"][h * 64:(h + 1) * 64, :]),
                                  (q.ap[:, 1, :], S["ZTB"][512 + h * 64:512 + (h + 1) * 64, :])],
                           q, reads=zdeps, writes=[q])
                for s in range(2):
                    R0 = s * ROWS
                    for jb in range(ROWS // 4):
                        Q0 = R0 + 4 * jb
                        if jb == 0:
                            al, which = [R0 + 6, R0 + 4, R0 + 2, R0], 1
                        elif jb == ROWS // 4 - 1:
                            al, which = [R0 + ROWS - 2, R0 + ROWS - 4, R0 + ROWS - 6, R0 + ROWS - 8], 1
                        else:
                            al, which = [Q0 - 4 + 2 * t for t in range(5, -1, -1)], 0
                        o = ob.next()
                        mid = (s == 0 and jb == ROWS // 4 - 1) or (s == 1 and jb == 0)

                        def st(o=o, h=h, Q0=Q0):
                            self.store(S["MIXT"][h * 64:(h + 1) * 64, Q0 * 64:Q0 * 64 + 256], o.v,
                                       writes=[self.P.dbuf("MIXT", (Q0 * 64) // self.TT)])
                        if not mid:
                            submit(h, hl, q, Q0, al, which, o.v, st)
                        else:
                            e1 = oe.next()
                            e2 = oe.next()

                            def blend(o=o, e1=e1, e2=e2, st=st):
                                self.ts("pool", e1.v, e1.v, self.cflag[0:64, 1:2])
                                self.stt("dve", o.v, e2.v, self.cflag[0:64, 0:1], e1.v, ALU.mult, ALU.add)
                                st()
                            submit(h, hl, q, Q0, al, which, e1.v, None)
                            submit(h, hl, q, Q0, [Q0 - 4 + 2 * t for t in range(5, -1, -1)], 0, e2.v, blend)
            flush()

    def phase_rec(self, l, mixer):
        S, I, SEG, TS, TT, NT, L = self.S, self.I, self.SEG, self.TS, self.TT, self.NT, self.L
        P = self.P
        self.off = self.base_off
        CPT = TT // 64
        f3 = [64, 4, TT]
        qin = self.ring("r_q", 2, f3, F32)
        zin = self.ring("r_z", 2, f3, F32)
        gaf = self.ring("r_ga", 2, [16, TT], F32)
        gafb = self.ring("r_gab", 2, [16, TT], BF16)
        t1 = self.sb("r_t1", f3, F32)
        t2 = self.sb("r_t2", f3, F32)
        gl = self.sb("r_g", f3, F32)
        be = self.sb("r_be", f3, F32)
        eq = self.sb("r_eq", f3, F32)
        ek = self.sb("r_ek", f3, F32)
        qt = self.ring("r_qt", 2, f3, BF16)
        kt_ = self.ring("r_kt", 2, f3, BF16)
        fac = self.ring("r_fac", 2, [64, 3, 4, CPT], F32)
        vtok = self.ring("r_v", 2, [64, CPT, 256], BF16)
        ktok = self.ring("r_ktok", 4, [64, 4, 64], BF16)
        atm = self.ring("r_atm", 5, [64, 4, 64], BF16)
        T = self.sb("r_T", [64, 4, 64], F32)
        Sp = self.ring("r_Sp", 4, [64, 4, 64], BF16)
        Wc = self.ring("r_W", 3, [64, 4, 64], F32)
        osum = self.ring("r_os", 2, [64, 4, TT], F32)
        ofl = self.ring("r_of", 1, [64, 4, TT], F32)
        gate = self.ring("r_gate", 1, [64, 4, TT], F32)
        sqo = self.sb("r_sqo", [64, 4, TT], BF16)
        rs = self.sb("r_rs", [64, 4, TT], F32)
        og = self.ring("r_og", 2, [64, 4, TT], BF16)
        if mixer == 1:
            for b in qin.items + zin.items:
                self.memset("pool", b.v, 0.0)
        ZTF = S["ZTF"]
        VT = S["VTOK"]
        zdeps = lambda ti: [P.dbuf("ZTF", ti)]
        vcol = 512 if mixer == 0 else 768
        rq = R_HQ if mixer == 0 else R_GQ
        rgate = R_HGATE if mixer == 0 else R_GGATE
        moff = 4 * 64 * mixer + 512
        idn64 = self.ident[0:64, 0:64]
        self.ps_lo = 6

        def headload(dst, row0, ti):
            t0 = ti * TT
            if mixer == 0:
                P.dma("sp", [(dst.ap, ZTF[row0:row0 + 256, t0:t0 + TT].rearrange("(h k) t -> k h t", k=64))],
                      dst, reads=zdeps(ti), writes=[dst])
            else:
                P.dma("sp", [(dst.ap[0:32], ZTF[row0:row0 + 128, t0:t0 + TT].rearrange("(h k) t -> k h t", k=32))],
                      dst, reads=zdeps(ti), writes=[dst])

        for d in range(2):
            order = list(range(NT)) if d == 0 else list(range(NT - 1, -1, -1))
            self.memset("dve", T.v, 0.0)
            for idx, ti in enumerate(order):
                t0 = ti * TT
                seg_start = (idx > 0) and ((t0 // SEG) != ((order[idx - 1] * TT) // SEG))
                q = qin.next()
                headload(q, rq, ti)
                z = zin.next()
                if mixer == 0:
                    headload(z, R_HZF if d == 0 else R_HZB, ti)
                else:
                    headload(z, R_GK, ti)
                    ga = gaf.next()
                    self.load(ga.v, ZTF[R_GA + 16 * d:R_GA + 16 * d + 16, t0:t0 + TT], reads=zdeps(ti))
                vt = vtok.next()
                P.dma("sp", [(vt.ap, VT[t0:t0 + TT, vcol:vcol + 256].rearrange("(c s) f -> s c f", s=64))],
                      vt, reads=[P.dbuf("VTOK", ti)], writes=[vt])
                if d == 1:
                    of_ = ofl.next()
                    P.dma("sp", [(of_.ap, S["OF"][:, :, t0:t0 + TT].rearrange("h v t -> v h t"))], of_,
                          reads=[P.dbuf("OF", ti)], writes=[of_])
                    gt = gate.next()
                    P.dma("sp", [(gt.ap, ZTF[rgate:rgate + 256, t0:t0 + TT].rearrange("(h v) t -> v h t", v=64))],
                          gt, reads=zdeps(ti), writes=[gt])
                if mixer == 0:
                    self.act(t1.v, z.v, AF.Exp, scale=-1.0)
                    self.ts("pool", t1.v, t1.v, 1.0, None, ALU.add)
                    self.recip(t1.v, t1.v)
                    for hh in range(4):
                        c = (l * 2 + d) * 4 + hh
                        self.ts("dve", t2[:, hh, :], t1[:, hh, :], self.omlb[0:64, c:c + 1], self.lb[0:64, c:c + 1],
                                ALU.mult, ALU.add)
                    self.act(gl.v, t2.v, AF.Ln)
                    kk = t1
                    self.ts("pool", kk.v, t2.v, -1.0, 1.0, ALU.mult, ALU.add)
                    qq = q
                else:
                    gb = gafb.next()
                    self.copy("pool", gb.v, ga.v)
                    for hh in range(4):
                        c = (l * 2 + d) * 4 + hh
                        ps = self.psn()
                        self.mm([(ps[0:64, :], [(self.walphab[:, c * 64:(c + 1) * 64], gb.v)])], ps)
                        self.act(t1[:, hh, :], ps[0:64, :], AF.Exp, scale=-1.0, bias=self.nbalpha[0:64, c:c + 1])
                    self.act(t2.v, t1.v, AF.Ln, bias=self.one_f[0:64, 0:1])
                    self.ts("pool", gl.v, t2.v, -1.0 / 16.0)
                    kk = z
                    qq = t1
                    self.ts("pool", qq.v, q.v, 32.0 ** -0.5)
                g4 = gl.v.re("p a (c s) -> p a c s", s=64)
                b4 = be.v.re("p a (c s) -> p a c s", s=64)
                for hh in range(4):
                    self.scan(be[:, hh, :], self.rmask[0:64, :], gl[:, hh, :])
                if d == 0:
                    mpos, lpos = 31, 63
                    bt = be
                else:
                    self.tt("pool", gl.v, gl.v, be.v, ALU.subtract)
                    bt = t2
                    bt4 = bt.v.re("p a (c s) -> p a c s", s=64)
                    for hh in range(4):
                        self.tt("dve", bt4[:, hh], g4[:, hh], b4[:, hh, :, 63:64].bc([64, CPT, 64]), ALU.add)
                    mpos, lpos = 32, 0
                bt4 = bt.v.re("p a (c s) -> p a c s", s=64)
                e4 = eq.v.re("p a (c s) -> p a c s", s=64)
                fc = fac.next()
                f4 = fc.v
                for hh in range(4):
                    self.act(f4[:, 0, hh, :].re("p (c o) -> p c o", o=1), bt4[:, hh, :, mpos:mpos + 1], AF.Exp)
                    self.act(f4[:, 1, hh, :].re("p (c o) -> p c o", o=1), bt4[:, hh, :, lpos:lpos + 1], AF.Exp)
                    self.tt("dve", e4[:, hh], bt4[:, hh], bt4[:, hh, :, mpos:mpos + 1].bc([64, CPT, 64]), ALU.subtract)
                self.act(ek.v, eq.v, AF.Exp, scale=-1.0)
                self.act(eq.v, eq.v, AF.Exp)
                for hh in range(4):
                    self.copy("pool", f4[:, 2, hh, :].re("p (c o) -> p c o", o=1), e4[:, hh, :, lpos:lpos + 1])
                qt_ = qt.next()
                ktl = kt_.next()
                self.tt("dve", qt_.v, qq.v, eq.v, ALU.mult)
                self.tt("pool", ek.v, kk.v, ek.v, ALU.mult)
                self.copy("pool", ktl.v, ek.v)
                osm = osum.next()
                chunks = list(range(CPT)) if d == 0 else list(range(CPT - 1, -1, -1))
                st_ = {}

                def stage1(ci, c):
                    cs = slice(c * 64, (c + 1) * 64)
                    bk = self.psb[ci % 2]
                    self.mm([(bk[0:64, hh * 64:(hh + 1) * 64], [(ktl[:, hh, cs], qt_[:, hh, cs])]) for hh in range(4)], bk)
                    am = atm.next()
                    self.tt("dve", am.v, bk[0:64, 0:256].re("p (h t) -> p h t", h=4),
                            self.tri[:, d, :].re("p (o t) -> p o t", o=1).bc([64, 4, 64]), ALU.mult)
                    self.transp([(bk[0:64, 256 + hh * 64:256 + (hh + 1) * 64], ek[:, hh, cs]) for hh in range(4)], bk, idn64)
                    ktk = ktok.next()
                    self.copy("act", ktk.v, bk[0:64, 256:512].re("p (h k) -> p h k", h=4))
                    st_[ci] = {"am": am, "ktk": ktk}

                def stage2(ci, c):
                    ktk = st_[ci]["ktk"]
                    pu = self.psb[2 + ci % 2]
                    self.mm([(pu[0:64, hh * 64:(hh + 1) * 64], [(ktk[:, hh, :], vt[:, c, hh * 64:(hh + 1) * 64])])
                             for hh in range(4)], pu)
                    if seg_start and ci == 0:
                        self.ts("dve", T.v, T.v, self.cflag[0:64, 0:1])
                    sp = Sp.next()
                    self.tt("pool", sp.v, T.v, f4[:, 0, :, c:c + 1].bc([64, 4, 64]), ALU.mult)
                    w = Wc.next()
                    self.tt("dve", w.v, pu[0:64, 0:256].re("p (a v) -> p a v", a=4), f4[:, 2, :, c:c + 1].bc([64, 4, 64]),
                            ALU.mult)
                    self.tt("dve", T.v, T.v, f4[:, 1, :, c:c + 1].bc([64, 4, 64]), ALU.mult)
                    self.tt("dve", T.v, T.v, w.v, ALU.add)
                    st_[ci]["sp"] = sp

                def stage3(ci, c):
                    cs = slice(c * 64, (c + 1) * 64)
                    am, sp = st_[ci]["am"], st_[ci]["sp"]
                    po = self.psb[4 + ci % 2]
                    self.mm([(po[0:64, hh * 64:(hh + 1) * 64],
                              [(vt[:, c, hh * 64:(hh + 1) * 64], am[:, hh, :]), (sp[:, hh, :], qt_[:, hh, cs])])
                             for hh in range(4)], po)
                    if d == 0:
                        self.copy("act", osm[:, :, cs], po[0:64, 0:256].re("p (h t) -> p h t", h=4))
                    else:
                        self.tt("dve", osm[:, :, cs], po[0:64, 0:256].re("p (h t) -> p h t", h=4), of_[:, :, cs], ALU.add)

                nch = len(chunks)
                for it in range(nch + 2):
                    if it < nch:
                        stage1(it, chunks[it])
                    if 0 <= it - 1 < nch:
                        stage2(it - 1, chunks[it - 1])
                    if 0 <= it - 2 < nch:
                        stage3(it - 2, chunks[it - 2])
                if d == 0:
                    P.dma("pool", [(S["OF"][:, :, t0:t0 + TT].rearrange("h v t -> v h t"), osm.ap)], osm,
                          reads=[osm], writes=[P.dbuf("OF", ti)])
                else:
                    self.tt("pool", sqo.v, osm.v, osm.v, ALU.mult)
                    for hh in range(4):
                        ps = self.psn()
                        self.mm([(ps[0:64, :], [(self.ones64[0:64, :], sqo[:, hh, :])])], ps)
                        self.act(rs[:, hh, :], ps[0:64, :], AF.Ln, bias=self.epsb[0:64, 0:1])
                    self.act(rs.v, rs.v, AF.Exp, scale=-0.5)
                    self.stt("dve", osm.v, osm.v, self.hnorm[0:64, mixer * L + l:mixer * L + l + 1], rs.v, ALU.mult, ALU.mult)
                    self.act(rs.v, gt.v, AF.Exp, scale=-1.0)
                    self.ts("pool", rs.v, rs.v, 1.0, None, ALU.add)
                    self.recip(rs.v, rs.v)
                    self.tt("pool", gt.v, gt.v, rs.v, ALU.mult)
                    o_ = og.next()
                    self.tt("dve", o_.v, osm.v, gt.v, ALU.mult)
                    P.dma("pool", [(S["MIXT"][moff:moff + 256, t0:t0 + TT].rearrange("(h v) t -> v h t", v=64), o_.ap)],
                          o_, reads=[o_], writes=[P.dbuf("MIXT", ti)])


class Ring:
    def __init__(self, items):
        self.items = items
        self.i = 0

    def next(self):
        b = self.items[self.i % len(self.items)]
        self.i += 1
        return b


def _host_consts(L, na_rpb, hg_lb, hg_norm, gla_norm, gla_w_alpha, gla_b_alpha, gains_list, g_final):
    out = {}
    NG = 5 * L + 1
    gains = np.zeros((128, NG * DT), np.float32)
    for w, g in enumerate(gains_list):
        for l in range(L):
            gains[:, (w * L + l) * DT:(w * L + l + 1) * DT] = g[l].reshape(DT, 128).T
    gains[:, 5 * L * DT:(5 * L + 1) * DT] = g_final.reshape(DT, 128).T
    out["gains"] = gains
    hglb = np.zeros((128, L * 8), np.float32)
    for l in range(L):
        for d in range(2):
            for hd in range(4):
                hglb[0:64, (l * 2 + d) * 4 + hd] = hg_lb[l, d, hd * 64:(hd + 1) * 64]
    out["hglb"] = hglb
    hn = np.zeros((128, 2 * L), np.float32)
    hn[0:64, 0:L] = hg_norm[:L].T
    hn[0:64, L:2 * L] = gla_norm[:L].T
    out["hnorm"] = hn
    wa = np.zeros((16, L * 8 * 64), np.float32)
    ba = np.zeros((128, L * 8), np.float32)
    for l in range(L):
        for d in range(2):
            for hd in range(4):
                c = (l * 2 + d) * 4 + hd
                wa[:, c * 64:c * 64 + 32] = gla_w_alpha[l, d, :, hd * 32:(hd + 1) * 32]
                ba[0:32, c] = gla_b_alpha[l, d, hd * 32:(hd + 1) * 32]
    out["walpha"] = wa
    out["balpha"] = ba
    p = np.arange(128)
    kc = p % 64
    half = p // 64
    n = np.arange(NTAB)
    qc = np.arange(64)
    dr = 6 - n[None, :] + half[:, None]
    dc = kc[:, None] - qc[None, :] + 15
    c0 = np.clip(qc - 8, 0, 48)
    colok = (kc[:, None] >= c0[None, :]) & (kc[:, None] < c0[None, :] + 16)
    dcc = np.clip(dc, 0, 30)
    drc = np.clip(dr + 7, 0, 14)
    g = na_rpb[:L][:, :, drc[:, :, None], dcc[:, None, :]]
    out["rpbg"] = np.ascontiguousarray(g.transpose(0, 2, 1, 3, 4)).reshape(L, 128, 8 * NTAB * 64).astype(np.float32)
    rowok_int = (dr >= -4) & (dr <= 3)
    rowok_all = (dr >= -7) & (dr <= 7)
    m = np.zeros((128, 2, NTAB, 64), np.float32)
    m[:, 0] = (rowok_int[:, :, None] & colok[:, None, :])
    m[:, 1] = (rowok_all[:, :, None] & colok[:, None, :])
    out["namask"] = m.reshape(128, 2 * NTAB * 64)
    out["ident"] = np.eye(128, dtype=np.float32)
    s = np.arange(64)
    tri = np.zeros((64, 2, 64), np.float32)
    tri[:, 0, :] = (s[None, :] >= s[:, None])
    tri[:, 1, :] = (s[None, :] <= s[:, None])
    out["tri"] = tri.reshape(64, 128)
    rm = np.ones((128, 512), np.float32)
    rm[:, ::64] = 0.0
    out["rmask"] = rm
    return out


_CACHE = {}


def run_slots(SEG, L, slots, weights, n_cores):
    key = (SEG, L)
    if key not in _CACHE:
        _CACHE[key] = K(SEG, L).build()
    nc = _CACHE[key]
    hc = _host_consts(L, weights["na_rpb"], weights["hg_lb"], weights["hg_norm"], weights["gla_norm"],
                      weights["gla_w_alpha"], weights["gla_b_alpha"],
                      [weights["g_ffn1"], weights["g_mix"], weights["g_cross"], weights["g_mem"], weights["g_ffn2"]],
                      weights["g_final"])
    in_maps = []
    for x, mem, c in slots:
        m = dict(hc)
        m["x"] = np.ascontiguousarray(x, dtype=np.float32)
        m["mem"] = np.ascontiguousarray(mem, dtype=np.float32)
        cf = np.zeros((128, 2), np.float32)
        cf[:, 0] = c
        cf[:, 1] = 1 - c
        m["cflag"] = cf
        for nm in ("w_ffn1_up", "w_ffn1_down", "w_ffn2_up", "w_ffn2_down", "w_in", "w_out", "ca_wq", "ca_wkv", "ca_wo"):
            m[nm] = np.ascontiguousarray(weights[nm][:L], dtype=np.float32)
        in_maps.append(m)
    res = run_bass_kernel_spmd(nc, in_maps, core_ids=list(range(n_cores)))
    return [r["y"] for r in res.results]


def kernel(**inputs):
    inputs = {k: np.asarray(v) for k, v in inputs.items()}
    xp, xs = inputs["x_prompt"], inputs["x_sample"]
    mp, ms = inputs["mem_prompt"], inputs["mem_sample"]
    SEG = xp.shape[1]
    L = inputs["w_in"].shape[0]
    slots = []
    for i in range(4):
        slots.append((xs[i], np.stack([ms[i], ms[i]]), 1.0))
    for k in range(2):
        slots.append((np.concatenate([xp[2 * k], xp[2 * k + 1]], axis=0), np.stack([mp[2 * k], mp[2 * k + 1]]), 0.0))
    slots.append(slots[4])
    slots.append(slots[5])
    ys = run_slots(SEG, L, slots, inputs, 8)
    y_sample = np.stack(ys[0:4]).astype(np.float32)
    y_prompt = np.stack([ys[4][:SEG], ys[4][SEG:], ys[5][:SEG], ys[5][SEG:]]).astype(np.float32)
    return (y_prompt, y_sample)
```
